# Optimizing a Trainium2 kernel written in Bass

```python
import math
import jax, jax.numpy as jnp
from jax import lax
import numpy as np

D_MODEL = 2048
BATCH = 2
SEQ = 8192
DEPTH = 1

GRID_W = 64
CTX_LEN = 256
ROWS_PER_CHUNK = 2
CHUNK = ROWS_PER_CHUNK * GRID_W
D_MIX = D_MODEL
W_A = D_MIX // 2
HEAD_DIM_A = 128
N_HEADS_A = W_A // HEAD_DIM_A
W_B = D_MIX - W_A
S5_CH = 16
S5_GROUPS = W_B // S5_CH
S5_STATE = 64
N_DIR = 2
IN_COLS = 3 * W_A + 2 * W_B
ALPHA = (2.0 * DEPTH) ** 0.25
OUT_INIT_SCALE = (8.0 * DEPTH) ** -0.25
LN_EPS = 1e-6
F32 = jnp.float32

kernel_name = "hybrid_gmlp_s5_parallel_heads_deepnorm"


def _layer_norm(x):
    x32 = x.astype(F32)
    mu = jnp.mean(x32, axis=-1, keepdims=True)
    var = jnp.mean(jnp.square(x32 - mu), axis=-1, keepdims=True)
    return ((x32 - mu) * lax.rsqrt(var + LN_EPS)).astype(x.dtype)


def _modulate(x, shift, scale):
    return _layer_norm(x) * (1 + scale[:, None, :]) + shift[:, None, :]


def _chunk_mlp(uv, n_chunks, ln_g, ln_b, w_s, b_s):
    bsz = uv.shape[0]
    u, v = jnp.split(jax.nn.gelu(uv, approximate=False), 2, axis=-1)
    v = _layer_norm(v) * ln_g + ln_b
    v = v.reshape(bsz, n_chunks, CHUNK, N_HEADS_A, HEAD_DIM_A)
    mixed = jnp.einsum("hpq,bnqhd->bnphd", w_s, v) + b_s.T[None, None, :, :, None]
    return u * mixed.reshape(bsz, n_chunks * CHUNK, W_A)


def _s5_discretize(lam_re, lam_im, log_step, b_re, b_im):
    step = jnp.exp(log_step.astype(F32))[:, None]
    lr, li = lam_re.astype(F32), lam_im.astype(F32)
    dr, di = lr * step, li * step
    mag = jnp.exp(dr)
    ab_re, ab_im = mag * jnp.cos(di), mag * jnp.sin(di)
    den = lr * lr + li * li
    nr, ni = ab_re - 1.0, ab_im
    f_re = (nr * lr + ni * li) / den
    f_im = (ni * lr - nr * li) / den
    br, bi = b_re.astype(F32), b_im.astype(F32)
    bb_re = f_re[..., None] * br - f_im[..., None] * bi
    bb_im = f_re[..., None] * bi + f_im[..., None] * br
    return ab_re, ab_im, bb_re, bb_im


def _ssm_combine(e1, e2):
    a1r, a1i, b1r, b1i = e1
    a2r, a2i, b2r, b2i = e2
    ar = a1r * a2r - a1i * a2i
    ai = a1r * a2i + a1i * a2r
    br = a2r * b1r - a2i * b1i + b2r
    bi = a2r * b1i + a2i * b1r + b2i
    return ar, ai, br, bi


def _s5_scan(u, ab_re, ab_im, bb_re, bb_im, h0):
    bu_re = jnp.einsum("lbgc,gpc->lbgp", u, bb_re)
    bu_im = jnp.einsum("lbgc,gpc->lbgp", u, bb_im)
    if h0 is not None:
        h0_re, h0_im = h0
        bu_re = bu_re.at[0].add(ab_re * h0_re - ab_im * h0_im)
        bu_im = bu_im.at[0].add(ab_re * h0_im + ab_im * h0_re)
    length = u.shape[0]
    a_re = jnp.broadcast_to(ab_re, (length, 1) + ab_re.shape)
    a_im = jnp.broadcast_to(ab_im, (length, 1) + ab_im.shape)
    _, _, h_re, h_im = lax.associative_scan(_ssm_combine, (a_re, a_im, bu_re, bu_im), axis=0)
    return h_re, h_im


def _s5_readout(h_re, h_im, c_re, c_im):
    return (jnp.einsum("lbgp,gcp->lbgc", h_re, c_re.astype(F32))
            - jnp.einsum("lbgp,gcp->lbgc", h_im, c_im.astype(F32)))


def _s5_branch(u_lat, u_ctx, lam_re, lam_im, log_step, b_re, b_im, c_re, c_im,
               d_skip, w_glu, b_glu, with_ctx_out):
    def to_lbgc(u):
        bsz, length, _ = u.shape
        return u.astype(F32).reshape(bsz, length, S5_GROUPS, S5_CH).transpose(1, 0, 2, 3)

    ul, uc = to_lbgc(u_lat), to_lbgc(u_ctx)
    ys_lat, ys_ctx = [], []
    for d in range(N_DIR):
        ab_re, ab_im, bb_re, bb_im = _s5_discretize(lam_re[d], lam_im[d], log_step[d], b_re[d], b_im[d])
        ucd, uld = (uc, ul) if d == 0 else (uc[::-1], ul[::-1])
        hc_re, hc_im = _s5_scan(ucd, ab_re, ab_im, bb_re, bb_im, None)
        hl_re, hl_im = _s5_scan(uld, ab_re, ab_im, bb_re, bb_im, (hc_re[-1], hc_im[-1]))
        yl = _s5_readout(hl_re, hl_im, c_re[d], c_im[d])
        ys_lat.append(yl if d == 0 else yl[::-1])
        if with_ctx_out:
            yc = _s5_readout(hc_re, hc_im, c_re[d], c_im[d])
            ys_ctx.append(yc if d == 0 else yc[::-1])

    d_grp = d_skip.astype(F32).reshape(S5_GROUPS, S5_CH)

    def finish(y, u, dtype):
        y = y + d_grp * u
        length, bsz = y.shape[0], y.shape[1]
        y = y.transpose(1, 0, 2, 3).reshape(bsz, length, W_B)
        y = jax.nn.gelu(y, approximate=False).astype(dtype)
        return y * jax.nn.sigmoid(y @ w_glu + b_glu)

    y_lat = finish(ys_lat[0] + ys_lat[1], ul, u_lat.dtype)
    y_ctx = finish(ys_ctx[0] + ys_ctx[1], uc, u_ctx.dtype) if with_ctx_out else None
    return y_lat, y_ctx


def setup_inputs(seed: int = 0) -> dict:
    key = jax.random.key(seed)
    ks = jax.random.split(key, 24)
    nrm = jax.random.normal
    D = D_MODEL
    x = nrm(ks[0], (BATCH, SEQ, D), F32)
    c = nrm(ks[1], (BATCH, D), F32)
    ctx = nrm(ks[2], (BATCH, CTX_LEN, D), F32)
    c_ctx = nrm(ks[3], (D,), F32)
    w_ada = nrm(ks[4], (DEPTH, D, 3 * D), F32) * (D ** -0.5) * 0.5
    b_ada = 0.02 * nrm(ks[5], (DEPTH, 3 * D), F32) + jnp.concatenate(
        [jnp.zeros((2 * D,), F32), jnp.ones((D,), F32)])[None]
    w_in = nrm(ks[6], (DEPTH, D, IN_COLS), F32) * (D ** -0.5)
    sgu_ln_g = 1.0 + 0.02 * nrm(ks[7], (DEPTH, W_A), F32)
    sgu_ln_b = 0.02 * nrm(ks[8], (DEPTH, W_A), F32)
    w_spatial = nrm(ks[9], (DEPTH, N_HEADS_A, CHUNK, CHUNK), F32) * (CHUNK ** -0.5)
    b_spatial = 1.0 + 0.02 * nrm(ks[10], (DEPTH, N_HEADS_A, CHUNK), F32)
    n_idx = jnp.arange(S5_STATE, dtype=F32)
    s5_shape = (DEPTH, N_DIR, S5_GROUPS, S5_STATE)
    s5_lam_re = -0.5 + 0.01 * nrm(ks[11], s5_shape, F32)
    s5_lam_im = math.pi * n_idx + 0.01 * nrm(ks[12], s5_shape, F32)
    s5_log_step = jax.random.uniform(ks[13], (DEPTH, N_DIR, S5_GROUPS), F32,
                                     minval=math.log(1e-3), maxval=math.log(1e-1))
    b_shape = (DEPTH, N_DIR, S5_GROUPS, S5_STATE, S5_CH)
    s5_b_re = nrm(ks[14], b_shape, F32) * ((2 * S5_CH) ** -0.5)
    s5_b_im = nrm(ks[15], b_shape, F32) * ((2 * S5_CH) ** -0.5)
    c_shape = (DEPTH, N_DIR, S5_GROUPS, S5_CH, S5_STATE)
    s5_c_re = nrm(ks[16], c_shape, F32) * (0.5 ** 0.5)
    s5_c_im = nrm(ks[17], c_shape, F32) * (0.5 ** 0.5)
    s5_d = nrm(ks[18], (DEPTH, W_B), F32)
    w_glu = nrm(ks[19], (DEPTH, W_B, W_B), F32) * (W_B ** -0.5)
    b_glu = 0.02 * nrm(ks[20], (DEPTH, W_B), F32)
    w_out = nrm(ks[21], (DEPTH, D_MIX, D), F32) * (D_MIX ** -0.5) * OUT_INIT_SCALE
    ln_g = 1.0 + 0.02 * nrm(ks[22], (DEPTH, D), F32)
    ln_b = 0.02 * nrm(ks[23], (DEPTH, D), F32)
    return {"x": x, "c": c, "ctx": ctx, "c_ctx": c_ctx,
            "w_ada": w_ada, "b_ada": b_ada, "w_in": w_in,
            "sgu_ln_g": sgu_ln_g, "sgu_ln_b": sgu_ln_b,
            "w_spatial": w_spatial, "b_spatial": b_spatial,
            "s5_lam_re": s5_lam_re, "s5_lam_im": s5_lam_im, "s5_log_step": s5_log_step,
            "s5_b_re": s5_b_re, "s5_b_im": s5_b_im, "s5_c_re": s5_c_re, "s5_c_im": s5_c_im,
            "s5_d": s5_d, "w_glu": w_glu, "b_glu": b_glu, "w_out": w_out,
            "ln_g": ln_g, "ln_b": ln_b}


def reference(x, c, ctx, c_ctx, w_ada, b_ada, w_in, sgu_ln_g, sgu_ln_b, w_spatial, b_spatial,
              s5_lam_re, s5_lam_im, s5_log_step, s5_b_re, s5_b_im, s5_c_re, s5_c_im,
              s5_d, w_glu, b_glu, w_out, ln_g, ln_b):
    rows = x.shape[1] // GRID_W
    n_chunks_lat = rows // ROWS_PER_CHUNK
    n_chunks_ctx = ctx.shape[1] // CHUNK
    col_b0, col_b1 = 3 * W_A, 3 * W_A + W_B
    for i in range(DEPTH):
        update_ctx = i < DEPTH - 1
        mod_x = jax.nn.silu(c) @ w_ada[i] + b_ada[i]
        mod_c = (jax.nn.silu(c_ctx) @ w_ada[i] + b_ada[i])[None]
        shift_x, scale_x, gate_x = jnp.split(mod_x, 3, axis=-1)
        shift_c, scale_c, gate_c = jnp.split(mod_c, 3, axis=-1)

        proj_x = _modulate(x, shift_x, scale_x) @ w_in[i]
        hc = _modulate(ctx, shift_c, scale_c)
        ub_c = hc @ w_in[i][:, col_b0:col_b1]
        uv_x, za_x = proj_x[..., :2 * W_A], proj_x[..., 2 * W_A:col_b0]
        ub_x, zb_x = proj_x[..., col_b0:col_b1], proj_x[..., col_b1:]

        ya_x = _chunk_mlp(uv_x, n_chunks_lat, sgu_ln_g[i], sgu_ln_b[i],
                          w_spatial[i], b_spatial[i]) * jax.nn.silu(za_x)
        yb_x, yb_c = _s5_branch(ub_x, ub_c, s5_lam_re[i], s5_lam_im[i], s5_log_step[i],
                                s5_b_re[i], s5_b_im[i], s5_c_re[i], s5_c_im[i],
                                s5_d[i], w_glu[i], b_glu[i], update_ctx)
        yb_x = yb_x * jax.nn.silu(zb_x)

        out_x = jnp.concatenate([ya_x, yb_x], axis=-1) @ w_out[i]
        x_new = _layer_norm(ALPHA * x + gate_x[:, None, :] * out_x) * ln_g[i] + ln_b[i]

        if update_ctx:
            rest_c = hc @ w_in[i]
            ya_c = _chunk_mlp(rest_c[..., :2 * W_A], n_chunks_ctx, sgu_ln_g[i], sgu_ln_b[i],
                              w_spatial[i], b_spatial[i]) * jax.nn.silu(rest_c[..., 2 * W_A:col_b0])
            yb_c = yb_c * jax.nn.silu(rest_c[..., col_b1:])
            out_c = jnp.concatenate([ya_c, yb_c], axis=-1) @ w_out[i]
            ctx = _layer_norm(ALPHA * ctx + gate_c[:, None, :] * out_c) * ln_g[i] + ln_b[i]
        x = x_new
    return x
```

```python
import time, sys, math
from contextlib import ExitStack
import numpy as np
import concourse.bass as bass
import concourse.mybir as mybir
from concourse.bass_utils import run_bass_kernel_spmd

F32 = mybir.dt.float32
BF16 = mybir.dt.bfloat16
ALU = mybir.AluOpType
AF = mybir.ActivationFunctionType
PI = math.pi


class Prog:
    def __init__(self, nc, n_dma_sems=10):
        self.nc = nc
        self.engs = {"pe": nc.tensor, "act": nc.scalar, "dve": nc.vector, "pool": nc.gpsimd, "sp": nc.sync}
        self.sem = {}
        self.cnt = {k: 0 for k in self.engs}
        self.seen = {k: {} for k in self.engs}
        self.es = ExitStack()
        for k in self.engs:
            self.sem[k] = self.es.enter_context(nc.semaphore("s_" + k))
        self.dsem = [self.es.enter_context(nc.semaphore("d_%d" % i)) for i in range(n_dma_sems)]
        self.dcnt = [0] * n_dma_sems
        self.ccsem = self.es.enter_context(nc.semaphore("cc_sem"))
        self.cccnt = 0
        self._fz = self.es.enter_context(nc.sbuf_tensor("fence_z", [128, 8], F32))
        self.dnext = 0
        self.lastw = {}
        self.readers = {}

    def _wait(self, eng, tok):
        if tok is None:
            return
        kind, key, val = tok
        if kind == "e" and key == "pe" and eng == "pe":
            return
        if kind == "e" and key == eng and getattr(self, "_nosame", False):
            return
        seen = self.seen[eng]
        if seen.get((kind, key), 0) >= val:
            return
        seen[(kind, key)] = val
        s = self.sem[key] if kind == "e" else (self.ccsem if kind == "c" else self.dsem[key])
        self.engs[eng].wait_ge(s, val)

    def _deps(self, eng, reads, writes):
        for r in reads:
            self._wait(eng, self.lastw.get(r))
        for w in writes:
            self._wait(eng, self.lastw.get(w))
            for t in self.readers.get(w, []):
                self._wait(eng, t)

    def _commit(self, tok, reads, writes):
        for r in reads:
            self.readers.setdefault(r, []).append(tok)
        for w in writes:
            self.lastw[w] = tok
            self.readers[w] = []

    def _pe_mode_guard(self, lhsT, kind):
        def r(n):
            return 32 if n <= 32 else (64 if n <= 64 else 128)
        m = 1
        for dmn in lhsT.shape[1:]:
            m *= dmn
        mode = (r(lhsT.shape[0]), r(m), str(lhsT.dtype), kind)
        if getattr(self, "_pe_mode", None) not in (None, mode) and self.cnt["pe"] > 0:
            self.engs["pe"].wait_ge(self.sem["pe"], self.cnt["pe"])
        self._pe_mode = mode

    def op(self, eng, fn, reads=(), writes=(), nosame=False):
        self._nosame = nosame
        self._deps(eng, reads, writes)
        self._nosame = False
        if eng == "pe":
            prog = self

            class _PE:
                def matmul(self_, out, lhsT, rhs, **kw):
                    prog._pe_mode_guard(lhsT, "mm")
                    return prog.engs["pe"].matmul(out, lhsT=lhsT, rhs=rhs, **kw)

                def transpose(self_, out, in_, ident):
                    prog._pe_mode_guard(in_, "tr")
                    return prog.engs["pe"].transpose(out, in_, ident)

            ins = fn(_PE())
            self.cnt[eng] += 1
            ins.then_inc(self.sem[eng], 1)
            tok = ("e", eng, self.cnt[eng])
            self._commit(tok, reads, writes)
            return tok
        ins = fn(self.engs[eng])
        self.cnt[eng] += 1
        ins.then_inc(self.sem[eng], 1)
        tok = ("e", eng, self.cnt[eng])
        self._commit(tok, reads, writes)
        return tok

    def dma(self, eng, out, in_, reads=(), writes=()):
        k = self.dnext
        self.dnext = (self.dnext + 1) % len(self.dsem)
        if self.dcnt[k] > 0:
            self._wait(eng, ("d", k, self.dcnt[k]))
        if eng == "pool" and len(getattr(self, "pool_dmas", [])) >= 2:
            self._wait(eng, self.pool_dmas[-2])
        self._deps(eng, reads, writes)
        ins = self.engs[eng].dma_start(out=out, in_=in_)
        self.dcnt[k] += 16
        ins.then_inc(self.dsem[k], 16)
        tok = ("d", k, self.dcnt[k])
        if eng == "pool":
            if not hasattr(self, "pool_dmas"):
                self.pool_dmas = []
            self.pool_dmas.append(tok)
        self._commit(tok, reads, writes)
        return tok

    def collective(self, kind, groups, src, dst, reads, writes, after):
        self._wait("pool", after)
        for tk in getattr(self, "pool_dmas", [])[-2:]:
            self._wait("pool", tk)
        self._deps("pool", reads, writes)
        ins = self.nc.gpsimd.collective_compute(kind, ALU.bypass, replica_groups=groups, ins=[src], outs=[dst])
        self.cccnt += 1
        ins.then_inc(self.ccsem, 1)
        tok = ("c", 0, self.cccnt)
        self._commit(tok, reads, writes)
        return tok

    def fence(self, eng, res):
        if eng == "act":
            self.op(eng, lambda e: e.activation(self._fz[:, 0:1], self._fz[:, 1:2], AF.Copy), reads=list(res), writes=list(res) + ["fence_z"])
        else:
            self.op(eng, lambda e: e.memset(self._fz[:, 0:1], 0.0), reads=list(res), writes=list(res) + ["fence_z"])

    def barrier(self):
        toks = [("e", k, self.cnt[k]) for k in self.engs if self.cnt[k] > 0]
        toks += [("d", i, c) for i, c in enumerate(self.dcnt) if c > 0]
        if self.cccnt > 0:
            toks.append(("c", 0, self.cccnt))
        for e in self.engs:
            for t in toks:
                self._wait(e, t)

    def finish(self, eng, toks):
        for t in toks:
            self._wait(eng, t)

    def close(self):
        self.es.close()


def cmul(P, eng, o_re, o_im, a_re, a_im, b_re, b_im, t0, t1, rd, wr, neg_im=False):
    T = ["cm_t0", "cm_t1"]
    P.op(eng, lambda e: e.tensor_tensor(t0, a_re, b_re, ALU.mult), reads=rd, writes=[T[0]])
    P.op(eng, lambda e: e.tensor_tensor(t1, a_im, b_im, ALU.mult), reads=rd, writes=[T[1]])
    P.op(eng, lambda e: e.tensor_tensor(o_re, t0, t1, ALU.subtract), reads=T, writes=wr)
    P.op(eng, lambda e: e.tensor_tensor(t0, a_re, b_im, ALU.mult), reads=rd + wr, writes=[T[0]])
    P.op(eng, lambda e: e.tensor_tensor(t1, a_im, b_re, ALU.mult), reads=rd + wr, writes=[T[1]])
    if neg_im:
        P.op(eng, lambda e: e.scalar_tensor_tensor(o_im, t0, -1.0, t1, ALU.mult, ALU.subtract), reads=T, writes=wr)
    else:
        P.op(eng, lambda e: e.tensor_tensor(o_im, t0, t1, ALU.add), reads=T, writes=wr)


class S5:
    def __init__(self, nc, P, NS, NSC, dr, consts):
        self.nc, self.P, self.NS, self.NSC, self.d = nc, P, NS, NSC, dr
        self.NSB = min(128, NS)
        self.NBLK = NS // self.NSB
        self.c = consts

    def setup(self, es_keep):
        nc, P, d = self.nc, self.P, self.d
        es = ExitStack()

        def sb(name, shape, dt, keep=False):
            return (es_keep if keep else es).enter_context(nc.sbuf_tensor(name, shape, dt))

        lr = sb("lr", [128, 64], F32); li = sb("li", [128, 64], F32); ls = sb("ls", [128, 1], F32)
        br = sb("br", [128, 64, 16], F32); bi = sb("bi", [128, 64, 16], F32)
        cr = sb("cr", [128, 16, 64], F32); ci = sb("ci", [128, 16, 64], F32)
        P.dma("sp", lr[:], d["lam_re"], writes=["lr"])
        P.dma("sp", li[:], d["lam_im"], writes=["li"])
        P.dma("sp", ls[:], d["log_step"], writes=["ls"])
        P.dma("sp", br[:], d["b_re"], writes=["br"])
        P.dma("sp", bi[:], d["b_im"], writes=["bi"])
        P.dma("sp", cr[:], d["c_re"], writes=["cr"])
        P.dma("sp", ci[:], d["c_im"], writes=["ci"])
        step = sb("step", [128, 1], F32)
        drt = sb("drt", [128, 64], F32); dit = sb("dit", [128, 64], F32)
        mag = sb("mag", [128, 64], F32); sn = sb("sn", [128, 64], F32); cs = sb("cs", [128, 64], F32)
        tA = sb("tA", [128, 64], F32); tB = sb("tB", [128, 64], F32)
        PW = sb("PW", [128, 16, 2, 64], F32)
        P.op("act", lambda e: e.activation(step[:], ls[:], AF.Exp), reads=["ls"], writes=["step"])
        P.op("dve", lambda e: e.tensor_scalar(drt[:], lr[:], step[:, 0:1], None, ALU.mult), reads=["lr", "step"], writes=["drt"])
        P.op("dve", lambda e: e.tensor_scalar(dit[:], li[:], step[:, 0:1], None, ALU.mult), reads=["li", "step"], writes=["dit"])
        P.op("act", lambda e: e.activation(mag[:], drt[:], AF.Exp), reads=["drt"], writes=["mag"])
        kk = sb("kk", [128, 64], F32)
        for (dst, dn, off) in ((tA, "cm_t0", 0.0), (tB, "cm_t1", 0.5 * PI)):
            P.op("dve", lambda e: e.tensor_scalar(dst[:], dit[:], off, None, ALU.add), reads=["dit"], writes=[dn])
            P.op("dve", lambda e: e.tensor_scalar(kk[:], dst[:], PI, None, ALU.is_ge), reads=[dn], writes=["kk"])
            for m in (3, 5, 7):
                P.op("dve", lambda e: e.scalar_tensor_tensor(kk[:], dst[:], m * PI, kk[:], ALU.is_ge, ALU.add), reads=[dn, "kk"], writes=["kk"])
            P.op("dve", lambda e: e.scalar_tensor_tensor(dst[:], kk[:], -2 * PI, dst[:], ALU.mult, ALU.add), reads=[dn, "kk"], writes=[dn])
        P.op("act", lambda e: e.activation(sn[:], tA[:], AF.Sin), reads=["cm_t0"], writes=["sn"])
        P.op("act", lambda e: e.activation(cs[:], tB[:], AF.Sin), reads=["cm_t1"], writes=["cs"])
        P.op("dve", lambda e: e.tensor_tensor(PW[:, 8, 0, :], cs[:], mag[:], ALU.mult), reads=["cs", "mag"], writes=["PW8"])
        P.op("dve", lambda e: e.tensor_tensor(PW[:, 8, 1, :], sn[:], mag[:], ALU.mult), reads=["sn", "mag"], writes=["PW8"])
        P.op("dve", lambda e: e.memset(PW[:, 7, 0, :], 1.0), writes=["PW7"])
        P.op("dve", lambda e: e.memset(PW[:, 7, 1, :], 0.0), writes=["PW7"])
        for k in range(2, 9):
            cmul(P, "dve", PW[:, 7 + k, 0, :], PW[:, 7 + k, 1, :], PW[:, 6 + k, 0, :], PW[:, 6 + k, 1, :],
                 PW[:, 8, 0, :], PW[:, 8, 1, :], tA[:], tB[:], ["PW%d" % (6 + k), "PW8"], ["PW%d" % (7 + k)])
        den = sb("den", [128, 64], F32)
        P.op("dve", lambda e: e.tensor_tensor(tA[:], PW[:, 8, 0, :], PW[:, 8, 0, :], ALU.mult), reads=["PW8", "cm_t0", "cm_t1"], writes=["cm_t0"])
        P.op("dve", lambda e: e.tensor_tensor(tB[:], PW[:, 8, 1, :], PW[:, 8, 1, :], ALU.mult), reads=["PW8", "cm_t0", "cm_t1"], writes=["cm_t1"])
        P.op("dve", lambda e: e.tensor_tensor(den[:], tA[:], tB[:], ALU.add), reads=["cm_t0", "cm_t1"], writes=["den"])
        P.op("dve", lambda e: e.reciprocal(den[:], den[:]), reads=["den"], writes=["den"])
        P.op("dve", lambda e: e.tensor_tensor(PW[:, 6, 0, :], PW[:, 8, 0, :], den[:], ALU.mult), reads=["PW8", "den"], writes=["PW6"])
        P.op("dve", lambda e: e.scalar_tensor_tensor(PW[:, 6, 1, :], PW[:, 8, 1, :], -1.0, den[:], ALU.mult, ALU.mult), reads=["PW8", "den"], writes=["PW6"])
        for k in range(2, 8):
            cmul(P, "dve", PW[:, 7 - k, 0, :], PW[:, 7 - k, 1, :], PW[:, 8 - k, 0, :], PW[:, 8 - k, 1, :],
                 PW[:, 6, 0, :], PW[:, 6, 1, :], tA[:], tB[:], ["PW%d" % (8 - k), "PW6", "cm_t0", "cm_t1"], ["PW%d" % (7 - k)])
        allpw = ["PW%d" % k for k in range(16)]
        fr = sb("fr", [128, 64], F32); fi = sb("fi", [128, 64], F32); nr = sb("nr", [128, 64], F32)
        P.op("dve", lambda e: e.tensor_scalar(nr[:], PW[:, 8, 0, :], -1.0, None, ALU.add), reads=["PW8"], writes=["nr"])
        P.op("dve", lambda e: e.tensor_tensor(tA[:], lr[:], lr[:], ALU.mult), reads=["lr", "cm_t0", "cm_t1"], writes=["cm_t0"])
        P.op("dve", lambda e: e.tensor_tensor(tB[:], li[:], li[:], ALU.mult), reads=["li", "cm_t0", "cm_t1"], writes=["cm_t1"])
        P.op("dve", lambda e: e.tensor_tensor(den[:], tA[:], tB[:], ALU.add), reads=["cm_t0", "cm_t1"], writes=["den"])
        P.op("dve", lambda e: e.reciprocal(den[:], den[:]), reads=["den"], writes=["den"])
        P.op("dve", lambda e: e.tensor_tensor(tA[:], nr[:], lr[:], ALU.mult), reads=["nr", "lr", "den"], writes=["cm_t0"])
        P.op("dve", lambda e: e.tensor_tensor(tB[:], PW[:, 8, 1, :], li[:], ALU.mult), reads=["PW8", "li", "den"], writes=["cm_t1"])
        P.op("dve", lambda e: e.tensor_tensor(fr[:], tA[:], tB[:], ALU.add), reads=["cm_t0", "cm_t1"], writes=["fr"])
        P.op("dve", lambda e: e.tensor_tensor(fr[:], fr[:], den[:], ALU.mult), reads=["fr", "den"], writes=["fr"])
        P.op("dve", lambda e: e.tensor_tensor(tA[:], PW[:, 8, 1, :], lr[:], ALU.mult), reads=["PW8", "lr", "fr"], writes=["cm_t0"])
        P.op("dve", lambda e: e.tensor_tensor(tB[:], nr[:], li[:], ALU.mult), reads=["nr", "li", "fr"], writes=["cm_t1"])
        P.op("dve", lambda e: e.tensor_tensor(fi[:], tA[:], tB[:], ALU.subtract), reads=["cm_t0", "cm_t1"], writes=["fi"])
        P.op("dve", lambda e: e.tensor_tensor(fi[:], fi[:], den[:], ALU.mult), reads=["fi", "den"], writes=["fi"])
        bb = sb("bb", [128, 2, 64, 16], F32)
        t0 = sb("t0", [128, 4096], F32); t1 = sb("t1", [128, 4096], F32)
        t0b = t0[:, 0:1024].rearrange("q (p c) -> q p c", c=16); t1b = t1[:, 0:1024].rearrange("q (p c) -> q p c", c=16)
        frb = fr[:].unsqueeze(2).to_broadcast([128, 64, 16]); fib = fi[:].unsqueeze(2).to_broadcast([128, 64, 16])
        cmul(P, "dve", bb[:, 0], bb[:, 1], frb, fib, br[:], bi[:], t0b, t1b, ["fr", "fi", "br", "bi", "cm_t0", "cm_t1"], ["bb"])
        PWa = sb("PWa", [128, 8, 2, 64], F32); PWc = sb("PWc", [128, 8, 2, 64], F32); PWg = sb("PWg", [128, 8, 2, 64], F32)
        for x in range(8):
            for (tbl, nm, kf, kb) in ((PWa, "PWa", 7 - x, x), (PWc, "PWc", x + 1, 8 - x), (PWg, "PWg", x - 7, -x)):
                P.op("pool", lambda e: e.tensor_copy(tbl[0:64, x], PW[0:64, kf + 7]), reads=allpw, writes=[nm])
                P.op("pool", lambda e: e.tensor_copy(tbl[64:128, x], PW[64:128, kb + 7]), reads=allpw, writes=[nm])
        fam = sb("fam", [128, 4, 16, 2, 64], F32)
        t0f = t0[:].rearrange("q (x c p) -> q x c p", x=4, c=16); t1f = t1[:].rearrange("q (x c p) -> q x c p", x=4, c=16)

        def bx(ap3):
            return ap3.unsqueeze(2).to_broadcast([128, 4, 16, 64])

        bbr = bb[:, 0].rearrange("q p c -> q c p").unsqueeze(1).to_broadcast([128, 4, 16, 64])
        bbi = bb[:, 1].rearrange("q p c -> q c p").unsqueeze(1).to_broadcast([128, 4, 16, 64])
        crb = cr[:].unsqueeze(1).to_broadcast([128, 4, 16, 64]); cib = ci[:].unsqueeze(1).to_broadcast([128, 4, 16, 64])
        for h in range(2):
            xs = slice(4 * h, 4 * h + 4)
            for f, (tbl, nm, br_, bi_, rdn, neg) in enumerate(((PWa, "PWa", bbr, bbi, ["bb"], False), (PWc, "PWc", crb, cib, ["cr", "ci"], True),
                                                               (PWg, "PWg", crb, cib, ["cr", "ci"], True))):
                cmul(P, "dve", fam[:, :, :, 0, :], fam[:, :, :, 1, :], bx(tbl[:, xs, 0, :]), bx(tbl[:, xs, 1, :]), br_, bi_, t0f, t1f,
                     [nm] + rdn, ["fam"], neg_im=neg)
                P.fence("dve", ["fam"])
                P.dma("sp", d["SC"][f][:, xs], fam[:], reads=["fam"], writes=["SC%d" % f])
        sq = sb("sq", [128, 2, 2, 64], F32)
        P.op("dve", lambda e: e.tensor_copy(sq[:, 0], PW[:, 15]), reads=["PW15"], writes=["sq0"])
        cur = 0
        nsq = int(round(math.log2(self.NS)))
        assert 2 ** nsq == self.NS
        for s in range(nsq):
            cmul(P, "dve", sq[:, 1 - cur, 0, :], sq[:, 1 - cur, 1, :], sq[:, cur, 0, :], sq[:, cur, 1, :], sq[:, cur, 0, :], sq[:, cur, 1, :],
                 tA[:], tB[:], ["sq%d" % cur, "cm_t0", "cm_t1"], ["sq%d" % (1 - cur)])
            cur = 1 - cur
        for (R, I, src, rs) in ((self.AR2, self.AI2, PW[:, 15], "PW15"), (self.ANR, self.ANI, sq[:, cur], "sq%d" % cur)):
            nm = "AC"
            P.op("dve", lambda e: e.tensor_copy(R[:, 0, :], src[:, 0, :]), reads=[rs], writes=[nm])
            P.op("dve", lambda e: e.tensor_copy(R[:, 1, :], src[:, 0, :]), reads=[rs], writes=[nm])
            P.op("dve", lambda e: e.tensor_scalar(I[:, 0, :], src[:, 1, :], -1.0, None, ALU.mult), reads=[rs], writes=[nm])
            P.op("dve", lambda e: e.tensor_copy(I[:, 1, :], src[:, 1, :]), reads=[rs], writes=[nm])
        P.barrier()
        es.close()
    def setup_b(self):
        nc, P, d = self.nc, self.P, self.d
        es2 = ExitStack()
        CT = es2.enter_context(nc.sbuf_tensor("CTb", [128, 128, 128], BF16))
        DT = es2.enter_context(nc.sbuf_tensor("DTb", [128, 64, 128], BF16))
        XC = es2.enter_context(nc.sbuf_tensor("XC", [128, 128, 128], BF16))
        XG = XC
        BT = XC
        GB = es2.enter_context(nc.sbuf_tensor("GB", [128, 128, 128], BF16))
        GT = es2.enter_context(nc.sbuf_tensor("GT", [128, 128, 128], BF16))
        tmpd = es2.enter_context(nc.sbuf_tensor("tmpd", [128, 4, 128], F32))
        tmpe = es2.enter_context(nc.sbuf_tensor("tmpe", [128, 4, 128], F32))
        pT = es2.enter_context(nc.psum_tensor("pT", [128, 4, 4, 128], F32))
        pD = es2.enter_context(nc.psum_tensor("pD", [128, 2, 2, 4, 128], F32))
        idb = self.c["idb"]
        k = 0
        for f, (srcT, sn_, dstT, dn_) in enumerate(((BT, "XC", GB, "GB"), (XC, "XC", CT, "CT"), (XG, "XC", GT, "GT"))):
            src = d["SC"][f].rearrange("q x c s -> (x c) q s")
            for h in range(4):
                P.dma("pool", srcT[:, 32 * h:32 * h + 32, :], src[:, 32 * h:32 * h + 32, :], reads=["SC%d" % f], writes=[sn_ + str(h)])
            for q4 in range(32):
                slot = k % 4
                k += 1
                for u in range(4):
                    q = q4 * 4 + u
                    P.op("pe", lambda e: e.matmul(pT[:, slot, u, :], lhsT=srcT[:, q, :], rhs=idb[:], start=True, stop=True), reads=[sn_ + str(q // 32), "idb"], writes=["pT%d" % slot])
                P.op("act", lambda e: e.activation(dstT[:, q4 * 4:q4 * 4 + 4, :], pT[:, slot], AF.Copy), reads=["pT%d" % slot], writes=[dn_])
        self._b2 = lambda: self._setup_b2(GB, GT, CT, DT, tmpd, tmpe, pD)
        return es2

    def _setup_b2(self, GB, GT, CT, DT, tmpd, tmpe, pD):
        nc, P, d = self.nc, self.P, self.d
        ML, MU, idf, dcol = self.c["ML"], self.c["MU"], self.c["idf"], self.c["dcol"]
        MLb = ML[:].unsqueeze(1).to_broadcast([128, 4, 128]); MUb = MU[:].unsqueeze(1).to_broadcast([128, 4, 128])
        idb4 = idf[:].unsqueeze(1).to_broadcast([128, 4, 128])
        for g4 in range(16):
            s = g4 % 2
            for u in range(4):
                g = g4 * 4 + u
                P.op("pe", lambda e: e.matmul(pD[:, s, 0, u, :], lhsT=GB[:, g, :], rhs=GT[:, g, :], start=True, stop=True), reads=["GB", "GT"], writes=["pDf%d" % s])
            for u in range(4):
                g = g4 * 4 + u
                P.op("pe", lambda e: e.matmul(pD[:, s, 1, u, :], lhsT=GB[:, 64 + g, :], rhs=GT[:, 64 + g, :], start=True, stop=True), reads=["GB", "GT"], writes=["pDb%d" % s])
            P.op("act", lambda e: e.activation(tmpd[:], pD[:, s, 0], AF.Copy), reads=["pDf%d" % s], writes=["tmpd"])
            P.op("act", lambda e: e.activation(tmpe[:], pD[:, s, 1], AF.Copy), reads=["pDb%d" % s], writes=["tmpe"])
            P.op("pool", lambda e: e.tensor_tensor(tmpd[:], tmpd[:], MLb, ALU.mult), reads=["tmpd", "ML"], writes=["tmpd"])
            P.op("pool", lambda e: e.tensor_tensor(tmpe[:], tmpe[:], MUb, ALU.mult), reads=["tmpe", "MU"], writes=["tmpe"])
            P.op("pool", lambda e: e.tensor_tensor(tmpd[:], tmpd[:], tmpe[:], ALU.add), reads=["tmpd", "tmpe"], writes=["tmpd"])
            dcb = dcol[:, g4 * 4:g4 * 4 + 4].unsqueeze(2).to_broadcast([128, 4, 128])
            P.op("pool", lambda e: e.tensor_tensor(tmpe[:], idb4, dcb, ALU.mult), reads=["tmpd", "idf", "dcol"], writes=["tmpe"])
            P.op("pool", lambda e: e.tensor_tensor(DT[:, g4 * 4:g4 * 4 + 4, :], tmpe[:], tmpd[:], ALU.add), reads=["tmpd", "tmpe"], writes=["DT"])
        P.fence("pool", ["DT"]); P.fence("act", ["CT"])
        P.dma("pool", d["CTd"], CT[:], reads=["CT"], writes=["CTd"])
        P.dma("pool", d["DTd"], DT[:], reads=["DT"], writes=["DTd"])

    def scan_steps(self, SG, ZG, n, store):
        P = self.P
        tA, tB = self.sc_tA, self.sc_tB
        for i in range(n):
            a = i if store else i % 2
            b = i + 1 if store else (i + 1) % 2
            S = SG[:, a]
            Ssw = bass.AP(SG[:].tensor, SG[:, a, 1, :].offset, [list(SG[:].ap[0]), [-64, 2], [1, 64]])
            P.op("dve", lambda e: e.tensor_tensor(tA[:], self.AR2[:], S, ALU.mult), reads=["SGs%d" % a, "AC"], writes=["sc_tA"])
            P.op("dve", lambda e: e.tensor_tensor(tB[:], self.AI2[:], Ssw, ALU.mult), reads=["SGs%d" % a, "AC"], writes=["sc_tB"])
            P.op("dve", lambda e: e.tensor_tensor(tA[:], tA[:], tB[:], ALU.add), reads=["sc_tA", "sc_tB"], writes=["sc_tA"])
            P.op("dve", lambda e: e.tensor_tensor(SG[:, b], tA[:], ZG[:, i], ALU.add), reads=["sc_tA", "ZG"], writes=["SGs%d" % b])
        return (n if store else n % 2)


    def phase_z(self):
        nc, P, d, c = self.nc, self.P, self.d, self.c
        NS, NSC, NSB, NBLK = self.NS, self.NSC, self.NSB, self.NBLK
        idb, Jb = c["idb"], c["Jb"]
        es = ExitStack()

        def sb(name, shape, dt):
            return es.enter_context(nc.sbuf_tensor(name, shape, dt))

        U = sb("U5", [NSB, NBLK, 64, 128], BF16); Uc = sb("Uc5", [NSC, 64, 128], BF16)
        BT = sb("BT5", [128, 128, 128], BF16)
        UT = sb("UT", [128, 64, NS], BF16); UTr = sb("UTr", [128, 64, NS], BF16)
        UcT = sb("UcT", [128, 64, NSC], BF16); UcTr = sb("UcTr", [128, 64, NSC], BF16)
        ZR = sb("ZR", [128, 4, 8, 128], F32)
        pU = es.enter_context(nc.psum_tensor("pU", [128, 4, 4, 128], F32))
        P.dma("sp", U[:], d["U_d"], reads=["U_d"], writes=["U"])
        P.dma("sp", Uc[:], d["Uc_d"], reads=["Uc_d"], writes=["Uc"])
        srcb = d["SC"][0].rearrange("q x c s -> (x c) q s")
        stgz = sb("stgz", [128, 2, 16, 128], F32)
        for h in range(8):
            zs = h % 2
            P.dma("sp", stgz[:, zs], srcb[:, 16 * h:16 * h + 16, :], reads=["SC0"], writes=["stgz%d" % zs])
            if h % 2 == 0:
                P.op("act", lambda e: e.activation(BT[:, 16 * h:16 * h + 16, :], stgz[:, zs], AF.Copy), reads=["stgz%d" % zs], writes=["BT"])
            else:
                P.op("dve", lambda e: e.tensor_copy(BT[:, 16 * h:16 * h + 16, :], stgz[:, zs]), reads=["stgz%d" % zs], writes=["BT"])
        kslot = [0]

        def transposes(src_fn, nrows, nblk, dstT, dstTr, sname, dname):
            Isub = idb[0:nrows, 0:nrows]; Jsub = Jb[0:nrows, 128 - nrows:128]
            for blk in range(nblk):
                for g4 in range(16):
                    for (rhs, dst, col0, tag) in ((Isub, dstT, blk * nrows, "n"), (Jsub, dstTr, (nblk - 1 - blk) * nrows, "r")):
                        slot = kslot[0] % 4; kslot[0] += 1
                        for u in range(4):
                            g = g4 * 4 + u
                            P.op("pe", lambda e: e.matmul(pU[:, slot, u, 0:nrows], lhsT=src_fn(blk, g), rhs=rhs, start=True, stop=True),
                                 reads=[sname, "idb", "Jb"], writes=["pU%d" % slot])
                        if kslot[0] % 2 == 0:
                            P.op("act", lambda e: e.activation(dst[:, g4 * 4:g4 * 4 + 4, col0:col0 + nrows], pU[:, slot, :, 0:nrows], AF.Copy),
                                 reads=["pU%d" % slot], writes=[dname + tag + "_%d" % g4])
                        else:
                            P.op("dve", lambda e: e.tensor_copy(dst[:, g4 * 4:g4 * 4 + 4, col0:col0 + nrows], pU[:, slot, :, 0:nrows]),
                                 reads=["pU%d" % slot], writes=[dname + tag + "_%d" % g4])

        transposes(lambda blk, g: U[0:NSB, blk, g, :], NSB, NBLK, UT, UTr, "U", "UT")
        transposes(lambda blk, g: Uc[0:NSC, g, :], NSC, 1, UcT, UcTr, "Uc", "UcT")

        def zrows(T, Tr, nrows, nblk, Zd, tname, zname):
            for dd in range(2):
                src = T if dd == 0 else Tr
                for blk in range(nblk):
                    for g4 in range(16):
                        slot = kslot[0] % 4; kslot[0] += 1
                        for u in range(4):
                            g = g4 * 4 + u
                            P.op("pe", lambda e: e.matmul(pU[0:nrows, slot, u, :], lhsT=src[:, g, blk * nrows:(blk + 1) * nrows],
                                                          rhs=BT[:, dd * 64 + g, :], start=True, stop=True),
                                 reads=[tname + ("n" if dd == 0 else "r") + "_%d" % g4, "BT"], writes=["pU%d" % slot])
                        zb = (g4 // 2) % 4
                        P.op("act", lambda e: e.activation(ZR[0:nrows, zb, (g4 % 2) * 4:(g4 % 2) * 4 + 4, :], pU[0:nrows, slot], AF.Copy),
                             reads=["pU%d" % slot], writes=["ZR%d" % zb])
                        if g4 % 2 == 1:
                            gg = (g4 // 2) * 8
                            P.fence("act", ["ZR%d" % zb])
                            P.dma("sp", Zd[dd, blk * nrows:(blk + 1) * nrows, gg:gg + 8], ZR[0:nrows, zb], reads=["ZR%d" % zb], writes=[zname])

        zrows(UcT, UcTr, NSC, 1, d["Zc"], "UcT", "Zc")
        zrows(UT, UTr, NSB, NBLK, d["Z"], "UT", "Z")
        utn_all = ["UTn_%d" % i_ for i_ in range(16)]
        P.fence("act", utn_all); P.fence("dve", utn_all)
        P.dma("sp", d["UT_d"], UT[:], reads=utn_all, writes=["UT_d"])
        P.barrier()
        es.close()

    def phase_scan(self, flags, pre_emit=None):
        nc, P, d = self.nc, self.P, self.d
        NS, NSC = self.NS, self.NSC
        es = ExitStack()

        def sb(name, shape, dt):
            return es.enter_context(nc.sbuf_tensor(name, shape, dt))

        SGc = sb("SGc", [128, 2, 2, 64], F32); SGp = sb("SGp", [128, 2, 2, 64], F32)
        CH = min(16, NS)
        ZG = sb("ZG", [128, 2, CH, 2, 64], F32)
        SG = sb("SG", [128, 2, CH + 1, 2, 64], F32)
        self.sc_tA = sb("sc_tA", [128, 2, 64], F32); self.sc_tB = sb("sc_tB", [128, 2, 64], F32)
        EG = sb("EG", [128, 4, 128], F32); Es = sb("Es", [128, 2, 64], F32); acc = sb("acc", [128, 2, 64], F32)
        cand = sb("cand", [128, 2, 64], F32)
        zgk = [0]
        es_pre = pre_emit() if pre_emit is not None else None

        def load_zg(Zd, n0, n, zname):
            b = zgk[0] % 2; zgk[0] += 1
            for dd in range(2):
                P.dma("sp", ZG[dd * 64:(dd + 1) * 64, b, 0:n].rearrange("q n r p -> q n (r p)"),
                      Zd[dd, n0:n0 + n].rearrange("n g s -> g n s"), reads=[zname], writes=["ZG%d" % b])
            return b

        def steps(SGt, b, n, store, pre="SGs"):
            tA, tB = self.sc_tA, self.sc_tB
            fz = P._fz

            def spacer():
                P.op("dve", lambda e: e.memset(fz[:, 4:5], 0.0), writes=["fz_sp"], nosame=True)

            P.op("dve", lambda e: e.memset(fz[:, 5:6], 0.0), reads=[pre + str(i) for i in range(CH + 1)] + ["sc_tA", "sc_tB"], writes=["fz_sp2"])
            spacer()
            for i in range(n):
                a = i if store else i % 2
                bb = i + 1 if store else (i + 1) % 2
                S = SGt[:, a]
                Ssw = bass.AP(SGt[:].tensor, SGt[:, a, 1, :].offset, [list(SGt[:].ap[0]), [-64, 2], [1, 64]])
                P.op("dve", lambda e: e.tensor_tensor(tB[:], self.AI2[:], Ssw, ALU.mult), reads=[pre + str(a), "AC"], writes=["sc_tB"], nosame=True)
                P.op("dve", lambda e: e.tensor_tensor(tA[:], self.AR2[:], S, ALU.mult), reads=[pre + str(a), "AC"], writes=["sc_tA"], nosame=True)
                P.op("dve", lambda e: e.tensor_tensor(tB[:], tB[:], ZG[:, b, i], ALU.add), reads=["sc_tB", "ZG%d" % b], writes=["sc_tB"], nosame=True)
                spacer()
                P.op("dve", lambda e: e.tensor_tensor(SGt[:, bb], tA[:], tB[:], ALU.add), reads=["sc_tA", "sc_tB"], writes=[pre + str(bb)], nosame=True)
                spacer()
            P.op("dve", lambda e: e.memset(fz[:, 5:6], 0.0), reads=[pre + str(i) for i in range(CH + 1)] + ["sc_tA", "sc_tB", "fz_sp"], writes=["fz_sp2"] + [pre + str(i) for i in range(CH + 1)])

        P.op("dve", lambda e: e.memset(SGc[:, 0], 0.0), writes=["SGs0"])
        for ch in range(NSC // CH):
            b = load_zg(d["Zc"], ch * CH, CH, "Zc")
            steps(SGc, b, CH, False)
        P.op("dve", lambda e: e.tensor_copy(acc[:], SGc[:, 0]), reads=["SGs0"], writes=["acc"])
        P.op("dve", lambda e: e.memset(SGp[:, 0], 0.0), writes=["SGs0"], reads=["SGs0", "SGs1"])
        for ch in range(NS // CH):
            b = load_zg(d["Z"], ch * CH, CH, "Z")
            steps(SGp, b, CH, False)
        P.fence("dve", ["SGs0"])
        t = P.dma("sp", d["Ein"], SGp[:, 0].rearrange("q r p -> q (r p)"), reads=["SGs0"], writes=["Ein"])
        P.collective("AllGather", [[0, 1, 2, 3], [4, 5, 6, 7]], d["Ein"], d["Eout"], ["Ein"], ["Eout"], t)
        if es_pre is not None:
            self._b2()
        P.dma("sp", EG[:], d["Eout"].rearrange("(r q) s -> q r s", q=128), reads=["Eout"], writes=["EG"])
        tA, tB = self.sc_tA, self.sc_tB
        for jj in range(3):
            P.op("dve", lambda e: e.tensor_copy(Es[0:64], EG[0:64, jj, :].rearrange("q (r p) -> q r p", r=2)), reads=["EG"], writes=["Es"])
            P.op("dve", lambda e: e.tensor_copy(Es[64:128], EG[64:128, 3 - jj, :].rearrange("q (r p) -> q r p", r=2)), reads=["EG"], writes=["Es"])
            accsw = bass.AP(acc[:].tensor, acc[:, 1, :].offset, [list(acc[:].ap[0]), [-64, 2], [1, 64]])
            P.op("dve", lambda e: e.tensor_tensor(tA[:], self.ANR[:], acc[:], ALU.mult), reads=["acc", "AC"], writes=["sc_tA"])
            P.op("dve", lambda e: e.tensor_tensor(tB[:], self.ANI[:], accsw, ALU.mult), reads=["acc", "AC"], writes=["sc_tB"])
            P.op("dve", lambda e: e.tensor_tensor(tA[:], tA[:], tB[:], ALU.add), reads=["sc_tA", "sc_tB"], writes=["sc_tA"])
            P.op("dve", lambda e: e.tensor_tensor(cand[:], tA[:], Es[:], ALU.add), reads=["sc_tA", "Es"], writes=["cand"])
            P.op("dve", lambda e: e.tensor_tensor(cand[:], cand[:], acc[:], ALU.subtract), reads=["cand", "acc"], writes=["cand"])
            P.op("dve", lambda e: e.scalar_tensor_tensor(acc[:], cand[:], flags[:, jj:jj + 1], acc[:], ALU.mult, ALU.add),
                 reads=["cand", "acc", "flags"], writes=["acc"])
        names = [["SGA%d" % i for i in range(CH + 1)], ["SGB%d" % i for i in range(CH + 1)]]
        P.op("dve", lambda e: e.tensor_copy(SG[:, 0, 0], acc[:]), reads=["acc"], writes=[names[0][0]])
        bnext = load_zg(d["Z"], 0, CH, "Z")
        for ch in range(NS // CH):
            b = bnext
            kb_ = ch % 2
            pre = "SGA" if kb_ == 0 else "SGB"
            steps(SG[:, kb_], b, CH, True, pre)
            if ch + 1 < NS // CH:
                bnext = load_zg(d["Z"], (ch + 1) * CH, CH, "Z")
            allr = names[kb_]
            P.fence("dve", allr)
            for dd in range(2):
                P.dma("sp", d["SD"][dd, ch * CH:(ch + 1) * CH].rearrange("n g s -> g n s"),
                      SG[dd * 64:(dd + 1) * 64, kb_, 0:CH].rearrange("q n r p -> q n (r p)"), reads=allr, writes=["SD"])
            if ch + 1 < NS // CH:
                P.op("dve", lambda e: e.tensor_copy(SG[:, 1 - kb_, 0], SG[:, kb_, CH]), reads=[names[kb_][CH]], writes=[names[1 - kb_][0]])
        P.barrier()
        if es_pre is not None:
            es_pre.close()
        es.close()

    def phase_read(self):
        nc, P, d, c = self.nc, self.P, self.d, self.c
        NS, NSB, NBLK = self.NS, self.NSB, self.NBLK
        idb, Jb = c["idb"], c["Jb"]
        es = ExitStack()

        def sb(name, shape, dt):
            return es.enter_context(nc.sbuf_tensor(name, shape, dt))

        UT = sb("UT7", [128, 64, NS], BF16)
        ST = sb("ST", [128, 2, 64, NS], BF16)
        CT = sb("CT7", [128, 128, 128], BF16); DT = sb("DT7", [128, 64, 128], BF16)
        SRb = sb("SRb", [128, 2, 32, 128], BF16)
        YG = sb("YG", [NSB, NBLK, 8, 1024], BF16)
        pU = es.enter_context(nc.psum_tensor("pU7", [128, 4, 4, 128], F32))
        pY = es.enter_context(nc.psum_tensor("pY", [128, 2, 4, 128], F32))
        P.dma("sp", UT[:], d["UT_d"], reads=["UT_d"], writes=["UT7"])
        P.dma("sp", CT[:], d["CTd"], reads=["CTd"], writes=["CT7"])
        P.dma("sp", DT[:], d["DTd"], reads=["DTd"], writes=["DT7"])
        kslot = 0
        kb = 0
        for dd in range(2):
            for blk in range(NBLK):
                nat = blk if dd == 0 else NBLK - 1 - blk
                rhs = idb[0:NSB, 0:NSB] if dd == 0 else Jb[0:NSB, 128 - NSB:128]
                for gh in range(2):
                    bsel = kb % 2; kb += 1
                    P.dma("pool", SRb[0:NSB, bsel], d["SD"][dd, blk * NSB:(blk + 1) * NSB, gh * 32:(gh + 1) * 32], reads=["SD"], writes=["SRb%d" % bsel])
                    for g4 in range(8):
                        slot = kslot % 4; kslot += 1
                        for u in range(4):
                            P.op("pe", lambda e: e.matmul(pU[:, slot, u, 0:NSB], lhsT=SRb[0:NSB, bsel, g4 * 4 + u, :], rhs=rhs, start=True, stop=True),
                                 reads=["SRb%d" % bsel, "idb", "Jb"], writes=["pU%d" % slot])
                        G0 = gh * 32 + g4 * 4
                        if g4 % 2 == 0:
                            P.op("act", lambda e: e.activation(ST[:, dd, G0:G0 + 4, nat * NSB:(nat + 1) * NSB], pU[:, slot, :, 0:NSB], AF.Copy),
                                 reads=["pU%d" % slot], writes=["ST%d_%d" % (dd, G0 // 4)])
                        else:
                            P.op("dve", lambda e: e.tensor_copy(ST[:, dd, G0:G0 + 4, nat * NSB:(nat + 1) * NSB], pU[:, slot, :, 0:NSB]),
                                 reads=["pU%d" % slot], writes=["ST%d_%d" % (dd, G0 // 4)])
        for blk in range(NBLK):
            ns = slice(blk * NSB, (blk + 1) * NSB)
            for g4 in range(16):
                slot = g4 % 2
                for u in range(4):
                    g = g4 * 4 + u
                    P.op("pe", lambda e: e.matmul(pY[0:NSB, slot, u, :], lhsT=ST[:, 0, g, ns], rhs=CT[:, g, :], start=True, stop=False),
                         reads=["ST0_%d" % g4, "CT7"], writes=["pY%d" % slot])
                    P.op("pe", lambda e: e.matmul(pY[0:NSB, slot, u, :], lhsT=ST[:, 1, g, ns], rhs=CT[:, 64 + g, :], start=False, stop=False),
                         reads=["ST1_%d" % g4, "CT7"], writes=["pY%d" % slot])
                    P.op("pe", lambda e: e.matmul(pY[0:NSB, slot, u, :], lhsT=UT[:, g, ns], rhs=DT[:, g, :], start=False, stop=True),
                         reads=["UT7", "DT7"], writes=["pY%d" % slot])
                outv = YG[0:NSB, blk, :, g4 * 64:(g4 + 1) * 64].rearrange("n i (u c) -> n u i c", u=4)
                inv = pY[0:NSB, slot].rearrange("n u (i c) -> n u i c", i=8)
                P.op("act", lambda e: e.activation(outv, inv, AF.Gelu), reads=["pY%d" % slot], writes=["YG"])
        P.fence("act", ["YG"])
        for blk in range(NBLK):
            P.dma("sp", d["YG_d"][blk * NSB * 8:(blk + 1) * NSB * 8].rearrange("(n i) c -> n i c", i=8), YG[0:NSB, blk], reads=["YG"], writes=["YG_d"])
        P.barrier()
        es.close()


D = 2048
LN_EPS = 1e-6
ALPHA = 2.0 ** 0.25


def bcast_rows(ap_flat, n):
    return bass.AP(ap_flat.tensor, ap_flat.offset, [[0, 128], [1, n]])


class WL:
    def __init__(self, P, stg, nm):
        self.P, self.stg, self.nm, self.k = P, stg, nm, 0

    def dma(self, src, pat=None, **kw):
        s = self.k % 3; self.k += 1
        n = 1
        for dmn in src.shape[1:]:
            n *= dmn
        view = self.stg[:, s, 0:n]
        if pat is not None:
            view = view.rearrange(pat, **kw)
        self.P.dma("sp", view, src, writes=["%s%d" % (self.nm, s)])
        return (s, view)

    def cast(self, eng, h, dst, dst_res):
        s, view = h
        if eng == "act":
            self.P.op("act", lambda e: e.activation(dst, view, AF.Copy), reads=["%s%d" % (self.nm, s)], writes=dst_res)
        else:
            self.P.op(eng, lambda e: e.tensor_copy(dst, view), reads=["%s%d" % (self.nm, s)], writes=dst_res)


def build(NT, NCTX=256, debug=False, stop_after=None):
    nc = bass.Bass("TRN2", target_bir_lowering=False)
    NS, NSC = NT // 8, NCTX // 8
    NSB = min(128, NS); NBLK = NS // NSB
    NTT = NT // 128
    TB = min(512, NT)
    NTB = NT // TB

    def din(name, shape, dt=F32):
        return nc.dram_tensor(name, shape, dt, kind="ExternalInput").ap()

    def dsc(name, shape, dt=F32):
        return nc.dram_tensor(name, shape, dt).ap()

    x = din("x", [NT, D]); ctx = din("ctx", [NCTX, D]); cT = din("cT", [128, 16, 2])
    w_ada = din("w_ada", [D, 1536]); b_adaT = din("b_adaT", [128, 12]); w_in = din("w_in", [40, 128, 16, 128])
    sgu_g = din("sgu_g", [1, 1024]); sgu_b = din("sgu_b", [1, 1024])
    w_sp = din("w_sp", [8, 128, 128]); b_sp = din("b_sp", [1, 1024])
    w_glu = din("w_glu", [1024, 1024]); b_gluT = din("b_gluT", [128, 8]); w_out = din("w_out", [D, D])
    ln_g = din("ln_g", [1, D]); ln_b = din("ln_b", [1, D])
    dr = {}
    for nm, shp in (("lam_re", [128, 64]), ("lam_im", [128, 64]), ("log_step", [128, 1]), ("b_re", [128, 64, 16]),
                    ("b_im", [128, 64, 16]), ("c_re", [128, 16, 64]), ("c_im", [128, 16, 64])):
        dr[nm] = din(nm, shp)
    cd = {}
    for nm, shp, dt in (("idb", [128, 128], BF16), ("Jb", [128, 128], BF16), ("idf", [128, 128], F32), ("ML", [128, 128], F32), ("MU", [128, 128], F32),
                        ("dcol", [128, 64], F32), ("flags", [128, 3], F32)):
        cd[nm] = din("c_" + nm, shp, dt)
    y = nc.dram_tensor("y", [NT, D], F32, kind="ExternalOutput").ap()
    dr["SC"] = [dsc("SC%d" % f, [128, 8, 16, 128]) for f in range(3)]
    dr["BTd"] = dsc("BTd", [128, 128, 128], BF16); dr["CTd"] = dsc("CTd", [128, 128, 128], BF16); dr["DTd"] = dsc("DTd", [128, 64, 128], BF16)
    dr["Z"] = dsc("Zs", [2, NS, 64, 128]); dr["Zc"] = dsc("Zcs", [2, NSC, 64, 128]); dr["SD"] = dsc("SDs", [2, NS, 64, 128])
    dr["Ein"] = dsc("Ein", [128, 128]); dr["Eout"] = dsc("Eout", [512, 128])
    dr["U_d"] = dsc("U_d", [NSB, NBLK, 64, 128], BF16); dr["Uc_d"] = dsc("Uc_d", [NSC, 64, 128], BF16)
    dr["UT_d"] = dsc("UT_d", [128, 64, NS], BF16); dr["YG_d"] = dsc("YG_d", [NT, 1024], BF16)
    gsc = dsc("gsc", [16, 128]); mIn = dsc("mIn", [128, 24]); mOut = dsc("mOut", [512, 24]); YA_d = dsc("YA_d", [8, 128, NT], BF16); ZB_d = dsc("ZB_d", [8, 128, NT], BF16)
    dbg = {}
    if debug:
        for nm, shp, dt in (("xmT", [128, 16, NT], BF16), ("YA", [8, 128, NT], BF16), ("U", [NSB, NBLK, 64, 128], BF16), ("YG", [NT, 1024], BF16),
                            ("YB", [128, 8, NT], BF16), ("modT", [128, 48, 2], F32)):
            dbg[nm] = nc.dram_tensor("dbg_" + nm, shp, dt, kind="ExternalOutput").ap()

    P = Prog(nc, n_dma_sems=12)
    P.op("dve", lambda e: e.memset(P._fz[:], 0.0), writes=["fence_z"])
    keep = ExitStack()

    def kb(name, shape, dt):
        return keep.enter_context(nc.sbuf_tensor(name, shape, dt))

    consts = {}
    for nm in cd:
        consts[nm] = kb("k_" + nm, list(cd[nm].shape), cd[nm].dtype)
        P.dma("sp", consts[nm][:], cd[nm], writes=[nm])
    idb, idf = consts["idb"], consts["idf"]
    modT = kb("modT", [128, 48, 2], F32)
    S1 = kb("S1", [128, 16, 2], F32)
    bada = kb("bada", [128, 12], F32); bglu = kb("bglu", [128, 8], F32)
    P.dma("sp", bada[:], b_adaT, writes=["bada"]); P.dma("sp", bglu[:], b_gluT, writes=["bglu"])
    s5 = S5(nc, P, NS, NSC, dr, consts)

    s5.AR2 = kb("AR2", [128, 2, 64], F32); s5.AI2 = kb("AI2", [128, 2, 64], F32)
    s5.ANR = kb("ANR", [128, 2, 64], F32); s5.ANI = kb("ANI", [128, 2, 64], F32)
    es = ExitStack()
    cTt = es.enter_context(nc.sbuf_tensor("cTt", [128, 16, 2], F32))
    Wt = es.enter_context(nc.sbuf_tensor("Wt", [128, 2, 16, 384], F32))
    gt = es.enter_context(nc.sbuf_tensor("gt", [128, 16], F32)); gt2 = es.enter_context(nc.sbuf_tensor("gt2", [16, 128], F32))
    modP = es.enter_context(nc.sbuf_tensor("modP", [128, 12, 2], F32))
    P.dma("sp", cTt[:], cT, writes=["cTt"])
    wsrc = w_ada.rearrange("(kt p) n -> p kt n", p=128)
    for cb in range(2):
        P.dma("sp", Wt[:, cb], wsrc[:, :, cb * 384:(cb + 1) * 384], writes=["Wt%d" % cb])
    s5.setup(keep)
    if stop_after == "p1":
        P.barrier()
        es.close(); keep.close(); P.close()
        return nc

    P.op("act", lambda e: e.activation(cTt[:], cTt[:], AF.Silu), reads=["cTt"], writes=["cTt"])
    pM = es.enter_context(nc.psum_tensor("pM", [128, 2, 512], F32))
    for cb in range(4):
        wb = cb % 2
        if cb >= 2:
            P.dma("sp", Wt[:, wb], wsrc[:, :, cb * 384:(cb + 1) * 384], writes=["Wt%d" % wb])
        for c3 in range(3):
            ct = cb * 3 + c3
            slot = ct % 2
            for kt in range(16):
                P.op("pe", lambda e: e.matmul(pM[:, slot, 0:2], lhsT=Wt[:, wb, kt, c3 * 128:(c3 + 1) * 128], rhs=cTt[:, kt, :],
                                              start=(kt == 0), stop=(kt == 15)), reads=["Wt%d" % wb, "cTt"], writes=["pM%d" % slot])
            P.op("act", lambda e: e.activation(modP[:, ct, :], pM[:, slot, 0:2], AF.Identity, bias=bada[:, ct:ct + 1]),
                 reads=["pM%d" % slot, "bada"], writes=["modP"])
    P.fence("act", ["modP"])
    tg = P.dma("sp", mIn, modP[:].rearrange("p c t -> p (c t)"), reads=["modP"], writes=["mIn"])
    P.collective("AllGather", [[0, 1, 2, 3], [4, 5, 6, 7]], mIn, mOut, ["mIn"], ["mOut"], tg)
    P.dma("sp", modT[:].rearrange("p (r c) t -> p r (c t)", r=4), mOut.rearrange("(r p) f -> p r f", p=128), reads=["mOut"], writes=["modT"])
    P.op("dve", lambda e: e.tensor_scalar(S1[:], modT[:, 16:32, :], 1.0, None, ALU.add), reads=["modT"], writes=["S1"])
    P.op("dve", lambda e: e.tensor_copy(gt[:], modT[:, 32:48, 0]), reads=["modT"], writes=["gt"])
    P.op("pe", lambda e: e.transpose(pM[0:16, 0, 0:128], gt[:], idf[:]), reads=["gt", "idf", "pM0"], writes=["pM0"])
    P.op("act", lambda e: e.activation(gt2[:], pM[0:16, 0, 0:128], AF.Copy), reads=["pM0"], writes=["gt2"])
    P.fence("act", ["gt2"])
    P.dma("sp", gsc, gt2[:], reads=["gt2"], writes=["gsc"])
    if debug:
        P.fence("act", ["modT"])
        P.dma("sp", dbg["modT"], modT[:], reads=["modT"])
    P.barrier()
    es.close()
    if stop_after == 'p2':
        P.barrier()
        keep.close(); P.close()
        return nc

    esA = ExitStack()
    xmT = esA.enter_context(nc.sbuf_tensor("xmT", [128, 16, NT], BF16))
    xcT = esA.enter_context(nc.sbuf_tensor("xcT", [128, 16, NCTX], BF16))
    es = ExitStack()
    xt = es.enter_context(nc.sbuf_tensor("xt", [128, 2, D], F32))
    st6 = es.enter_context(nc.sbuf_tensor("st6", [128, 4, 6], F32)); mv = es.enter_context(nc.sbuf_tensor("mv", [128, 2], F32))
    rstd = es.enter_context(nc.sbuf_tensor("rstd", [128, 1], F32))
    pX = es.enter_context(nc.psum_tensor("pX", [128, 4, 4, 128], F32))

    def ln_stats(src, nm):
        for q in range(4):
            P.op("dve", lambda e: e.bn_stats(st6[:, q, :], src[:, q * 512:(q + 1) * 512]), reads=[nm], writes=["st6"])
        P.op("dve", lambda e: e.bn_aggr(mv[:], st6[:]), reads=["st6"], writes=["mv"])
        P.op("dve", lambda e: e.tensor_scalar(rstd[:], mv[:, 1:2], LN_EPS, None, ALU.add), reads=["mv"], writes=["rstd"])
        P.op("act", lambda e: e.activation(rstd[:], rstd[:], AF.Sqrt), reads=["rstd"], writes=["rstd"])
        P.op("dve", lambda e: e.reciprocal(rstd[:], rstd[:]), reads=["rstd"], writes=["rstd"])

    ks = 0
    for t in range(NTT + NCTX // 128):
        isx = t < NTT
        src = x[t * 128:(t + 1) * 128] if isx else ctx[(t - NTT) * 128:(t - NTT + 1) * 128]
        dstT, tt, col = (xmT, t, 0) if isx else (xcT, t - NTT, 1)
        b = t % 2
        P.dma("sp", xt[:, b], src, writes=["xt%d" % b])
        ln_stats(xt[:, b], "xt%d" % b)
        P.op("dve", lambda e: e.tensor_scalar(xt[:, b], xt[:, b], mv[:, 0:1], rstd[:, 0:1], ALU.subtract, ALU.mult), reads=["xt%d" % b, "mv", "rstd"], writes=["xt%d" % b])
        for k4 in range(4):
            slot = ks % 4; ks += 1
            for u in range(4):
                kt = k4 * 4 + u
                P.op("pe", lambda e: e.transpose(pX[:, slot, u, :], xt[:, b, kt * 128:(kt + 1) * 128], idf[:]), reads=["xt%d" % b, "idf"], writes=["pX%d" % slot])
            for u in range(4):
                kt = k4 * 4 + u
                P.op("act", lambda e: e.activation(dstT[:, kt, tt * 128:(tt + 1) * 128], pX[:, slot, u, :], AF.Identity,
                                                   scale=S1[:, kt, col:col + 1], bias=modT[:, kt, col:col + 1]),
                     reads=["pX%d" % slot, "S1", "modT"], writes=["xmT" if isx else "xcT"])
    if debug:
        P.fence("act", ["xmT"])
        P.dma("sp", dbg["xmT"], xmT[:], reads=["xmT"])
    P.barrier()
    es.close()
    if stop_after == 'p3':
        P.barrier()
        esA.close()
        keep.close(); P.close()
        return nc

    es = ExitStack()
    BIGN = max(8 * NT, NBLK * 8192)
    BIG = es.enter_context(nc.sbuf_tensor("BIG", [128, BIGN], BF16))
    MXF = BIG[:, 0:8 * NT].rearrange("p (c t) -> p c t", c=8)
    Wv = es.enter_context(nc.sbuf_tensor("Wv", [128, 16, 1024], BF16))
    Wu = es.enter_context(nc.sbuf_tensor("Wu", [128, 3, 16, 128], BF16))
    stg4 = es.enter_context(nc.sbuf_tensor("stg4", [128, 3, 2048], F32))
    wl = WL(P, stg4, "stg4_")
    Ucs = stg4[0:NSC].rearrange("p a b -> p (a b)").bitcast(BF16)[:, 0:8192].rearrange("p (g s) -> p g s", g=64)
    WsT = es.enter_context(nc.sbuf_tensor("WsT", [128, 8, 128], BF16))
    grow = es.enter_context(nc.sbuf_tensor("grow", [128, 1024], F32)); brow = es.enter_context(nc.sbuf_tensor("brow", [128, 1024], F32))
    bsrow = es.enter_context(nc.sbuf_tensor("bsrow", [128, 8, 128], F32)); mxt = es.enter_context(nc.sbuf_tensor("mxt", [128, 8, 128], F32))
    vg = es.enter_context(nc.sbuf_tensor("vg", [128, 1024], F32)); vnb = es.enter_context(nc.sbuf_tensor("vnb", [128, 2, 1024], BF16))
    tmpu = es.enter_context(nc.sbuf_tensor("tmpu", [128, 2, TB], BF16))
    zbt = es.enter_context(nc.sbuf_tensor("zbt", [128, 2, TB], BF16))
    st2 = es.enter_context(nc.sbuf_tensor("st2", [128, 2, 6], F32)); mv2 = es.enter_context(nc.sbuf_tensor("mv2", [128, 2], F32))
    rs2 = es.enter_context(nc.sbuf_tensor("rs2", [128, 1], F32))
    pV = es.enter_context(nc.psum_tensor("pV", [128, 2, 2, 512], F32))
    pS = es.enter_context(nc.psum_tensor("pS", [128, 2, 4, 128], F32))
    pA = es.enter_context(nc.psum_tensor("pA", [128, 2, 512], F32))
    P.dma("sp", grow[:], bcast_rows(sgu_g, 1024), writes=["grow"]); P.dma("sp", brow[:], bcast_rows(sgu_b, 1024), writes=["brow"])
    P.dma("sp", bsrow[:].rearrange("p h q -> p (h q)"), bcast_rows(b_sp, 1024), writes=["bsrow"])
    Wsl = mxt[:].rearrange("p h q -> p (h q)").bitcast(BF16)[:, 0:1024].rearrange("p (h q) -> p h q", h=8)
    P.dma("pool", Wsl, w_sp.rearrange("h p q -> p h q"), writes=["mxt"])
    for h in range(8):
        P.op("pe", lambda e: e.matmul(pS[:, h // 4, h % 4, :], lhsT=Wsl[:, h, :], rhs=idb[:], start=True, stop=True), reads=["mxt", "idb"], writes=["pS%d" % (h // 4)])
    for hh in range(2):
        P.op("act", lambda e: e.activation(WsT[:, hh * 4:hh * 4 + 4, :], pS[:, hh], AF.Copy), reads=["pS%d" % hh], writes=["WsT"])
    for i8 in range(8):
        h_ = wl.dma(w_in[8 + i8], "p (k c) -> p k c", c=128)
        wl.cast("act" if i8 % 2 == 0 else "dve", h_, Wv[:, :, i8 * 128:(i8 + 1) * 128], ["Wv"])
    def p4a_proj(c_):
        vb = c_ % 2
        for half in range(2):
            for kt in range(16):
                P.op("pe", lambda e: e.matmul(pV[:, vb, half, :], lhsT=xmT[:, kt, c_ * 128:(c_ + 1) * 128], rhs=Wv[:, kt, half * 512:(half + 1) * 512],
                                              start=(kt == 0), stop=(kt == 15)), reads=["xmT", "Wv"], writes=["pV%d" % vb])
        P.op("act", lambda e: e.activation(vg[:].rearrange("p (h n) -> p h n", h=2), pV[:, vb], AF.Gelu), reads=["pV%d" % vb], writes=["vg"])
        for q in range(2):
            P.op("dve", lambda e: e.bn_stats(st2[:, q, :], vg[:, q * 512:(q + 1) * 512]), reads=["vg"], writes=["st2"])
        P.op("dve", lambda e: e.bn_aggr(mv2[:], st2[:]), reads=["st2"], writes=["mv2"])
        P.op("dve", lambda e: e.tensor_scalar(rs2[:], mv2[:, 1:2], LN_EPS, None, ALU.add), reads=["mv2"], writes=["rs2"])
        P.op("act", lambda e: e.activation(rs2[:], rs2[:], AF.Sqrt), reads=["rs2"], writes=["rs2"])
        P.op("dve", lambda e: e.reciprocal(rs2[:], rs2[:]), reads=["rs2"], writes=["rs2"])
        P.op("dve", lambda e: e.tensor_scalar(vg[:], vg[:], mv2[:, 0:1], rs2[:, 0:1], ALU.subtract, ALU.mult), reads=["vg", "mv2", "rs2"], writes=["vg"])
        P.op("dve", lambda e: e.tensor_tensor(vg[:], vg[:], grow[:], ALU.mult), reads=["vg", "grow"], writes=["vg"])
        P.op("dve", lambda e: e.tensor_tensor(vnb[:, vb, :], vg[:], brow[:], ALU.add), reads=["vg", "brow"], writes=["vnb%d" % vb])

    def p4a_spatial(c_):
        for h in range(8):
            hs = "pS%d" % (h // 4)
            o = pS[:, h // 4, h % 4, :]
            P.op("pe", lambda e: e.matmul(o, lhsT=vnb[:, c_ % 2, h * 128:(h + 1) * 128], rhs=WsT[:, h, :], start=True, stop=True), reads=["vnb%d" % (c_ % 2), "WsT"], writes=[hs])
        P.op("act", lambda e: e.activation(mxt[:, 0:4, :], pS[:, 0], AF.Copy), reads=["pS0"], writes=["mxt"])
        P.op("act", lambda e: e.activation(mxt[:, 4:8, :], pS[:, 1], AF.Copy), reads=["pS1"], writes=["mxt"])
        P.op("dve", lambda e: e.tensor_tensor(MXF[:, :, c_ * 128:(c_ + 1) * 128], mxt[:], bsrow[:], ALU.add), reads=["mxt", "bsrow"], writes=["MXF"])

    for c_ in range(NTT + 1):
        if c_ < NTT:
            p4a_proj(c_)
        if c_ >= 1:
            p4a_spatial(c_ - 1)

    blocks = [(grp, ct) for grp in range(3) for ct in range(8)]
    grp_info = ((0, AF.Gelu), (16, AF.Silu), (32, AF.Silu))
    hnd = {}
    ak = 0
    for k in range(-2, len(blocks)):
        if 0 <= k + 2 < len(blocks):
            g2, c2 = blocks[k + 2]
            hnd[k + 2] = wl.dma(w_in[grp_info[g2][0] + c2], "p (k c) -> p k c", c=128)
        if 0 <= k + 1 < len(blocks):
            wl.cast("act", hnd[k + 1], Wu[:, (k + 1) % 3], ["Wu%d" % ((k + 1) % 3)])
        if k < 0:
            continue
        grp, ct = blocks[k]
        func = grp_info[grp][1]
        wb = k % 3
        zb_ = ct % 2
        for tb in range(NTB):
            slot = ak % 2; ak += 1
            ts = slice(tb * TB, (tb + 1) * TB)
            for kt in range(16):
                P.op("pe", lambda e: e.matmul(pA[:, slot, 0:TB], lhsT=Wu[:, wb, kt, :], rhs=xmT[:, kt, ts], start=(kt == 0), stop=(kt == 15)),
                     reads=["Wu%d" % wb, "xmT"], writes=["pA%d" % slot])
            if grp < 2:
                P.op("act", lambda e: e.activation(tmpu[:, slot, :], pA[:, slot, 0:TB], func), reads=["pA%d" % slot], writes=["tmpu%d" % slot])
                P.op("dve", lambda e: e.tensor_tensor(MXF[:, ct, ts], MXF[:, ct, ts], tmpu[:, slot, :], ALU.mult), reads=["MXF", "tmpu%d" % slot], writes=["MXF"])
            else:
                P.op("act", lambda e: e.activation(zbt[:, slot, :], pA[:, slot, 0:TB], func), reads=["pA%d" % slot], writes=["zbt%d" % slot])
                P.fence("act", ["zbt%d" % slot])
                P.dma("pool", ZB_d[ct][:, ts], zbt[:, slot, :], reads=["zbt%d" % slot], writes=["ZB_d"])
        if grp == 1 and ct == 7:
            P.fence("dve", ["MXF"])
            P.dma("pool", YA_d.rearrange("c p t -> p c t"), MXF, reads=["MXF"], writes=["YA_d"])
            if debug:
                P.dma("sp", dbg["YA"].rearrange("c p t -> p c t"), MXF, reads=["MXF"])
    Uv = BIG[0:NSB, 0:NBLK * 8192].rearrange("n (b g s) -> n b g s", b=NBLK, g=64)
    for i8 in range(8):
        h_ = wl.dma(w_in[24 + i8], "p (k c) -> p k c", c=128)
        wl.cast("act" if i8 % 2 == 0 else "dve", h_, Wv[:, :, i8 * 128:(i8 + 1) * 128], ["Wv"])
    vk = 0
    for (srcT, nrows, nblk, dstv, sname, dname) in ((xmT, NSB, NBLK, None, "xmT", "Uv"), (xcT, NSC, 1, None, "xcT", "Ucs")):
        for blk in range(nblk):
            for j in range(8):
                vb = vk % 2; vk += 1
                t0 = blk * nrows * 8 + j
                for half in range(2):
                    for kt in range(16):
                        P.op("pe", lambda e: e.matmul(pV[0:nrows, vb, half, :], lhsT=srcT[:, kt, t0:t0 + (nrows - 1) * 8 + 1:8], rhs=Wv[:, kt, half * 512:(half + 1) * 512],
                                                      start=(kt == 0), stop=(kt == 15)), reads=[sname, "Wv"], writes=["pV%d" % vb])
                if dname == "Uv":
                    outv = Uv[:, blk, :, j * 16:(j + 1) * 16]
                else:
                    outv = Ucs[:, :, j * 16:(j + 1) * 16]
                inv = pV[0:nrows, vb].rearrange("n h (g c) -> n (h g) c", c=16)
                if vk % 2 == 0:
                    P.op("act", lambda e: e.activation(outv, inv, AF.Copy), reads=["pV%d" % vb], writes=[dname, "MXF"] if dname == "Uv" else [dname, "stg4_0", "stg4_1", "stg4_2"])
                else:
                    P.op("dve", lambda e: e.tensor_copy(outv, inv), reads=["pV%d" % vb], writes=[dname, "MXF"] if dname == "Uv" else [dname, "stg4_0", "stg4_1", "stg4_2"])
    P.fence("act", ["Uv", "Ucs"]); P.fence("dve", ["Uv", "Ucs"])
    P.dma("sp", dr["U_d"], Uv, reads=["Uv"], writes=["U_d"])
    P.dma("sp", dr["Uc_d"], Ucs, reads=["Ucs"], writes=["Uc_d"])
    if debug:
        P.dma("sp", dbg["U"], Uv, reads=["Uv"])
    P.barrier()
    es.close()
    esA.close()

    if stop_after == 'p4':
        P.barrier()
        keep.close(); P.close()
        return nc
    s5.phase_z()
    if stop_after == 'p5':
        P.barrier()
        keep.close(); P.close()
        return nc
    s5.phase_scan(consts["flags"], pre_emit=s5.setup_b)
    if stop_after == 'p6':
        P.barrier()
        keep.close(); P.close()
        return nc
    s5.phase_read()
    if stop_after == 'p7':
        P.barrier()
        keep.close(); P.close()
        return nc
    if debug:
        P.dma("sp", dbg["YG"], dr["YG_d"], reads=["YG_d"])

    esB = ExitStack()
    YB = esB.enter_context(nc.sbuf_tensor("YB", [128, 8, NT], BF16))
    Wo = esB.enter_context(nc.sbuf_tensor("Wo", [128, 16, D], BF16))
    stg8 = esB.enter_context(nc.sbuf_tensor("stg8", [128, 3, 2048], F32))
    wl8 = WL(P, stg8, "stg8_")
    wo = w_out.rearrange("(ct p) n -> p ct n", p=128)
    wgl = w_glu.rearrange("(ct p) n -> p ct n", p=128)
    es = ExitStack()
    ygT = es.enter_context(nc.sbuf_tensor("ygT", [128, 8, NT], BF16))
    YGt = es.enter_context(nc.sbuf_tensor("YGt", [128, 2, 1024], BF16))
    Wg = es.enter_context(nc.sbuf_tensor("Wg", [128, 8, 1024], BF16))
    sgt = es.enter_context(nc.sbuf_tensor("sgt", [128, 2, TB], BF16))
    zb2 = es.enter_context(nc.sbuf_tensor("zb2", [128, 2, NT], BF16))
    pX = es.enter_context(nc.psum_tensor("pX8", [128, 2, 4, 128], F32))
    pA = es.enter_context(nc.psum_tensor("pA8", [128, 2, 512], F32))
    for ct in range(8):
        h_ = wl8.dma(wgl[:, ct, :])
        wl8.cast("act" if ct % 2 == 0 else "dve", h_, Wg[:, ct, :], ["Wg"])
    wo_next = [0]

    def load_wo(n):
        for _ in range(n):
            if wo_next[0] < 16:
                c_ = wo_next[0]; wo_next[0] += 1
                h_ = wl8.dma(wo[:, c_, :])
                wl8.cast("dve", h_, Wo[:, c_, :], ["Wo"])

    for t in range(NTT):
        b = t % 2
        P.dma("sp", YGt[:, b], dr["YG_d"][t * 128:(t + 1) * 128], reads=["YG_d"], writes=["YGt%d" % b])
        load_wo(1)
        for hh in range(2):
            for u in range(4):
                ct = hh * 4 + u
                P.op("pe", lambda e: e.matmul(pX[:, hh, u, :], lhsT=YGt[:, b, ct * 128:(ct + 1) * 128], rhs=idb[:], start=True, stop=True),
                     reads=["YGt%d" % b, "idb"], writes=["pX%d" % hh])
            if hh == 0:
                P.op("act", lambda e: e.activation(ygT[:, 0:4, t * 128:(t + 1) * 128], pX[:, 0], AF.Copy), reads=["pX0"], writes=["ygT"])
            else:
                P.op("dve", lambda e: e.tensor_copy(ygT[:, 4:8, t * 128:(t + 1) * 128], pX[:, 1]), reads=["pX1"], writes=["ygT"])
    ak = 0
    for co in range(8):
        zb_ = co % 2
        P.dma("sp", zb2[:, zb_], ZB_d[co], reads=["ZB_d"], writes=["zb2%d" % zb_])
        for tb in range(NTB):
            slot = ak % 2; ak += 1
            ts = slice(tb * TB, (tb + 1) * TB)
            for ct in range(8):
                P.op("pe", lambda e: e.matmul(pA[:, slot, 0:TB], lhsT=Wg[:, ct, co * 128:(co + 1) * 128], rhs=ygT[:, ct, ts], start=(ct == 0), stop=(ct == 7)),
                     reads=["Wg", "ygT"], writes=["pA%d" % slot])
            P.op("act", lambda e: e.activation(sgt[:, slot, :], pA[:, slot, 0:TB], AF.Sigmoid, bias=bglu[:, co:co + 1]), reads=["pA%d" % slot, "bglu"], writes=["sgt%d" % slot])
            P.op("dve", lambda e: e.tensor_tensor(sgt[:, slot, :], sgt[:, slot, :], ygT[:, co, ts], ALU.mult), reads=["sgt%d" % slot, "ygT"], writes=["sgt%d" % slot])
            P.op("dve", lambda e: e.tensor_tensor(YB[:, co, ts], sgt[:, slot, :], zb2[:, zb_, ts], ALU.mult), reads=["sgt%d" % slot, "zb2%d" % zb_], writes=["YB"])
    load_wo(16)
    if debug:
        P.fence("dve", ["YB"])
        P.dma("sp", dbg["YB"], YB[:], reads=["YB"])
    P.barrier()
    es.close()
    if stop_after == 'p8':
        P.barrier()
        esB.close()
        keep.close(); P.close()
        return nc

    es = ExitStack()
    YA = es.enter_context(nc.sbuf_tensor("YA", [128, 8, NT], BF16))
    P.dma("sp", YA[:], YA_d.rearrange("c p t -> p c t"), reads=["YA_d"], writes=["YA"])
    Grow = es.enter_context(nc.sbuf_tensor("Grow", [128, D], F32))
    lgr = es.enter_context(nc.sbuf_tensor("lgr", [128, D], F32)); lbr = es.enter_context(nc.sbuf_tensor("lbr", [128, D], F32))
    xt = es.enter_context(nc.sbuf_tensor("xt9", [128, 2, D], F32)); ot = stg8
    st6 = es.enter_context(nc.sbuf_tensor("st69", [128, 4, 6], F32)); mv = es.enter_context(nc.sbuf_tensor("mv9", [128, 2], F32))
    rstd = es.enter_context(nc.sbuf_tensor("rstd9", [128, 1], F32))
    pO = es.enter_context(nc.psum_tensor("pO", [128, 2, 4, 512], F32))
    P.dma("sp", Grow[:], bcast_rows(gsc, D), reads=["gsc"], writes=["Grow"])
    P.dma("sp", lgr[:], bcast_rows(ln_g, D), writes=["lgr"]); P.dma("sp", lbr[:], bcast_rows(ln_b, D), writes=["lbr"])
    outs = []
    for t in range(NTT):
        b = t % 2
        tsl = slice(t * 128, (t + 1) * 128)
        P.dma("sp", xt[:, b], x[tsl], writes=["xt%d" % b])
        for db in range(4):
            for ct in range(16):
                src = YA if ct < 8 else YB
                P.op("pe", lambda e: e.matmul(pO[:, b, db, :], lhsT=src[:, ct % 8, tsl], rhs=Wo[:, ct, db * 512:(db + 1) * 512], start=(ct == 0), stop=(ct == 15)),
                     reads=["YA", "YB", "Wo"], writes=["pO%d" % b])
        o = ot[:, b]
        P.op("act", lambda e: e.activation(o.rearrange("p (a n) -> p a n", a=4), pO[:, b], AF.Copy), reads=["pO%d" % b], writes=["ot%d" % b])
        P.op("dve", lambda e: e.tensor_tensor(o, o, Grow[:], ALU.mult), reads=["ot%d" % b, "Grow"], writes=["ot%d" % b])
        P.op("dve", lambda e: e.scalar_tensor_tensor(o, xt[:, b], ALPHA, o, ALU.mult, ALU.add), reads=["ot%d" % b, "xt%d" % b], writes=["ot%d" % b])
        for q in range(4):
            P.op("dve", lambda e: e.bn_stats(st6[:, q, :], o[:, q * 512:(q + 1) * 512]), reads=["ot%d" % b], writes=["st6"])
        P.op("dve", lambda e: e.bn_aggr(mv[:], st6[:]), reads=["st6"], writes=["mv"])
        P.op("dve", lambda e: e.tensor_scalar(rstd[:], mv[:, 1:2], LN_EPS, None, ALU.add), reads=["mv"], writes=["rstd"])
        P.op("act", lambda e: e.activation(rstd[:], rstd[:], AF.Sqrt), reads=["rstd"], writes=["rstd"])
        P.op("dve", lambda e: e.reciprocal(rstd[:], rstd[:]), reads=["rstd"], writes=["rstd"])
        P.op("dve", lambda e: e.tensor_scalar(o, o, mv[:, 0:1], rstd[:, 0:1], ALU.subtract, ALU.mult), reads=["ot%d" % b, "mv", "rstd"], writes=["ot%d" % b])
        P.op("pool", lambda e: e.tensor_tensor(o, o, lgr[:], ALU.mult), reads=["ot%d" % b, "lgr"], writes=["ot%d" % b])
        P.op("pool", lambda e: e.tensor_tensor(o, o, lbr[:], ALU.add), reads=["ot%d" % b, "lbr"], writes=["ot%d" % b])
        P.fence("pool", ["ot%d" % b])
        outs.append(P.dma("pool", y[tsl], o, reads=["ot%d" % b], writes=["y"]))
    P.finish("sp", outs)
    P.barrier()
    es.close()
    esB.close()
    keep.close()
    P.close()
    return nc


_NC_CACHE = {}


def make_in_maps(inputs, NT):
    import ml_dtypes
    f = lambda k: np.ascontiguousarray(np.asarray(inputs[k])[0], dtype=np.float32)
    x = np.asarray(inputs["x"], dtype=np.float32); c = np.asarray(inputs["c"], dtype=np.float32)
    ctx = np.asarray(inputs["ctx"], dtype=np.float32); cc = np.asarray(inputs["c_ctx"], dtype=np.float32)
    B, L, _ = x.shape
    cpb = L // NT
    assert B * cpb == 8 and cpb == 4
    ML = np.ascontiguousarray(np.kron(np.tril(np.ones((8, 8))), np.ones((16, 16))).astype(np.float32).T)
    MU = np.kron(np.tril(np.ones((8, 8))), np.ones((16, 16))).astype(np.float32)
    common = {
        "w_in": np.ascontiguousarray(f("w_in").reshape(16, 128, 40, 128).transpose(2, 1, 0, 3)),
        "sgu_g": f("sgu_ln_g").reshape(1, 1024), "sgu_b": f("sgu_ln_b").reshape(1, 1024),
        "w_sp": f("w_spatial"), "b_sp": f("b_spatial").reshape(1, 1024),
        "w_glu": f("w_glu"), "b_gluT": np.ascontiguousarray(f("b_glu").reshape(8, 128).T), "w_out": f("w_out"),
        "ln_g": f("ln_g").reshape(1, D), "ln_b": f("ln_b").reshape(1, D),
        "lam_re": f("s5_lam_re").reshape(128, 64), "lam_im": f("s5_lam_im").reshape(128, 64), "log_step": f("s5_log_step").reshape(128, 1),
        "b_re": f("s5_b_re").reshape(128, 64, 16), "b_im": f("s5_b_im").reshape(128, 64, 16),
        "c_re": f("s5_c_re").reshape(128, 16, 64), "c_im": f("s5_c_im").reshape(128, 16, 64),
        "c_idb": np.eye(128).astype(ml_dtypes.bfloat16), "c_Jb": np.ascontiguousarray(np.eye(128)[::-1]).astype(ml_dtypes.bfloat16),
        "c_idf": np.eye(128, dtype=np.float32), "c_ML": ML, "c_MU": MU,
        "c_dcol": np.ascontiguousarray(np.tile(f("s5_d").reshape(64, 16).T, (8, 1))),
    }
    ims = []
    for core in range(8):
        b, k = core // cpb, core % cpb
        fl = np.zeros((128, 3), np.float32)
        for jj in range(3):
            fl[0:64, jj] = 1.0 if jj < k else 0.0
            fl[64:128, jj] = 1.0 if (3 - jj) > k else 0.0
        cT = np.stack([c[b].reshape(16, 128).T, cc.reshape(16, 128).T], axis=-1).astype(np.float32)
        im = dict(common)
        wad = f("w_ada"); bad = f("b_ada")
        im["w_ada"] = np.ascontiguousarray(wad[:, k * 1536:(k + 1) * 1536])
        im["b_adaT"] = np.ascontiguousarray(bad[k * 1536:(k + 1) * 1536].reshape(12, 128).T)
        im.update({"x": np.ascontiguousarray(x[b, k * NT:(k + 1) * NT]), "ctx": np.ascontiguousarray(ctx[b]), "cT": np.ascontiguousarray(cT), "c_flags": fl})
        ims.append(im)
    return ims


def kernel(**inputs):
    x = np.asarray(inputs["x"])
    B, L, _ = x.shape
    NT = B * L // 8
    if NT not in _NC_CACHE:
        _NC_CACHE[NT] = build(NT, np.asarray(inputs["ctx"]).shape[1])
    nc = _NC_CACHE[NT]
    ims = make_in_maps(inputs, NT)
    res = run_bass_kernel_spmd(nc, ims, core_ids=list(range(8)))
    out = np.empty((B, L, D), np.float32)
    cpb = L // NT
    for core in range(8):
        b, k = core // cpb, core % cpb
        out[b, k * NT:(k + 1) * NT] = np.asarray(res.results[core]["y"])
    return out
```

```python
import time, sys, math
from contextlib import ExitStack
import numpy as np
import concourse.bass as bass
import concourse.mybir as mybir
from concourse.bass_utils import run_bass_kernel_spmd

F32 = mybir.dt.float32
BF16 = mybir.dt.bfloat16
ALU = mybir.AluOpType
AF = mybir.ActivationFunctionType
PI = math.pi


class Prog:
    def __init__(self, nc, n_dma_sems=10):
        self.nc = nc
        self.engs = {"pe": nc.tensor, "act": nc.scalar, "dve": nc.vector, "pool": nc.gpsimd, "sp": nc.sync}
        self.sem = {}
        self.cnt = {k: 0 for k in self.engs}
        self.seen = {k: {} for k in self.engs}
        self.es = ExitStack()
        for k in self.engs:
            self.sem[k] = self.es.enter_context(nc.semaphore("s_" + k))
        self.dsem = [self.es.enter_context(nc.semaphore("d_%d" % i)) for i in range(n_dma_sems)]
        self.dcnt = [0] * n_dma_sems
        self.ccsem = self.es.enter_context(nc.semaphore("cc_sem"))
        self.cccnt = 0
        self._fz = self.es.enter_context(nc.sbuf_tensor("fence_z", [128, 8], F32))
        self.dnext = 0
        self.lastw = {}
        self.readers = {}

    def _wait(self, eng, tok):
        if tok is None:
            return
        kind, key, val = tok
        if kind == "e" and key == "pe" and eng == "pe":
            return
        if kind == "e" and key == eng and getattr(self, "_nosame", False):
            return
        seen = self.seen[eng]
        if seen.get((kind, key), 0) >= val:
            return
        seen[(kind, key)] = val
        s = self.sem[key] if kind == "e" else (self.ccsem if kind == "c" else self.dsem[key])
        self.engs[eng].wait_ge(s, val)

    def _deps(self, eng, reads, writes):
        for r in reads:
            self._wait(eng, self.lastw.get(r))
        for w in writes:
            self._wait(eng, self.lastw.get(w))
            for t in self.readers.get(w, []):
                self._wait(eng, t)

    def _commit(self, tok, reads, writes):
        for r in reads:
            self.readers.setdefault(r, []).append(tok)
        for w in writes:
            self.lastw[w] = tok
            self.readers[w] = []

    def _pe_mode_guard(self, lhsT, kind):
        def r(n):
            return 32 if n <= 32 else (64 if n <= 64 else 128)
        m = 1
        for dmn in lhsT.shape[1:]:
            m *= dmn
        mode = (r(lhsT.shape[0]), r(m), str(lhsT.dtype), kind)
        if getattr(self, "_pe_mode", None) not in (None, mode) and self.cnt["pe"] > 0:
            self.engs["pe"].wait_ge(self.sem["pe"], self.cnt["pe"])
        self._pe_mode = mode

    def op(self, eng, fn, reads=(), writes=(), nosame=False):
        self._nosame = nosame
        self._deps(eng, reads, writes)
        self._nosame = False
        if eng == "pe":
            prog = self

            class _PE:
                def matmul(self_, out, lhsT, rhs, **kw):
                    prog._pe_mode_guard(lhsT, "mm")
                    return prog.engs["pe"].matmul(out, lhsT=lhsT, rhs=rhs, **kw)

                def transpose(self_, out, in_, ident):
                    prog._pe_mode_guard(in_, "tr")
                    return prog.engs["pe"].transpose(out, in_, ident)

            ins = fn(_PE())
            self.cnt[eng] += 1
            ins.then_inc(self.sem[eng], 1)
            tok = ("e", eng, self.cnt[eng])
            self._commit(tok, reads, writes)
            return tok
        ins = fn(self.engs[eng])
        self.cnt[eng] += 1
        ins.then_inc(self.sem[eng], 1)
        tok = ("e", eng, self.cnt[eng])
        self._commit(tok, reads, writes)
        return tok

    def dma(self, eng, out, in_, reads=(), writes=()):
        k = self.dnext
        self.dnext = (self.dnext + 1) % len(self.dsem)
        if self.dcnt[k] > 0:
            self._wait(eng, ("d", k, self.dcnt[k]))
        if eng == "pool" and len(getattr(self, "pool_dmas", [])) >= 2:
            self._wait(eng, self.pool_dmas[-2])
        self._deps(eng, reads, writes)
        ins = self.engs[eng].dma_start(out=out, in_=in_)
        self.dcnt[k] += 16
        ins.then_inc(self.dsem[k], 16)
        tok = ("d", k, self.dcnt[k])
        if eng == "pool":
            if not hasattr(self, "pool_dmas"):
                self.pool_dmas = []
            self.pool_dmas.append(tok)
        self._commit(tok, reads, writes)
        return tok

    def collective(self, kind, groups, src, dst, reads, writes, after):
        self._wait("pool", after)
        for tk in getattr(self, "pool_dmas", [])[-2:]:
            self._wait("pool", tk)
        self._deps("pool", reads, writes)
        ins = self.nc.gpsimd.collective_compute(kind, ALU.bypass, replica_groups=groups, ins=[src], outs=[dst])
        self.cccnt += 1
        ins.then_inc(self.ccsem, 1)
        tok = ("c", 0, self.cccnt)
        self._commit(tok, reads, writes)
        return tok

    def fence(self, eng, res):
        if eng == "act":
            self.op(eng, lambda e: e.activation(self._fz[:, 0:1], self._fz[:, 1:2], AF.Copy), reads=list(res), writes=list(res) + ["fence_z"])
        else:
            self.op(eng, lambda e: e.memset(self._fz[:, 0:1], 0.0), reads=list(res), writes=list(res) + ["fence_z"])

    def barrier(self):
        toks = [("e", k, self.cnt[k]) for k in self.engs if self.cnt[k] > 0]
        toks += [("d", i, c) for i, c in enumerate(self.dcnt) if c > 0]
        if self.cccnt > 0:
            toks.append(("c", 0, self.cccnt))
        for e in self.engs:
            for t in toks:
                self._wait(e, t)

    def finish(self, eng, toks):
        for t in toks:
            self._wait(eng, t)

    def close(self):
        self.es.close()


def cmul(P, eng, o_re, o_im, a_re, a_im, b_re, b_im, t0, t1, rd, wr, neg_im=False):
    T = ["cm_t0", "cm_t1"]
    P.op(eng, lambda e: e.tensor_tensor(t0, a_re, b_re, ALU.mult), reads=rd, writes=[T[0]])
    P.op(eng, lambda e: e.tensor_tensor(t1, a_im, b_im, ALU.mult), reads=rd, writes=[T[1]])
    P.op(eng, lambda e: e.tensor_tensor(o_re, t0, t1, ALU.subtract), reads=T, writes=wr)
    P.op(eng, lambda e: e.tensor_tensor(t0, a_re, b_im, ALU.mult), reads=rd + wr, writes=[T[0]])
    P.op(eng, lambda e: e.tensor_tensor(t1, a_im, b_re, ALU.mult), reads=rd + wr, writes=[T[1]])
    if neg_im:
        P.op(eng, lambda e: e.scalar_tensor_tensor(o_im, t0, -1.0, t1, ALU.mult, ALU.subtract), reads=T, writes=wr)
    else:
        P.op(eng, lambda e: e.tensor_tensor(o_im, t0, t1, ALU.add), reads=T, writes=wr)


class S5:
    def __init__(self, nc, P, NS, NSC, dr, consts):
        self.nc, self.P, self.NS, self.NSC, self.d = nc, P, NS, NSC, dr
        self.NSB = min(128, NS)
        self.NBLK = NS // self.NSB
        self.c = consts

    def setup(self, es_keep):
        nc, P, d = self.nc, self.P, self.d
        es = ExitStack()

        def sb(name, shape, dt, keep=False):
            return (es_keep if keep else es).enter_context(nc.sbuf_tensor(name, shape, dt))

        lr = sb("lr", [128, 64], F32); li = sb("li", [128, 64], F32); ls = sb("ls", [128, 1], F32)
        br = sb("br", [128, 64, 16], F32); bi = sb("bi", [128, 64, 16], F32)
        cr = sb("cr", [128, 16, 64], F32); ci = sb("ci", [128, 16, 64], F32)
        P.dma("sp", lr[:], d["lam_re"], writes=["lr"])
        P.dma("sp", li[:], d["lam_im"], writes=["li"])
        P.dma("sp", ls[:], d["log_step"], writes=["ls"])
        P.dma("sp", br[:], d["b_re"], writes=["br"])
        P.dma("sp", bi[:], d["b_im"], writes=["bi"])
        P.dma("sp", cr[:], d["c_re"], writes=["cr"])
        P.dma("sp", ci[:], d["c_im"], writes=["ci"])
        step = sb("step", [128, 1], F32)
        drt = sb("drt", [128, 64], F32); dit = sb("dit", [128, 64], F32)
        mag = sb("mag", [128, 64], F32); sn = sb("sn", [128, 64], F32); cs = sb("cs", [128, 64], F32)
        tA = sb("tA", [128, 64], F32); tB = sb("tB", [128, 64], F32)
        PW = sb("PW", [128, 16, 2, 64], F32)
        P.op("act", lambda e: e.activation(step[:], ls[:], AF.Exp), reads=["ls"], writes=["step"])
        P.op("dve", lambda e: e.tensor_scalar(drt[:], lr[:], step[:, 0:1], None, ALU.mult), reads=["lr", "step"], writes=["drt"])
        P.op("dve", lambda e: e.tensor_scalar(dit[:], li[:], step[:, 0:1], None, ALU.mult), reads=["li", "step"], writes=["dit"])
        P.op("act", lambda e: e.activation(mag[:], drt[:], AF.Exp), reads=["drt"], writes=["mag"])
        kk = sb("kk", [128, 64], F32)
        for (dst, dn, off) in ((tA, "cm_t0", 0.0), (tB, "cm_t1", 0.5 * PI)):
            P.op("dve", lambda e: e.tensor_scalar(dst[:], dit[:], off, None, ALU.add), reads=["dit"], writes=[dn])
            P.op("dve", lambda e: e.tensor_scalar(kk[:], dst[:], PI, None, ALU.is_ge), reads=[dn], writes=["kk"])
            for m in (3, 5, 7):
                P.op("dve", lambda e: e.scalar_tensor_tensor(kk[:], dst[:], m * PI, kk[:], ALU.is_ge, ALU.add), reads=[dn, "kk"], writes=["kk"])
            P.op("dve", lambda e: e.scalar_tensor_tensor(dst[:], kk[:], -2 * PI, dst[:], ALU.mult, ALU.add), reads=[dn, "kk"], writes=[dn])
        P.op("act", lambda e: e.activation(sn[:], tA[:], AF.Sin), reads=["cm_t0"], writes=["sn"])
        P.op("act", lambda e: e.activation(cs[:], tB[:], AF.Sin), reads=["cm_t1"], writes=["cs"])
        P.op("dve", lambda e: e.tensor_tensor(PW[:, 8, 0, :], cs[:], mag[:], ALU.mult), reads=["cs", "mag"], writes=["PW8"])
        P.op("dve", lambda e: e.tensor_tensor(PW[:, 8, 1, :], sn[:], mag[:], ALU.mult), reads=["sn", "mag"], writes=["PW8"])
        P.op("dve", lambda e: e.memset(PW[:, 7, 0, :], 1.0), writes=["PW7"])
        P.op("dve", lambda e: e.memset(PW[:, 7, 1, :], 0.0), writes=["PW7"])
        for k in range(2, 9):
            cmul(P, "dve", PW[:, 7 + k, 0, :], PW[:, 7 + k, 1, :], PW[:, 6 + k, 0, :], PW[:, 6 + k, 1, :],
                 PW[:, 8, 0, :], PW[:, 8, 1, :], tA[:], tB[:], ["PW%d" % (6 + k), "PW8"], ["PW%d" % (7 + k)])
        den = sb("den", [128, 64], F32)
        P.op("dve", lambda e: e.tensor_tensor(tA[:], PW[:, 8, 0, :], PW[:, 8, 0, :], ALU.mult), reads=["PW8", "cm_t0", "cm_t1"], writes=["cm_t0"])
        P.op("dve", lambda e: e.tensor_tensor(tB[:], PW[:, 8, 1, :], PW[:, 8, 1, :], ALU.mult), reads=["PW8", "cm_t0", "cm_t1"], writes=["cm_t1"])
        P.op("dve", lambda e: e.tensor_tensor(den[:], tA[:], tB[:], ALU.add), reads=["cm_t0", "cm_t1"], writes=["den"])
        P.op("dve", lambda e: e.reciprocal(den[:], den[:]), reads=["den"], writes=["den"])
        P.op("dve", lambda e: e.tensor_tensor(PW[:, 6, 0, :], PW[:, 8, 0, :], den[:], ALU.mult), reads=["PW8", "den"], writes=["PW6"])
        P.op("dve", lambda e: e.scalar_tensor_tensor(PW[:, 6, 1, :], PW[:, 8, 1, :], -1.0, den[:], ALU.mult, ALU.mult), reads=["PW8", "den"], writes=["PW6"])
        for k in range(2, 8):
            cmul(P, "dve", PW[:, 7 - k, 0, :], PW[:, 7 - k, 1, :], PW[:, 8 - k, 0, :], PW[:, 8 - k, 1, :],
                 PW[:, 6, 0, :], PW[:, 6, 1, :], tA[:], tB[:], ["PW%d" % (8 - k), "PW6", "cm_t0", "cm_t1"], ["PW%d" % (7 - k)])
        allpw = ["PW%d" % k for k in range(16)]
        fr = sb("fr", [128, 64], F32); fi = sb("fi", [128, 64], F32); nr = sb("nr", [128, 64], F32)
        P.op("dve", lambda e: e.tensor_scalar(nr[:], PW[:, 8, 0, :], -1.0, None, ALU.add), reads=["PW8"], writes=["nr"])
        P.op("dve", lambda e: e.tensor_tensor(tA[:], lr[:], lr[:], ALU.mult), reads=["lr", "cm_t0", "cm_t1"], writes=["cm_t0"])
        P.op("dve", lambda e: e.tensor_tensor(tB[:], li[:], li[:], ALU.mult), reads=["li", "cm_t0", "cm_t1"], writes=["cm_t1"])
        P.op("dve", lambda e: e.tensor_tensor(den[:], tA[:], tB[:], ALU.add), reads=["cm_t0", "cm_t1"], writes=["den"])
        P.op("dve", lambda e: e.reciprocal(den[:], den[:]), reads=["den"], writes=["den"])
        P.op("dve", lambda e: e.tensor_tensor(tA[:], nr[:], lr[:], ALU.mult), reads=["nr", "lr", "den"], writes=["cm_t0"])
        P.op("dve", lambda e: e.tensor_tensor(tB[:], PW[:, 8, 1, :], li[:], ALU.mult), reads=["PW8", "li", "den"], writes=["cm_t1"])
        P.op("dve", lambda e: e.tensor_tensor(fr[:], tA[:], tB[:], ALU.add), reads=["cm_t0", "cm_t1"], writes=["fr"])
        P.op("dve", lambda e: e.tensor_tensor(fr[:], fr[:], den[:], ALU.mult), reads=["fr", "den"], writes=["fr"])
        P.op("dve", lambda e: e.tensor_tensor(tA[:], PW[:, 8, 1, :], lr[:], ALU.mult), reads=["PW8", "lr", "fr"], writes=["cm_t0"])
        P.op("dve", lambda e: e.tensor_tensor(tB[:], nr[:], li[:], ALU.mult), reads=["nr", "li", "fr"], writes=["cm_t1"])
        P.op("dve", lambda e: e.tensor_tensor(fi[:], tA[:], tB[:], ALU.subtract), reads=["cm_t0", "cm_t1"], writes=["fi"])
        P.op("dve", lambda e: e.tensor_tensor(fi[:], fi[:], den[:], ALU.mult), reads=["fi", "den"], writes=["fi"])
        bb = sb("bb", [128, 2, 64, 16], F32)
        t0 = sb("t0", [128, 4096], F32); t1 = sb("t1", [128, 4096], F32)
        t0b = t0[:, 0:1024].rearrange("q (p c) -> q p c", c=16); t1b = t1[:, 0:1024].rearrange("q (p c) -> q p c", c=16)
        frb = fr[:].unsqueeze(2).to_broadcast([128, 64, 16]); fib = fi[:].unsqueeze(2).to_broadcast([128, 64, 16])
        cmul(P, "dve", bb[:, 0], bb[:, 1], frb, fib, br[:], bi[:], t0b, t1b, ["fr", "fi", "br", "bi", "cm_t0", "cm_t1"], ["bb"])
        PWa = sb("PWa", [128, 8, 2, 64], F32); PWc = sb("PWc", [128, 8, 2, 64], F32); PWg = sb("PWg", [128, 8, 2, 64], F32)
        for x in range(8):
            for (tbl, nm, kf, kb) in ((PWa, "PWa", 7 - x, x), (PWc, "PWc", x + 1, 8 - x), (PWg, "PWg", x - 7, -x)):
                P.op("pool", lambda e: e.tensor_copy(tbl[0:64, x], PW[0:64, kf + 7]), reads=allpw, writes=[nm])
                P.op("pool", lambda e: e.tensor_copy(tbl[64:128, x], PW[64:128, kb + 7]), reads=allpw, writes=[nm])
        fam = sb("fam", [128, 4, 16, 2, 64], F32)
        t0f = t0[:].rearrange("q (x c p) -> q x c p", x=4, c=16); t1f = t1[:].rearrange("q (x c p) -> q x c p", x=4, c=16)

        def bx(ap3):
            return ap3.unsqueeze(2).to_broadcast([128, 4, 16, 64])

        bbr = bb[:, 0].rearrange("q p c -> q c p").unsqueeze(1).to_broadcast([128, 4, 16, 64])
        bbi = bb[:, 1].rearrange("q p c -> q c p").unsqueeze(1).to_broadcast([128, 4, 16, 64])
        crb = cr[:].unsqueeze(1).to_broadcast([128, 4, 16, 64]); cib = ci[:].unsqueeze(1).to_broadcast([128, 4, 16, 64])
        for h in range(2):
            xs = slice(4 * h, 4 * h + 4)
            for f, (tbl, nm, br_, bi_, rdn, neg) in enumerate(((PWa, "PWa", bbr, bbi, ["bb"], False), (PWc, "PWc", crb, cib, ["cr", "ci"], True),
                                                               (PWg, "PWg", crb, cib, ["cr", "ci"], True))):
                cmul(P, "dve", fam[:, :, :, 0, :], fam[:, :, :, 1, :], bx(tbl[:, xs, 0, :]), bx(tbl[:, xs, 1, :]), br_, bi_, t0f, t1f,
                     [nm] + rdn, ["fam"], neg_im=neg)
                P.fence("dve", ["fam"])
                P.dma("sp", d["SC"][f][:, xs], fam[:], reads=["fam"], writes=["SC%d" % f])
        sq = sb("sq", [128, 2, 2, 64], F32)
        P.op("dve", lambda e: e.tensor_copy(sq[:, 0], PW[:, 15]), reads=["PW15"], writes=["sq0"])
        cur = 0
        nsq = int(round(math.log2(self.NS)))
        assert 2 ** nsq == self.NS
        for s in range(nsq):
            cmul(P, "dve", sq[:, 1 - cur, 0, :], sq[:, 1 - cur, 1, :], sq[:, cur, 0, :], sq[:, cur, 1, :], sq[:, cur, 0, :], sq[:, cur, 1, :],
                 tA[:], tB[:], ["sq%d" % cur, "cm_t0", "cm_t1"], ["sq%d" % (1 - cur)])
            cur = 1 - cur
        for (R, I, src, rs) in ((self.AR2, self.AI2, PW[:, 15], "PW15"), (self.ANR, self.ANI, sq[:, cur], "sq%d" % cur)):
            nm = "AC"
            P.op("dve", lambda e: e.tensor_copy(R[:, 0, :], src[:, 0, :]), reads=[rs], writes=[nm])
            P.op("dve", lambda e: e.tensor_copy(R[:, 1, :], src[:, 0, :]), reads=[rs], writes=[nm])
            P.op("dve", lambda e: e.tensor_scalar(I[:, 0, :], src[:, 1, :], -1.0, None, ALU.mult), reads=[rs], writes=[nm])
            P.op("dve", lambda e: e.tensor_copy(I[:, 1, :], src[:, 1, :]), reads=[rs], writes=[nm])
        P.barrier()
        es.close()
    def setup_b(self):
        nc, P, d = self.nc, self.P, self.d
        es2 = ExitStack()
        CT = es2.enter_context(nc.sbuf_tensor("CTb", [128, 128, 128], BF16))
        DT = es2.enter_context(nc.sbuf_tensor("DTb", [128, 64, 128], BF16))
        XC = es2.enter_context(nc.sbuf_tensor("XC", [128, 128, 128], BF16))
        XG = XC
        BT = XC
        GB = es2.enter_context(nc.sbuf_tensor("GB", [128, 128, 128], BF16))
        GT = es2.enter_context(nc.sbuf_tensor("GT", [128, 128, 128], BF16))
        tmpd = es2.enter_context(nc.sbuf_tensor("tmpd", [128, 4, 128], F32))
        tmpe = es2.enter_context(nc.sbuf_tensor("tmpe", [128, 4, 128], F32))
        pT = es2.enter_context(nc.psum_tensor("pT", [128, 4, 4, 128], F32))
        pD = es2.enter_context(nc.psum_tensor("pD", [128, 2, 2, 4, 128], F32))
        idb = self.c["idb"]
        k = 0
        for f, (srcT, sn_, dstT, dn_) in enumerate(((BT, "XC", GB, "GB"), (XC, "XC", CT, "CT"), (XG, "XC", GT, "GT"))):
            src = d["SC"][f].rearrange("q x c s -> (x c) q s")
            for h in range(4):
                P.dma("pool", srcT[:, 32 * h:32 * h + 32, :], src[:, 32 * h:32 * h + 32, :], reads=["SC%d" % f], writes=[sn_ + str(h)])
            for q4 in range(32):
                slot = k % 4
                k += 1
                for u in range(4):
                    q = q4 * 4 + u
                    P.op("pe", lambda e: e.matmul(pT[:, slot, u, :], lhsT=srcT[:, q, :], rhs=idb[:], start=True, stop=True), reads=[sn_ + str(q // 32), "idb"], writes=["pT%d" % slot])
                P.op("act", lambda e: e.activation(dstT[:, q4 * 4:q4 * 4 + 4, :], pT[:, slot], AF.Copy), reads=["pT%d" % slot], writes=[dn_])
        self._b2 = lambda: self._setup_b2(GB, GT, CT, DT, tmpd, tmpe, pD)
        return es2

    def _setup_b2(self, GB, GT, CT, DT, tmpd, tmpe, pD):
        nc, P, d = self.nc, self.P, self.d
        ML, MU, idf, dcol = self.c["ML"], self.c["MU"], self.c["idf"], self.c["dcol"]
        MLb = ML[:].unsqueeze(1).to_broadcast([128, 4, 128]); MUb = MU[:].unsqueeze(1).to_broadcast([128, 4, 128])
        idb4 = idf[:].unsqueeze(1).to_broadcast([128, 4, 128])
        for g4 in range(16):
            s = g4 % 2
            for u in range(4):
                g = g4 * 4 + u
                P.op("pe", lambda e: e.matmul(pD[:, s, 0, u, :], lhsT=GB[:, g, :], rhs=GT[:, g, :], start=True, stop=True), reads=["GB", "GT"], writes=["pDf%d" % s])
            for u in range(4):
                g = g4 * 4 + u
                P.op("pe", lambda e: e.matmul(pD[:, s, 1, u, :], lhsT=GB[:, 64 + g, :], rhs=GT[:, 64 + g, :], start=True, stop=True), reads=["GB", "GT"], writes=["pDb%d" % s])
            P.op("act", lambda e: e.activation(tmpd[:], pD[:, s, 0], AF.Copy), reads=["pDf%d" % s], writes=["tmpd"])
            P.op("act", lambda e: e.activation(tmpe[:], pD[:, s, 1], AF.Copy), reads=["pDb%d" % s], writes=["tmpe"])
            P.op("pool", lambda e: e.tensor_tensor(tmpd[:], tmpd[:], MLb, ALU.mult), reads=["tmpd", "ML"], writes=["tmpd"])
            P.op("pool", lambda e: e.tensor_tensor(tmpe[:], tmpe[:], MUb, ALU.mult), reads=["tmpe", "MU"], writes=["tmpe"])
            P.op("pool", lambda e: e.tensor_tensor(tmpd[:], tmpd[:], tmpe[:], ALU.add), reads=["tmpd", "tmpe"], writes=["tmpd"])
            dcb = dcol[:, g4 * 4:g4 * 4 + 4].unsqueeze(2).to_broadcast([128, 4, 128])
            P.op("pool", lambda e: e.tensor_tensor(tmpe[:], idb4, dcb, ALU.mult), reads=["tmpd", "idf", "dcol"], writes=["tmpe"])
            P.op("pool", lambda e: e.tensor_tensor(DT[:, g4 * 4:g4 * 4 + 4, :], tmpe[:], tmpd[:], ALU.add), reads=["tmpd", "tmpe"], writes=["DT"])
        P.fence("pool", ["DT"]); P.fence("act", ["CT"])
        P.dma("pool", d["CTd"], CT[:], reads=["CT"], writes=["CTd"])
        P.dma("pool", d["DTd"], DT[:], reads=["DT"], writes=["DTd"])

    def scan_steps(self, SG, ZG, n, store):
        P = self.P
        tA, tB = self.sc_tA, self.sc_tB
        for i in range(n):
            a = i if store else i % 2
            b = i + 1 if store else (i + 1) % 2
            S = SG[:, a]
            Ssw = bass.AP(SG[:].tensor, SG[:, a, 1, :].offset, [list(SG[:].ap[0]), [-64, 2], [1, 64]])
            P.op("dve", lambda e: e.tensor_tensor(tA[:], self.AR2[:], S, ALU.mult), reads=["SGs%d" % a, "AC"], writes=["sc_tA"])
            P.op("dve", lambda e: e.tensor_tensor(tB[:], self.AI2[:], Ssw, ALU.mult), reads=["SGs%d" % a, "AC"], writes=["sc_tB"])
            P.op("dve", lambda e: e.tensor_tensor(tA[:], tA[:], tB[:], ALU.add), reads=["sc_tA", "sc_tB"], writes=["sc_tA"])
            P.op("dve", lambda e: e.tensor_tensor(SG[:, b], tA[:], ZG[:, i], ALU.add), reads=["sc_tA", "ZG"], writes=["SGs%d" % b])
        return (n if store else n % 2)


    def phase_z(self):
        nc, P, d, c = self.nc, self.P, self.d, self.c
        NS, NSC, NSB, NBLK = self.NS, self.NSC, self.NSB, self.NBLK
        idb, Jb = c["idb"], c["Jb"]
        es = ExitStack()

        def sb(name, shape, dt):
            return es.enter_context(nc.sbuf_tensor(name, shape, dt))

        U = sb("U5", [NSB, NBLK, 64, 128], BF16); Uc = sb("Uc5", [NSC, 64, 128], BF16)
        BT = sb("BT5", [128, 128, 128], BF16)
        UT = sb("UT", [128, 64, NS], BF16); UTr = sb("UTr", [128, 64, NS], BF16)
        UcT = sb("UcT", [128, 64, NSC], BF16); UcTr = sb("UcTr", [128, 64, NSC], BF16)
        ZR = sb("ZR", [128, 4, 8, 128], F32)
        pU = es.enter_context(nc.psum_tensor("pU", [128, 4, 4, 128], F32))
        P.dma("sp", U[:], d["U_d"], reads=["U_d"], writes=["U"])
        P.dma("sp", Uc[:], d["Uc_d"], reads=["Uc_d"], writes=["Uc"])
        srcb = d["SC"][0].rearrange("q x c s -> (x c) q s")
        stgz = sb("stgz", [128, 2, 16, 128], F32)
        for h in range(8):
            zs = h % 2
            P.dma("sp", stgz[:, zs], srcb[:, 16 * h:16 * h + 16, :], reads=["SC0"], writes=["stgz%d" % zs])
            if h % 2 == 0:
                P.op("act", lambda e: e.activation(BT[:, 16 * h:16 * h + 16, :], stgz[:, zs], AF.Copy), reads=["stgz%d" % zs], writes=["BT"])
            else:
                P.op("dve", lambda e: e.tensor_copy(BT[:, 16 * h:16 * h + 16, :], stgz[:, zs]), reads=["stgz%d" % zs], writes=["BT"])
        kslot = [0]

        def transposes(src_fn, nrows, nblk, dstT, dstTr, sname, dname):
            Isub = idb[0:nrows, 0:nrows]; Jsub = Jb[0:nrows, 128 - nrows:128]
            for blk in range(nblk):
                for g4 in range(16):
                    for (rhs, dst, col0, tag) in ((Isub, dstT, blk * nrows, "n"), (Jsub, dstTr, (nblk - 1 - blk) * nrows, "r")):
                        slot = kslot[0] % 4; kslot[0] += 1
                        for u in range(4):
                            g = g4 * 4 + u
                            P.op("pe", lambda e: e.matmul(pU[:, slot, u, 0:nrows], lhsT=src_fn(blk, g), rhs=rhs, start=True, stop=True),
                                 reads=[sname, "idb", "Jb"], writes=["pU%d" % slot])
                        if kslot[0] % 2 == 0:
                            P.op("act", lambda e: e.activation(dst[:, g4 * 4:g4 * 4 + 4, col0:col0 + nrows], pU[:, slot, :, 0:nrows], AF.Copy),
                                 reads=["pU%d" % slot], writes=[dname + tag + "_%d" % g4])
                        else:
                            P.op("dve", lambda e: e.tensor_copy(dst[:, g4 * 4:g4 * 4 + 4, col0:col0 + nrows], pU[:, slot, :, 0:nrows]),
                                 reads=["pU%d" % slot], writes=[dname + tag + "_%d" % g4])

        transposes(lambda blk, g: U[0:NSB, blk, g, :], NSB, NBLK, UT, UTr, "U", "UT")
        transposes(lambda blk, g: Uc[0:NSC, g, :], NSC, 1, UcT, UcTr, "Uc", "UcT")

        def zrows(T, Tr, nrows, nblk, Zd, tname, zname):
            for dd in range(2):
                src = T if dd == 0 else Tr
                for blk in range(nblk):
                    for g4 in range(16):
                        slot = kslot[0] % 4; kslot[0] += 1
                        for u in range(4):
                            g = g4 * 4 + u
                            P.op("pe", lambda e: e.matmul(pU[0:nrows, slot, u, :], lhsT=src[:, g, blk * nrows:(blk + 1) * nrows],
                                                          rhs=BT[:, dd * 64 + g, :], start=True, stop=True),
                                 reads=[tname + ("n" if dd == 0 else "r") + "_%d" % g4, "BT"], writes=["pU%d" % slot])
                        zb = (g4 // 2) % 4
                        P.op("act", lambda e: e.activation(ZR[0:nrows, zb, (g4 % 2) * 4:(g4 % 2) * 4 + 4, :], pU[0:nrows, slot], AF.Copy),
                             reads=["pU%d" % slot], writes=["ZR%d" % zb])
                        if g4 % 2 == 1:
                            gg = (g4 // 2) * 8
                            P.fence("act", ["ZR%d" % zb])
                            P.dma("sp", Zd[dd, blk * nrows:(blk + 1) * nrows, gg:gg + 8], ZR[0:nrows, zb], reads=["ZR%d" % zb], writes=[zname])

        zrows(UcT, UcTr, NSC, 1, d["Zc"], "UcT", "Zc")
        zrows(UT, UTr, NSB, NBLK, d["Z"], "UT", "Z")
        utn_all = ["UTn_%d" % i_ for i_ in range(16)]
        P.fence("act", utn_all); P.fence("dve", utn_all)
        P.dma("sp", d["UT_d"], UT[:], reads=utn_all, writes=["UT_d"])
        P.barrier()
        es.close()

    def phase_scan(self, flags, pre_emit=None):
        nc, P, d = self.nc, self.P, self.d
        NS, NSC = self.NS, self.NSC
        es = ExitStack()

        def sb(name, shape, dt):
            return es.enter_context(nc.sbuf_tensor(name, shape, dt))

        SGc = sb("SGc", [128, 2, 2, 64], F32); SGp = sb("SGp", [128, 2, 2, 64], F32)
        CH = min(16, NS)
        ZG = sb("ZG", [128, 2, CH, 2, 64], F32)
        SG = sb("SG", [128, 2, CH + 1, 2, 64], F32)
        self.sc_tA = sb("sc_tA", [128, 2, 64], F32); self.sc_tB = sb("sc_tB", [128, 2, 64], F32)
        EG = sb("EG", [128, 4, 128], F32); Es = sb("Es", [128, 2, 64], F32); acc = sb("acc", [128, 2, 64], F32)
        cand = sb("cand", [128, 2, 64], F32)
        zgk = [0]
        es_pre = pre_emit() if pre_emit is not None else None

        def load_zg(Zd, n0, n, zname):
            b = zgk[0] % 2; zgk[0] += 1
            for dd in range(2):
                P.dma("sp", ZG[dd * 64:(dd + 1) * 64, b, 0:n].rearrange("q n r p -> q n (r p)"),
                      Zd[dd, n0:n0 + n].rearrange("n g s -> g n s"), reads=[zname], writes=["ZG%d" % b])
            return b

        def steps(SGt, b, n, store, pre="SGs"):
            tA, tB = self.sc_tA, self.sc_tB
            fz = P._fz

            def spacer():
                P.op("dve", lambda e: e.memset(fz[:, 4:5], 0.0), writes=["fz_sp"], nosame=True)

            P.op("dve", lambda e: e.memset(fz[:, 5:6], 0.0), reads=[pre + str(i) for i in range(CH + 1)] + ["sc_tA", "sc_tB"], writes=["fz_sp2"])
            spacer()
            for i in range(n):
                a = i if store else i % 2
                bb = i + 1 if store else (i + 1) % 2
                S = SGt[:, a]
                Ssw = bass.AP(SGt[:].tensor, SGt[:, a, 1, :].offset, [list(SGt[:].ap[0]), [-64, 2], [1, 64]])
                P.op("dve", lambda e: e.tensor_tensor(tB[:], self.AI2[:], Ssw, ALU.mult), reads=[pre + str(a), "AC"], writes=["sc_tB"], nosame=True)
                P.op("dve", lambda e: e.tensor_tensor(tA[:], self.AR2[:], S, ALU.mult), reads=[pre + str(a), "AC"], writes=["sc_tA"], nosame=True)
                P.op("dve", lambda e: e.tensor_tensor(tB[:], tB[:], ZG[:, b, i], ALU.add), reads=["sc_tB", "ZG%d" % b], writes=["sc_tB"], nosame=True)
                spacer()
                P.op("dve", lambda e: e.tensor_tensor(SGt[:, bb], tA[:], tB[:], ALU.add), reads=["sc_tA", "sc_tB"], writes=[pre + str(bb)], nosame=True)
                spacer()
            P.op("dve", lambda e: e.memset(fz[:, 5:6], 0.0), reads=[pre + str(i) for i in range(CH + 1)] + ["sc_tA", "sc_tB", "fz_sp"], writes=["fz_sp2"] + [pre + str(i) for i in range(CH + 1)])

        P.op("dve", lambda e: e.memset(SGc[:, 0], 0.0), writes=["SGs0"])
        for ch in range(NSC // CH):
            b = load_zg(d["Zc"], ch * CH, CH, "Zc")
            steps(SGc, b, CH, False)
        P.op("dve", lambda e: e.tensor_copy(acc[:], SGc[:, 0]), reads=["SGs0"], writes=["acc"])
        P.op("dve", lambda e: e.memset(SGp[:, 0], 0.0), writes=["SGs0"], reads=["SGs0", "SGs1"])
        for ch in range(NS // CH):
            b = load_zg(d["Z"], ch * CH, CH, "Z")
            steps(SGp, b, CH, False)
        P.fence("dve", ["SGs0"])
        t = P.dma("sp", d["Ein"], SGp[:, 0].rearrange("q r p -> q (r p)"), reads=["SGs0"], writes=["Ein"])
        P.collective("AllGather", [[0, 1, 2, 3], [4, 5, 6, 7]], d["Ein"], d["Eout"], ["Ein"], ["Eout"], t)
        if es_pre is not None:
            self._b2()
        P.dma("sp", EG[:], d["Eout"].rearrange("(r q) s -> q r s", q=128), reads=["Eout"], writes=["EG"])
        tA, tB = self.sc_tA, self.sc_tB
        for jj in range(3):
            P.op("dve", lambda e: e.tensor_copy(Es[0:64], EG[0:64, jj, :].rearrange("q (r p) -> q r p", r=2)), reads=["EG"], writes=["Es"])
            P.op("dve", lambda e: e.tensor_copy(Es[64:128], EG[64:128, 3 - jj, :].rearrange("q (r p) -> q r p", r=2)), reads=["EG"], writes=["Es"])
            accsw = bass.AP(acc[:].tensor, acc[:, 1, :].offset, [list(acc[:].ap[0]), [-64, 2], [1, 64]])
            P.op("dve", lambda e: e.tensor_tensor(tA[:], self.ANR[:], acc[:], ALU.mult), reads=["acc", "AC"], writes=["sc_tA"])
            P.op("dve", lambda e: e.tensor_tensor(tB[:], self.ANI[:], accsw, ALU.mult), reads=["acc", "AC"], writes=["sc_tB"])
            P.op("dve", lambda e: e.tensor_tensor(tA[:], tA[:], tB[:], ALU.add), reads=["sc_tA", "sc_tB"], writes=["sc_tA"])
            P.op("dve", lambda e: e.tensor_tensor(cand[:], tA[:], Es[:], ALU.add), reads=["sc_tA", "Es"], writes=["cand"])
            P.op("dve", lambda e: e.tensor_tensor(cand[:], cand[:], acc[:], ALU.subtract), reads=["cand", "acc"], writes=["cand"])
            P.op("dve", lambda e: e.scalar_tensor_tensor(acc[:], cand[:], flags[:, jj:jj + 1], acc[:], ALU.mult, ALU.add),
                 reads=["cand", "acc", "flags"], writes=["acc"])
        names = [["SGA%d" % i for i in range(CH + 1)], ["SGB%d" % i for i in range(CH + 1)]]
        P.op("dve", lambda e: e.tensor_copy(SG[:, 0, 0], acc[:]), reads=["acc"], writes=[names[0][0]])
        bnext = load_zg(d["Z"], 0, CH, "Z")
        for ch in range(NS // CH):
            b = bnext
            kb_ = ch % 2
            pre = "SGA" if kb_ == 0 else "SGB"
            steps(SG[:, kb_], b, CH, True, pre)
            if ch + 1 < NS // CH:
                bnext = load_zg(d["Z"], (ch + 1) * CH, CH, "Z")
            allr = names[kb_]
            P.fence("dve", allr)
            for dd in range(2):
                P.dma("sp", d["SD"][dd, ch * CH:(ch + 1) * CH].rearrange("n g s -> g n s"),
                      SG[dd * 64:(dd + 1) * 64, kb_, 0:CH].rearrange("q n r p -> q n (r p)"), reads=allr, writes=["SD"])
            if ch + 1 < NS // CH:
                P.op("dve", lambda e: e.tensor_copy(SG[:, 1 - kb_, 0], SG[:, kb_, CH]), reads=[names[kb_][CH]], writes=[names[1 - kb_][0]])
        P.barrier()
        if es_pre is not None:
            es_pre.close()
        es.close()

    def phase_read(self):
        nc, P, d, c = self.nc, self.P, self.d, self.c
        NS, NSB, NBLK = self.NS, self.NSB, self.NBLK
        idb, Jb = c["idb"], c["Jb"]
        es = ExitStack()

        def sb(name, shape, dt):
            return es.enter_context(nc.sbuf_tensor(name, shape, dt))

        UT = sb("UT7", [128, 64, NS], BF16)
        ST = sb("ST", [128, 2, 64, NS], BF16)
        CT = sb("CT7", [128, 128, 128], BF16); DT = sb("DT7", [128, 64, 128], BF16)
        SRb = sb("SRb", [128, 2, 32, 128], BF16)
        YG = sb("YG", [NSB, NBLK, 8, 1024], BF16)
        pU = es.enter_context(nc.psum_tensor("pU7", [128, 4, 4, 128], F32))
        pY = es.enter_context(nc.psum_tensor("pY", [128, 2, 4, 128], F32))
        P.dma("sp", UT[:], d["UT_d"], reads=["UT_d"], writes=["UT7"])
        P.dma("sp", CT[:], d["CTd"], reads=["CTd"], writes=["CT7"])
        P.dma("sp", DT[:], d["DTd"], reads=["DTd"], writes=["DT7"])
        kslot = 0
        kb = 0
        for dd in range(2):
            for blk in range(NBLK):
                nat = blk if dd == 0 else NBLK - 1 - blk
                rhs = idb[0:NSB, 0:NSB] if dd == 0 else Jb[0:NSB, 128 - NSB:128]
                for gh in range(2):
                    bsel = kb % 2; kb += 1
                    P.dma("pool", SRb[0:NSB, bsel], d["SD"][dd, blk * NSB:(blk + 1) * NSB, gh * 32:(gh + 1) * 32], reads=["SD"], writes=["SRb%d" % bsel])
                    for g4 in range(8):
                        slot = kslot % 4; kslot += 1
                        for u in range(4):
                            P.op("pe", lambda e: e.matmul(pU[:, slot, u, 0:NSB], lhsT=SRb[0:NSB, bsel, g4 * 4 + u, :], rhs=rhs, start=True, stop=True),
                                 reads=["SRb%d" % bsel, "idb", "Jb"], writes=["pU%d" % slot])
                        G0 = gh * 32 + g4 * 4
                        if g4 % 2 == 0:
                            P.op("act", lambda e: e.activation(ST[:, dd, G0:G0 + 4, nat * NSB:(nat + 1) * NSB], pU[:, slot, :, 0:NSB], AF.Copy),
                                 reads=["pU%d" % slot], writes=["ST%d_%d" % (dd, G0 // 4)])
                        else:
                            P.op("dve", lambda e: e.tensor_copy(ST[:, dd, G0:G0 + 4, nat * NSB:(nat + 1) * NSB], pU[:, slot, :, 0:NSB]),
                                 reads=["pU%d" % slot], writes=["ST%d_%d" % (dd, G0 // 4)])
        for blk in range(NBLK):
            ns = slice(blk * NSB, (blk + 1) * NSB)
            for g4 in range(16):
                slot = g4 % 2
                for u in range(4):
                    g = g4 * 4 + u
                    P.op("pe", lambda e: e.matmul(pY[0:NSB, slot, u, :], lhsT=ST[:, 0, g, ns], rhs=CT[:, g, :], start=True, stop=False),
                         reads=["ST0_%d" % g4, "CT7"], writes=["pY%d" % slot])
                    P.op("pe", lambda e: e.matmul(pY[0:NSB, slot, u, :], lhsT=ST[:, 1, g, ns], rhs=CT[:, 64 + g, :], start=False, stop=False),
                         reads=["ST1_%d" % g4, "CT7"], writes=["pY%d" % slot])
                    P.op("pe", lambda e: e.matmul(pY[0:NSB, slot, u, :], lhsT=UT[:, g, ns], rhs=DT[:, g, :], start=False, stop=True),
                         reads=["UT7", "DT7"], writes=["pY%d" % slot])
                outv = YG[0:NSB, blk, :, g4 * 64:(g4 + 1) * 64].rearrange("n i (u c) -> n u i c", u=4)
                inv = pY[0:NSB, slot].rearrange("n u (i c) -> n u i c", i=8)
                P.op("act", lambda e: e.activation(outv, inv, AF.Gelu), reads=["pY%d" % slot], writes=["YG"])
        P.fence("act", ["YG"])
        for blk in range(NBLK):
            P.dma("sp", d["YG_d"][blk * NSB * 8:(blk + 1) * NSB * 8].rearrange("(n i) c -> n i c", i=8), YG[0:NSB, blk], reads=["YG"], writes=["YG_d"])
        P.barrier()
        es.close()


D = 2048
LN_EPS = 1e-6
ALPHA = 2.0 ** 0.25


def bcast_rows(ap_flat, n):
    return bass.AP(ap_flat.tensor, ap_flat.offset, [[0, 128], [1, n]])


class WL:
    def __init__(self, P, stg, nm):
        self.P, self.stg, self.nm, self.k = P, stg, nm, 0

    def dma(self, src, pat=None, **kw):
        s = self.k % 3; self.k += 1
        n = 1
        for dmn in src.shape[1:]:
            n *= dmn
        view = self.stg[:, s, 0:n]
        if pat is not None:
            view = view.rearrange(pat, **kw)
        self.P.dma("sp", view, src, writes=["%s%d" % (self.nm, s)])
        return (s, view)

    def cast(self, eng, h, dst, dst_res):
        s, view = h
        if eng == "act":
            self.P.op("act", lambda e: e.activation(dst, view, AF.Copy), reads=["%s%d" % (self.nm, s)], writes=dst_res)
        else:
            self.P.op(eng, lambda e: e.tensor_copy(dst, view), reads=["%s%d" % (self.nm, s)], writes=dst_res)


def build(NT, NCTX=256, debug=False, stop_after=None):
    nc = bass.Bass("TRN2", target_bir_lowering=False)
    NS, NSC = NT // 8, NCTX // 8
    NSB = min(128, NS); NBLK = NS // NSB
    NTT = NT // 128
    TB = min(512, NT)
    NTB = NT // TB

    def din(name, shape, dt=F32):
        return nc.dram_tensor(name, shape, dt, kind="ExternalInput").ap()

    def dsc(name, shape, dt=F32):
        return nc.dram_tensor(name, shape, dt).ap()

    x = din("x", [NT, D]); ctx = din("ctx", [NCTX, D]); cT = din("cT", [128, 16, 2])
    w_ada = din("w_ada", [D, 1536]); b_adaT = din("b_adaT", [128, 12]); w_in = din("w_in", [40, 128, 16, 128])
    sgu_g = din("sgu_g", [1, 1024]); sgu_b = din("sgu_b", [1, 1024])
    w_sp = din("w_sp", [8, 128, 128]); b_sp = din("b_sp", [1, 1024])
    w_glu = din("w_glu", [1024, 1024]); b_gluT = din("b_gluT", [128, 8]); w_out = din("w_out", [D, D])
    ln_g = din("ln_g", [1, D]); ln_b = din("ln_b", [1, D])
    dr = {}
    for nm, shp in (("lam_re", [128, 64]), ("lam_im", [128, 64]), ("log_step", [128, 1]), ("b_re", [128, 64, 16]),
                    ("b_im", [128, 64, 16]), ("c_re", [128, 16, 64]), ("c_im", [128, 16, 64])):
        dr[nm] = din(nm, shp)
    cd = {}
    for nm, shp, dt in (("idb", [128, 128], BF16), ("Jb", [128, 128], BF16), ("idf", [128, 128], F32), ("ML", [128, 128], F32), ("MU", [128, 128], F32),
                        ("dcol", [128, 64], F32), ("flags", [128, 3], F32)):
        cd[nm] = din("c_" + nm, shp, dt)
    y = nc.dram_tensor("y", [NT, D], F32, kind="ExternalOutput").ap()
    dr["SC"] = [dsc("SC%d" % f, [128, 8, 16, 128]) for f in range(3)]
    dr["BTd"] = dsc("BTd", [128, 128, 128], BF16); dr["CTd"] = dsc("CTd", [128, 128, 128], BF16); dr["DTd"] = dsc("DTd", [128, 64, 128], BF16)
    dr["Z"] = dsc("Zs", [2, NS, 64, 128]); dr["Zc"] = dsc("Zcs", [2, NSC, 64, 128]); dr["SD"] = dsc("SDs", [2, NS, 64, 128])
    dr["Ein"] = dsc("Ein", [128, 128]); dr["Eout"] = dsc("Eout", [512, 128])
    dr["U_d"] = dsc("U_d", [NSB, NBLK, 64, 128], BF16); dr["Uc_d"] = dsc("Uc_d", [NSC, 64, 128], BF16)
    dr["UT_d"] = dsc("UT_d", [128, 64, NS], BF16); dr["YG_d"] = dsc("YG_d", [NT, 1024], BF16)
    gsc = dsc("gsc", [16, 128]); mIn = dsc("mIn", [128, 24]); mOut = dsc("mOut", [512, 24]); YA_d = dsc("YA_d", [8, 128, NT], BF16); ZB_d = dsc("ZB_d", [8, 128, NT], BF16)
    dbg = {}
    if debug:
        for nm, shp, dt in (("xmT", [128, 16, NT], BF16), ("YA", [8, 128, NT], BF16), ("U", [NSB, NBLK, 64, 128], BF16), ("YG", [NT, 1024], BF16),
                            ("YB", [128, 8, NT], BF16), ("modT", [128, 48, 2], F32)):
            dbg[nm] = nc.dram_tensor("dbg_" + nm, shp, dt, kind="ExternalOutput").ap()

    P = Prog(nc, n_dma_sems=12)
    P.op("dve", lambda e: e.memset(P._fz[:], 0.0), writes=["fence_z"])
    keep = ExitStack()

    def kb(name, shape, dt):
        return keep.enter_context(nc.sbuf_tensor(name, shape, dt))

    consts = {}
    for nm in cd:
        consts[nm] = kb("k_" + nm, list(cd[nm].shape), cd[nm].dtype)
        P.dma("sp", consts[nm][:], cd[nm], writes=[nm])
    idb, idf = consts["idb"], consts["idf"]
    modT = kb("modT", [128, 48, 2], F32)
    S1 = kb("S1", [128, 16, 2], F32)
    bada = kb("bada", [128, 12], F32); bglu = kb("bglu", [128, 8], F32)
    P.dma("sp", bada[:], b_adaT, writes=["bada"]); P.dma("sp", bglu[:], b_gluT, writes=["bglu"])
    s5 = S5(nc, P, NS, NSC, dr, consts)

    s5.AR2 = kb("AR2", [128, 2, 64], F32); s5.AI2 = kb("AI2", [128, 2, 64], F32)
    s5.ANR = kb("ANR", [128, 2, 64], F32); s5.ANI = kb("ANI", [128, 2, 64], F32)
    es = ExitStack()
    cTt = es.enter_context(nc.sbuf_tensor("cTt", [128, 16, 2], F32))
    Wt = es.enter_context(nc.sbuf_tensor("Wt", [128, 2, 16, 384], F32))
    gt = es.enter_context(nc.sbuf_tensor("gt", [128, 16], F32)); gt2 = es.enter_context(nc.sbuf_tensor("gt2", [16, 128], F32))
    modP = es.enter_context(nc.sbuf_tensor("modP", [128, 12, 2], F32))
    P.dma("sp", cTt[:], cT, writes=["cTt"])
    wsrc = w_ada.rearrange("(kt p) n -> p kt n", p=128)
    for cb in range(2):
        P.dma("sp", Wt[:, cb], wsrc[:, :, cb * 384:(cb + 1) * 384], writes=["Wt%d" % cb])
    s5.setup(keep)
    if stop_after == "p1":
        P.barrier()
        es.close(); keep.close(); P.close()
        return nc

    P.op("act", lambda e: e.activation(cTt[:], cTt[:], AF.Silu), reads=["cTt"], writes=["cTt"])
    pM = es.enter_context(nc.psum_tensor("pM", [128, 2, 512], F32))
    for cb in range(4):
        wb = cb % 2
        if cb >= 2:
            P.dma("sp", Wt[:, wb], wsrc[:, :, cb * 384:(cb + 1) * 384], writes=["Wt%d" % wb])
        for c3 in range(3):
            ct = cb * 3 + c3
            slot = ct % 2
            for kt in range(16):
                P.op("pe", lambda e: e.matmul(pM[:, slot, 0:2], lhsT=Wt[:, wb, kt, c3 * 128:(c3 + 1) * 128], rhs=cTt[:, kt, :],
                                              start=(kt == 0), stop=(kt == 15)), reads=["Wt%d" % wb, "cTt"], writes=["pM%d" % slot])
            P.op("act", lambda e: e.activation(modP[:, ct, :], pM[:, slot, 0:2], AF.Identity, bias=bada[:, ct:ct + 1]),
                 reads=["pM%d" % slot, "bada"], writes=["modP"])
    P.fence("act", ["modP"])
    tg = P.dma("sp", mIn, modP[:].rearrange("p c t -> p (c t)"), reads=["modP"], writes=["mIn"])
    P.collective("AllGather", [[0, 1, 2, 3], [4, 5, 6, 7]], mIn, mOut, ["mIn"], ["mOut"], tg)
    P.dma("sp", modT[:].rearrange("p (r c) t -> p r (c t)", r=4), mOut.rearrange("(r p) f -> p r f", p=128), reads=["mOut"], writes=["modT"])
    P.op("dve", lambda e: e.tensor_scalar(S1[:], modT[:, 16:32, :], 1.0, None, ALU.add), reads=["modT"], writes=["S1"])
    P.op("dve", lambda e: e.tensor_copy(gt[:], modT[:, 32:48, 0]), reads=["modT"], writes=["gt"])
    P.op("pe", lambda e: e.transpose(pM[0:16, 0, 0:128], gt[:], idf[:]), reads=["gt", "idf", "pM0"], writes=["pM0"])
    P.op("act", lambda e: e.activation(gt2[:], pM[0:16, 0, 0:128], AF.Copy), reads=["pM0"], writes=["gt2"])
    P.fence("act", ["gt2"])
    P.dma("sp", gsc, gt2[:], reads=["gt2"], writes=["gsc"])
    if debug:
        P.fence("act", ["modT"])
        P.dma("sp", dbg["modT"], modT[:], reads=["modT"])
    P.barrier()
    es.close()
    if stop_after == 'p2':
        P.barrier()
        keep.close(); P.close()
        return nc

    esA = ExitStack()
    xmT = esA.enter_context(nc.sbuf_tensor("xmT", [128, 16, NT], BF16))
    xcT = esA.enter_context(nc.sbuf_tensor("xcT", [128, 16, NCTX], BF16))
    es = ExitStack()
    xt = es.enter_context(nc.sbuf_tensor("xt", [128, 2, D], F32))
    st6 = es.enter_context(nc.sbuf_tensor("st6", [128, 4, 6], F32)); mv = es.enter_context(nc.sbuf_tensor("mv", [128, 2], F32))
    rstd = es.enter_context(nc.sbuf_tensor("rstd", [128, 1], F32))
    pX = es.enter_context(nc.psum_tensor("pX", [128, 4, 4, 128], F32))

    def ln_stats(src, nm):
        for q in range(4):
            P.op("dve", lambda e: e.bn_stats(st6[:, q, :], src[:, q * 512:(q + 1) * 512]), reads=[nm], writes=["st6"])
        P.op("dve", lambda e: e.bn_aggr(mv[:], st6[:]), reads=["st6"], writes=["mv"])
        P.op("dve", lambda e: e.tensor_scalar(rstd[:], mv[:, 1:2], LN_EPS, None, ALU.add), reads=["mv"], writes=["rstd"])
        P.op("act", lambda e: e.activation(rstd[:], rstd[:], AF.Sqrt), reads=["rstd"], writes=["rstd"])
        P.op("dve", lambda e: e.reciprocal(rstd[:], rstd[:]), reads=["rstd"], writes=["rstd"])

    ks = 0
    for t in range(NTT + NCTX // 128):
        isx = t < NTT
        src = x[t * 128:(t + 1) * 128] if isx else ctx[(t - NTT) * 128:(t - NTT + 1) * 128]
        dstT, tt, col = (xmT, t, 0) if isx else (xcT, t - NTT, 1)
        b = t % 2
        P.dma("sp", xt[:, b], src, writes=["xt%d" % b])
        ln_stats(xt[:, b], "xt%d" % b)
        P.op("dve", lambda e: e.tensor_scalar(xt[:, b], xt[:, b], mv[:, 0:1], rstd[:, 0:1], ALU.subtract, ALU.mult), reads=["xt%d" % b, "mv", "rstd"], writes=["xt%d" % b])
        for k4 in range(4):
            slot = ks % 4; ks += 1
            for u in range(4):
                kt = k4 * 4 + u
                P.op("pe", lambda e: e.transpose(pX[:, slot, u, :], xt[:, b, kt * 128:(kt + 1) * 128], idf[:]), reads=["xt%d" % b, "idf"], writes=["pX%d" % slot])
            for u in range(4):
                kt = k4 * 4 + u
                P.op("act", lambda e: e.activation(dstT[:, kt, tt * 128:(tt + 1) * 128], pX[:, slot, u, :], AF.Identity,
                                                   scale=S1[:, kt, col:col + 1], bias=modT[:, kt, col:col + 1]),
                     reads=["pX%d" % slot, "S1", "modT"], writes=[("xmT_%d_%d" if isx else "xcT_%d_%d") % (tt, kt)])
    if debug:
        P.barrier()
        P.dma("sp", dbg["xmT"], xmT[:], reads=[])
    P.barrier()
    es.close()
    if stop_after == 'p3':
        P.barrier()
        esA.close()
        keep.close(); P.close()
        return nc

    es = ExitStack()
    BIGN = max(8 * NT, NBLK * 8192)
    BIG = es.enter_context(nc.sbuf_tensor("BIG", [128, BIGN], BF16))
    MXF = BIG[:, 0:8 * NT].rearrange("p (c t) -> p c t", c=8)
    Wv = es.enter_context(nc.sbuf_tensor("Wv", [128, 16, 1024], BF16))
    Wu = es.enter_context(nc.sbuf_tensor("Wu", [128, 3, 16, 128], BF16))
    stg4 = es.enter_context(nc.sbuf_tensor("stg4", [128, 3, 2048], F32))
    wl = WL(P, stg4, "stg4_")
    Ucs = stg4[0:NSC].rearrange("p a b -> p (a b)").bitcast(BF16)[:, 0:8192].rearrange("p (g s) -> p g s", g=64)
    WsT = es.enter_context(nc.sbuf_tensor("WsT", [128, 8, 128], BF16))
    grow = es.enter_context(nc.sbuf_tensor("grow", [128, 1024], F32)); brow = es.enter_context(nc.sbuf_tensor("brow", [128, 1024], F32))
    bsrow = es.enter_context(nc.sbuf_tensor("bsrow", [128, 8, 128], F32)); mxt = es.enter_context(nc.sbuf_tensor("mxt", [128, 8, 128], F32))
    vg = es.enter_context(nc.sbuf_tensor("vg", [128, 1024], F32)); vnb = es.enter_context(nc.sbuf_tensor("vnb", [128, 2, 1024], BF16))
    tmpu = es.enter_context(nc.sbuf_tensor("tmpu", [128, 2, TB], BF16))
    zbt = es.enter_context(nc.sbuf_tensor("zbt", [128, 2, TB], BF16))
    st2 = es.enter_context(nc.sbuf_tensor("st2", [128, 2, 6], F32)); mv2 = es.enter_context(nc.sbuf_tensor("mv2", [128, 2], F32))
    rs2 = es.enter_context(nc.sbuf_tensor("rs2", [128, 1], F32))
    pV = es.enter_context(nc.psum_tensor("pV", [128, 2, 2, 512], F32))
    pS = es.enter_context(nc.psum_tensor("pS", [128, 2, 4, 128], F32))
    pA = es.enter_context(nc.psum_tensor("pA", [128, 2, 512], F32))
    P.dma("sp", grow[:], bcast_rows(sgu_g, 1024), writes=["grow"]); P.dma("sp", brow[:], bcast_rows(sgu_b, 1024), writes=["brow"])
    P.dma("sp", bsrow[:].rearrange("p h q -> p (h q)"), bcast_rows(b_sp, 1024), writes=["bsrow"])
    Wsl = mxt[:].rearrange("p h q -> p (h q)").bitcast(BF16)[:, 0:1024].rearrange("p (h q) -> p h q", h=8)
    P.dma("pool", Wsl, w_sp.rearrange("h p q -> p h q"), writes=["mxt"])
    for h in range(8):
        P.op("pe", lambda e: e.matmul(pS[:, h // 4, h % 4, :], lhsT=Wsl[:, h, :], rhs=idb[:], start=True, stop=True), reads=["mxt", "idb"], writes=["pS%d" % (h // 4)])
    for hh in range(2):
        P.op("act", lambda e: e.activation(WsT[:, hh * 4:hh * 4 + 4, :], pS[:, hh], AF.Copy), reads=["pS%d" % hh], writes=["WsT"])
    for i8 in range(8):
        h_ = wl.dma(w_in[8 + i8], "p (k c) -> p k c", c=128)
        wl.cast("act" if i8 % 2 == 0 else "dve", h_, Wv[:, :, i8 * 128:(i8 + 1) * 128], ["Wv"])
    def p4a_proj(c_):
        vb = c_ % 2
        for half in range(2):
            for kt in range(16):
                P.op("pe", lambda e: e.matmul(pV[:, vb, half, :], lhsT=xmT[:, kt, c_ * 128:(c_ + 1) * 128], rhs=Wv[:, kt, half * 512:(half + 1) * 512],
                                              start=(kt == 0), stop=(kt == 15)), reads=["xmT", "Wv"], writes=["pV%d" % vb])
        P.op("act", lambda e: e.activation(vg[:].rearrange("p (h n) -> p h n", h=2), pV[:, vb], AF.Gelu), reads=["pV%d" % vb], writes=["vg"])
        for q in range(2):
            P.op("dve", lambda e: e.bn_stats(st2[:, q, :], vg[:, q * 512:(q + 1) * 512]), reads=["vg"], writes=["st2"])
        P.op("dve", lambda e: e.bn_aggr(mv2[:], st2[:]), reads=["st2"], writes=["mv2"])
        P.op("dve", lambda e: e.tensor_scalar(rs2[:], mv2[:, 1:2], LN_EPS, None, ALU.add), reads=["mv2"], writes=["rs2"])
        P.op("act", lambda e: e.activation(rs2[:], rs2[:], AF.Sqrt), reads=["rs2"], writes=["rs2"])
        P.op("dve", lambda e: e.reciprocal(rs2[:], rs2[:]), reads=["rs2"], writes=["rs2"])
        P.op("dve", lambda e: e.tensor_scalar(vg[:], vg[:], mv2[:, 0:1], rs2[:, 0:1], ALU.subtract, ALU.mult), reads=["vg", "mv2", "rs2"], writes=["vg"])
        P.op("dve", lambda e: e.tensor_tensor(vg[:], vg[:], grow[:], ALU.mult), reads=["vg", "grow"], writes=["vg"])
        P.op("dve", lambda e: e.tensor_tensor(vnb[:, vb, :], vg[:], brow[:], ALU.add), reads=["vg", "brow"], writes=["vnb%d" % vb])

    def p4a_spatial(c_):
        for h in range(8):
            hs = "pS%d" % (h // 4)
            o = pS[:, h // 4, h % 4, :]
            P.op("pe", lambda e: e.matmul(o, lhsT=vnb[:, c_ % 2, h * 128:(h + 1) * 128], rhs=WsT[:, h, :], start=True, stop=True), reads=["vnb%d" % (c_ % 2), "WsT"], writes=[hs])
        P.op("act", lambda e: e.activation(mxt[:, 0:4, :], pS[:, 0], AF.Copy), reads=["pS0"], writes=["mxt"])
        P.op("act", lambda e: e.activation(mxt[:, 4:8, :], pS[:, 1], AF.Copy), reads=["pS1"], writes=["mxt"])
        P.op("dve", lambda e: e.tensor_tensor(MXF[:, :, c_ * 128:(c_ + 1) * 128], mxt[:], bsrow[:], ALU.add), reads=["mxt", "bsrow"], writes=["MXF"])

    for c_ in range(NTT + 1):
        if c_ < NTT:
            p4a_proj(c_)
        if c_ >= 1:
            p4a_spatial(c_ - 1)

    blocks = [(grp, ct) for grp in range(3) for ct in range(8)]
    grp_info = ((0, AF.Gelu), (16, AF.Silu), (32, AF.Silu))
    hnd = {}
    ak = 0
    for k in range(-2, len(blocks)):
        if 0 <= k + 2 < len(blocks):
            g2, c2 = blocks[k + 2]
            hnd[k + 2] = wl.dma(w_in[grp_info[g2][0] + c2], "p (k c) -> p k c", c=128)
        if 0 <= k + 1 < len(blocks):
            wl.cast("act", hnd[k + 1], Wu[:, (k + 1) % 3], ["Wu%d" % ((k + 1) % 3)])
        if k < 0:
            continue
        grp, ct = blocks[k]
        func = grp_info[grp][1]
        wb = k % 3
        zb_ = ct % 2
        for tb in range(NTB):
            slot = ak % 2; ak += 1
            ts = slice(tb * TB, (tb + 1) * TB)
            for kt in range(16):
                P.op("pe", lambda e: e.matmul(pA[:, slot, 0:TB], lhsT=Wu[:, wb, kt, :], rhs=xmT[:, kt, ts], start=(kt == 0), stop=(kt == 15)),
                     reads=["Wu%d" % wb, "xmT"], writes=["pA%d" % slot])
            if grp < 2:
                P.op("act", lambda e: e.activation(tmpu[:, slot, :], pA[:, slot, 0:TB], func), reads=["pA%d" % slot], writes=["tmpu%d" % slot])
                P.op("dve", lambda e: e.tensor_tensor(MXF[:, ct, ts], MXF[:, ct, ts], tmpu[:, slot, :], ALU.mult), reads=["MXF", "tmpu%d" % slot], writes=["MXF"])
            else:
                P.op("act", lambda e: e.activation(zbt[:, slot, :], pA[:, slot, 0:TB], func), reads=["pA%d" % slot], writes=["zbt%d" % slot])
                P.fence("act", ["zbt%d" % slot])
                P.dma("pool", ZB_d[ct][:, ts], zbt[:, slot, :], reads=["zbt%d" % slot], writes=["ZB_d"])
        if grp == 1 and ct == 7:
            P.fence("dve", ["MXF"])
            P.dma("pool", YA_d.rearrange("c p t -> p c t"), MXF, reads=["MXF"], writes=["YA_d"])
            if debug:
                P.dma("sp", dbg["YA"].rearrange("c p t -> p c t"), MXF, reads=["MXF"])
    Uv = BIG[0:NSB, 0:NBLK * 8192].rearrange("n (b g s) -> n b g s", b=NBLK, g=64)
    for i8 in range(8):
        h_ = wl.dma(w_in[24 + i8], "p (k c) -> p k c", c=128)
        wl.cast("act" if i8 % 2 == 0 else "dve", h_, Wv[:, :, i8 * 128:(i8 + 1) * 128], ["Wv"])
    vk = 0
    for (srcT, nrows, nblk, dstv, sname, dname) in ((xmT, NSB, NBLK, None, "xmT", "Uv"), (xcT, NSC, 1, None, "xcT", "Ucs")):
        for blk in range(nblk):
            for j in range(8):
                vb = vk % 2; vk += 1
                t0 = blk * nrows * 8 + j
                for half in range(2):
                    for kt in range(16):
                        P.op("pe", lambda e: e.matmul(pV[0:nrows, vb, half, :], lhsT=srcT[:, kt, t0:t0 + (nrows - 1) * 8 + 1:8], rhs=Wv[:, kt, half * 512:(half + 1) * 512],
                                                      start=(kt == 0), stop=(kt == 15)), reads=[sname, "Wv"], writes=["pV%d" % vb])
                if dname == "Uv":
                    outv = Uv[:, blk, :, j * 16:(j + 1) * 16]
                else:
                    outv = Ucs[:, :, j * 16:(j + 1) * 16]
                inv = pV[0:nrows, vb].rearrange("n h (g c) -> n (h g) c", c=16)
                if vk % 2 == 0:
                    P.op("act", lambda e: e.activation(outv, inv, AF.Copy), reads=["pV%d" % vb], writes=[dname, "MXF"] if dname == "Uv" else [dname, "stg4_0", "stg4_1", "stg4_2"])
                else:
                    P.op("dve", lambda e: e.tensor_copy(outv, inv), reads=["pV%d" % vb], writes=[dname, "MXF"] if dname == "Uv" else [dname, "stg4_0", "stg4_1", "stg4_2"])
    P.fence("act", ["Uv", "Ucs"]); P.fence("dve", ["Uv", "Ucs"])
    P.dma("sp", dr["U_d"], Uv, reads=["Uv"], writes=["U_d"])
    P.dma("sp", dr["Uc_d"], Ucs, reads=["Ucs"], writes=["Uc_d"])
    if debug:
        P.dma("sp", dbg["U"], Uv, reads=["Uv"])
    P.barrier()
    es.close()
    esA.close()

    if stop_after == 'p4':
        P.barrier()
        keep.close(); P.close()
        return nc
    s5.phase_z()
    if stop_after == 'p5':
        P.barrier()
        keep.close(); P.close()
        return nc
    s5.phase_scan(consts["flags"], pre_emit=s5.setup_b)
    if stop_after == 'p6':
        P.barrier()
        keep.close(); P.close()
        return nc
    s5.phase_read()
    if stop_after == 'p7':
        P.barrier()
        keep.close(); P.close()
        return nc
    if debug:
        P.dma("sp", dbg["YG"], dr["YG_d"], reads=["YG_d"])

    esB = ExitStack()
    YB = esB.enter_context(nc.sbuf_tensor("YB", [128, 8, NT], BF16))
    Wo = esB.enter_context(nc.sbuf_tensor("Wo", [128, 16, D], BF16))
    stg8 = esB.enter_context(nc.sbuf_tensor("stg8", [128, 3, 2048], F32))
    wl8 = WL(P, stg8, "stg8_")
    wo = w_out.rearrange("(ct p) n -> p ct n", p=128)
    wgl = w_glu.rearrange("(ct p) n -> p ct n", p=128)
    es = ExitStack()
    ygT = es.enter_context(nc.sbuf_tensor("ygT", [128, 8, NT], BF16))
    YGt = es.enter_context(nc.sbuf_tensor("YGt", [128, 2, 1024], BF16))
    Wg = es.enter_context(nc.sbuf_tensor("Wg", [128, 8, 1024], BF16))
    sgt = es.enter_context(nc.sbuf_tensor("sgt", [128, 2, TB], BF16))
    zb2 = es.enter_context(nc.sbuf_tensor("zb2", [128, 2, NT], BF16))
    pX = es.enter_context(nc.psum_tensor("pX8", [128, 2, 4, 128], F32))
    pA = es.enter_context(nc.psum_tensor("pA8", [128, 2, 512], F32))
    for ct in range(8):
        h_ = wl8.dma(wgl[:, ct, :])
        wl8.cast("act" if ct % 2 == 0 else "dve", h_, Wg[:, ct, :], ["Wg"])
    wo_next = [0]

    def load_wo(n):
        for _ in range(n):
            if wo_next[0] < 16:
                c_ = wo_next[0]; wo_next[0] += 1
                h_ = wl8.dma(wo[:, c_, :])
                wl8.cast("dve", h_, Wo[:, c_, :], ["Wo"])

    for t in range(NTT):
        b = t % 2
        P.dma("sp", YGt[:, b], dr["YG_d"][t * 128:(t + 1) * 128], reads=["YG_d"], writes=["YGt%d" % b])
        load_wo(1)
        for hh in range(2):
            for u in range(4):
                ct = hh * 4 + u
                P.op("pe", lambda e: e.matmul(pX[:, hh, u, :], lhsT=YGt[:, b, ct * 128:(ct + 1) * 128], rhs=idb[:], start=True, stop=True),
                     reads=["YGt%d" % b, "idb"], writes=["pX%d" % hh])
            if hh == 0:
                P.op("act", lambda e: e.activation(ygT[:, 0:4, t * 128:(t + 1) * 128], pX[:, 0], AF.Copy), reads=["pX0"], writes=["ygT"])
            else:
                P.op("dve", lambda e: e.tensor_copy(ygT[:, 4:8, t * 128:(t + 1) * 128], pX[:, 1]), reads=["pX1"], writes=["ygT"])
    ak = 0
    for co in range(8):
        zb_ = co % 2
        P.dma("sp", zb2[:, zb_], ZB_d[co], reads=["ZB_d"], writes=["zb2%d" % zb_])
        for tb in range(NTB):
            slot = ak % 2; ak += 1
            ts = slice(tb * TB, (tb + 1) * TB)
            for ct in range(8):
                P.op("pe", lambda e: e.matmul(pA[:, slot, 0:TB], lhsT=Wg[:, ct, co * 128:(co + 1) * 128], rhs=ygT[:, ct, ts], start=(ct == 0), stop=(ct == 7)),
                     reads=["Wg", "ygT"], writes=["pA%d" % slot])
            P.op("act", lambda e: e.activation(sgt[:, slot, :], pA[:, slot, 0:TB], AF.Sigmoid, bias=bglu[:, co:co + 1]), reads=["pA%d" % slot, "bglu"], writes=["sgt%d" % slot])
            P.op("dve", lambda e: e.tensor_tensor(sgt[:, slot, :], sgt[:, slot, :], ygT[:, co, ts], ALU.mult), reads=["sgt%d" % slot, "ygT"], writes=["sgt%d" % slot])
            P.op("dve", lambda e: e.tensor_tensor(YB[:, co, ts], sgt[:, slot, :], zb2[:, zb_, ts], ALU.mult), reads=["sgt%d" % slot, "zb2%d" % zb_], writes=["YB"])
    load_wo(16)
    if debug:
        P.fence("dve", ["YB"])
        P.dma("sp", dbg["YB"], YB[:], reads=["YB"])
    P.barrier()
    es.close()
    if stop_after == 'p8':
        P.barrier()
        esB.close()
        keep.close(); P.close()
        return nc

    es = ExitStack()
    YA = es.enter_context(nc.sbuf_tensor("YA", [128, 8, NT], BF16))
    P.dma("sp", YA[:], YA_d.rearrange("c p t -> p c t"), reads=["YA_d"], writes=["YA"])
    Grow = es.enter_context(nc.sbuf_tensor("Grow", [128, D], F32))
    lgr = es.enter_context(nc.sbuf_tensor("lgr", [128, D], F32)); lbr = es.enter_context(nc.sbuf_tensor("lbr", [128, D], F32))
    xt = es.enter_context(nc.sbuf_tensor("xt9", [128, 2, D], F32)); ot = stg8
    st6 = es.enter_context(nc.sbuf_tensor("st69", [128, 4, 6], F32)); mv = es.enter_context(nc.sbuf_tensor("mv9", [128, 2], F32))
    rstd = es.enter_context(nc.sbuf_tensor("rstd9", [128, 1], F32))
    pO = es.enter_context(nc.psum_tensor("pO", [128, 2, 4, 512], F32))
    P.dma("sp", Grow[:], bcast_rows(gsc, D), reads=["gsc"], writes=["Grow"])
    P.dma("sp", lgr[:], bcast_rows(ln_g, D), writes=["lgr"]); P.dma("sp", lbr[:], bcast_rows(ln_b, D), writes=["lbr"])
    outs = []
    for t in range(NTT):
        b = t % 2
        tsl = slice(t * 128, (t + 1) * 128)
        P.dma("sp", xt[:, b], x[tsl], writes=["xt%d" % b])
        for db in range(4):
            for ct in range(16):
                src = YA if ct < 8 else YB
                P.op("pe", lambda e: e.matmul(pO[:, b, db, :], lhsT=src[:, ct % 8, tsl], rhs=Wo[:, ct, db * 512:(db + 1) * 512], start=(ct == 0), stop=(ct == 15)),
                     reads=["YA", "YB", "Wo"], writes=["pO%d" % b])
        o = ot[:, b]
        P.op("act", lambda e: e.activation(o.rearrange("p (a n) -> p a n", a=4), pO[:, b], AF.Copy), reads=["pO%d" % b], writes=["ot%d" % b])
        P.op("dve", lambda e: e.tensor_tensor(o, o, Grow[:], ALU.mult), reads=["ot%d" % b, "Grow"], writes=["ot%d" % b])
        P.op("dve", lambda e: e.scalar_tensor_tensor(o, xt[:, b], ALPHA, o, ALU.mult, ALU.add), reads=["ot%d" % b, "xt%d" % b], writes=["ot%d" % b])
        for q in range(4):
            P.op("dve", lambda e: e.bn_stats(st6[:, q, :], o[:, q * 512:(q + 1) * 512]), reads=["ot%d" % b], writes=["st6"])
        P.op("dve", lambda e: e.bn_aggr(mv[:], st6[:]), reads=["st6"], writes=["mv"])
        P.op("dve", lambda e: e.tensor_scalar(rstd[:], mv[:, 1:2], LN_EPS, None, ALU.add), reads=["mv"], writes=["rstd"])
        P.op("act", lambda e: e.activation(rstd[:], rstd[:], AF.Sqrt), reads=["rstd"], writes=["rstd"])
        P.op("dve", lambda e: e.reciprocal(rstd[:], rstd[:]), reads=["rstd"], writes=["rstd"])
        P.op("dve", lambda e: e.tensor_scalar(o, o, mv[:, 0:1], rstd[:, 0:1], ALU.subtract, ALU.mult), reads=["ot%d" % b, "mv", "rstd"], writes=["ot%d" % b])
        P.op("pool", lambda e: e.tensor_tensor(o, o, lgr[:], ALU.mult), reads=["ot%d" % b, "lgr"], writes=["ot%d" % b])
        P.op("pool", lambda e: e.tensor_tensor(o, o, lbr[:], ALU.add), reads=["ot%d" % b, "lbr"], writes=["ot%d" % b])
        P.fence("pool", ["ot%d" % b])
        outs.append(P.dma("pool", y[tsl], o, reads=["ot%d" % b], writes=["y"]))
    P.finish("sp", outs)
    P.barrier()
    es.close()
    esB.close()
    keep.close()
    P.close()
    return nc


_NC_CACHE = {}


def make_in_maps(inputs, NT):
    import ml_dtypes
    f = lambda k: np.ascontiguousarray(np.asarray(inputs[k])[0], dtype=np.float32)
    x = np.asarray(inputs["x"], dtype=np.float32); c = np.asarray(inputs["c"], dtype=np.float32)
    ctx = np.asarray(inputs["ctx"], dtype=np.float32); cc = np.asarray(inputs["c_ctx"], dtype=np.float32)
    B, L, _ = x.shape
    cpb = L // NT
    assert B * cpb == 8 and cpb == 4
    ML = np.ascontiguousarray(np.kron(np.tril(np.ones((8, 8))), np.ones((16, 16))).astype(np.float32).T)
    MU = np.kron(np.tril(np.ones((8, 8))), np.ones((16, 16))).astype(np.float32)
    common = {
        "w_in": np.ascontiguousarray(f("w_in").reshape(16, 128, 40, 128).transpose(2, 1, 0, 3)),
        "sgu_g": f("sgu_ln_g").reshape(1, 1024), "sgu_b": f("sgu_ln_b").reshape(1, 1024),
        "w_sp": f("w_spatial"), "b_sp": f("b_spatial").reshape(1, 1024),
        "w_glu": f("w_glu"), "b_gluT": np.ascontiguousarray(f("b_glu").reshape(8, 128).T), "w_out": f("w_out"),
        "ln_g": f("ln_g").reshape(1, D), "ln_b": f("ln_b").reshape(1, D),
        "lam_re": f("s5_lam_re").reshape(128, 64), "lam_im": f("s5_lam_im").reshape(128, 64), "log_step": f("s5_log_step").reshape(128, 1),
        "b_re": f("s5_b_re").reshape(128, 64, 16), "b_im": f("s5_b_im").reshape(128, 64, 16),
        "c_re": f("s5_c_re").reshape(128, 16, 64), "c_im": f("s5_c_im").reshape(128, 16, 64),
        "c_idb": np.eye(128).astype(ml_dtypes.bfloat16), "c_Jb": np.ascontiguousarray(np.eye(128)[::-1]).astype(ml_dtypes.bfloat16),
        "c_idf": np.eye(128, dtype=np.float32), "c_ML": ML, "c_MU": MU,
        "c_dcol": np.ascontiguousarray(np.tile(f("s5_d").reshape(64, 16).T, (8, 1))),
    }
    ims = []
    for core in range(8):
        b, k = core // cpb, core % cpb
        fl = np.zeros((128, 3), np.float32)
        for jj in range(3):
            fl[0:64, jj] = 1.0 if jj < k else 0.0
            fl[64:128, jj] = 1.0 if (3 - jj) > k else 0.0
        cT = np.stack([c[b].reshape(16, 128).T, cc.reshape(16, 128).T], axis=-1).astype(np.float32)
        im = dict(common)
        wad = f("w_ada"); bad = f("b_ada")
        im["w_ada"] = np.ascontiguousarray(wad[:, k * 1536:(k + 1) * 1536])
        im["b_adaT"] = np.ascontiguousarray(bad[k * 1536:(k + 1) * 1536].reshape(12, 128).T)
        im.update({"x": np.ascontiguousarray(x[b, k * NT:(k + 1) * NT]), "ctx": np.ascontiguousarray(ctx[b]), "cT": np.ascontiguousarray(cT), "c_flags": fl})
        ims.append(im)
    return ims


def kernel(**inputs):
    x = np.asarray(inputs["x"])
    B, L, _ = x.shape
    NT = B * L // 8
    if NT not in _NC_CACHE:
        _NC_CACHE[NT] = build(NT, np.asarray(inputs["ctx"]).shape[1])
    nc = _NC_CACHE[NT]
    ims = make_in_maps(inputs, NT)
    res = run_bass_kernel_spmd(nc, ims, core_ids=list(range(8)))
    out = np.empty((B, L, D), np.float32)
    cpb = L // NT
    for core in range(8):
        b, k = core // cpb, core % cpb
        out[b, k * NT:(k + 1) * NT] = np.asarray(res.results[core]["y"])
    return out
```

```python
import time, sys, math
from contextlib import ExitStack
import numpy as np
import concourse.bass as bass
import concourse.mybir as mybir
from concourse.bass_utils import run_bass_kernel_spmd

F32 = mybir.dt.float32
BF16 = mybir.dt.bfloat16
ALU = mybir.AluOpType
AF = mybir.ActivationFunctionType
PI = math.pi


class Prog:
    def __init__(self, nc, n_dma_sems=10):
        self.nc = nc
        self.engs = {"pe": nc.tensor, "act": nc.scalar, "dve": nc.vector, "pool": nc.gpsimd, "sp": nc.sync}
        self.sem = {}
        self.cnt = {k: 0 for k in self.engs}
        self.seen = {k: {} for k in self.engs}
        self.es = ExitStack()
        for k in self.engs:
            self.sem[k] = self.es.enter_context(nc.semaphore("s_" + k))
        self.dsem = [self.es.enter_context(nc.semaphore("d_%d" % i)) for i in range(n_dma_sems)]
        self.dcnt = [0] * n_dma_sems
        self.ccsem = self.es.enter_context(nc.semaphore("cc_sem"))
        self.cccnt = 0
        self._fz = self.es.enter_context(nc.sbuf_tensor("fence_z", [128, 8], F32))
        self.dnext = 0
        self.lastw = {}
        self.readers = {}

    def _wait(self, eng, tok):
        if tok is None:
            return
        kind, key, val = tok
        if kind == "e" and key == "pe" and eng == "pe":
            return
        if kind == "e" and key == eng and getattr(self, "_nosame", False):
            return
        seen = self.seen[eng]
        if seen.get((kind, key), 0) >= val:
            return
        seen[(kind, key)] = val
        s = self.sem[key] if kind == "e" else (self.ccsem if kind == "c" else self.dsem[key])
        self.engs[eng].wait_ge(s, val)

    def _deps(self, eng, reads, writes):
        for r in reads:
            self._wait(eng, self.lastw.get(r))
        for w in writes:
            self._wait(eng, self.lastw.get(w))
            for t in self.readers.get(w, []):
                self._wait(eng, t)

    def _commit(self, tok, reads, writes):
        for r in reads:
            self.readers.setdefault(r, []).append(tok)
        for w in writes:
            self.lastw[w] = tok
            self.readers[w] = []

    def _pe_mode_guard(self, lhsT, kind):
        def r(n):
            return 32 if n <= 32 else (64 if n <= 64 else 128)
        m = 1
        for dmn in lhsT.shape[1:]:
            m *= dmn
        mode = (r(lhsT.shape[0]), r(m), str(lhsT.dtype), kind)
        if getattr(self, "_pe_mode", None) not in (None, mode) and self.cnt["pe"] > 0:
            self.engs["pe"].wait_ge(self.sem["pe"], self.cnt["pe"])
        self._pe_mode = mode

    def op(self, eng, fn, reads=(), writes=(), nosame=False):
        self._nosame = nosame
        self._deps(eng, reads, writes)
        self._nosame = False
        if eng == "pe":
            prog = self

            class _PE:
                def matmul(self_, out, lhsT, rhs, **kw):
                    prog._pe_mode_guard(lhsT, "mm")
                    return prog.engs["pe"].matmul(out, lhsT=lhsT, rhs=rhs, **kw)

                def transpose(self_, out, in_, ident):
                    prog._pe_mode_guard(in_, "tr")
                    return prog.engs["pe"].transpose(out, in_, ident)

            ins = fn(_PE())
            self.cnt[eng] += 1
            ins.then_inc(self.sem[eng], 1)
            tok = ("e", eng, self.cnt[eng])
            self._commit(tok, reads, writes)
            return tok
        ins = fn(self.engs[eng])
        self.cnt[eng] += 1
        ins.then_inc(self.sem[eng], 1)
        tok = ("e", eng, self.cnt[eng])
        self._commit(tok, reads, writes)
        return tok

    def dma(self, eng, out, in_, reads=(), writes=()):
        k = self.dnext
        self.dnext = (self.dnext + 1) % len(self.dsem)
        if self.dcnt[k] > 0:
            self._wait(eng, ("d", k, self.dcnt[k]))
        if eng == "pool" and len(getattr(self, "pool_dmas", [])) >= 2:
            self._wait(eng, self.pool_dmas[-2])
        self._deps(eng, reads, writes)
        ins = self.engs[eng].dma_start(out=out, in_=in_)
        self.dcnt[k] += 16
        ins.then_inc(self.dsem[k], 16)
        tok = ("d", k, self.dcnt[k])
        if eng == "pool":
            if not hasattr(self, "pool_dmas"):
                self.pool_dmas = []
            self.pool_dmas.append(tok)
        self._commit(tok, reads, writes)
        return tok

    def collective(self, kind, groups, src, dst, reads, writes, after):
        self._wait("pool", after)
        for tk in getattr(self, "pool_dmas", [])[-2:]:
            self._wait("pool", tk)
        self._deps("pool", reads, writes)
        ins = self.nc.gpsimd.collective_compute(kind, ALU.bypass, replica_groups=groups, ins=[src], outs=[dst])
        self.cccnt += 1
        ins.then_inc(self.ccsem, 1)
        tok = ("c", 0, self.cccnt)
        self._commit(tok, reads, writes)
        return tok

    def fence(self, eng, res):
        if eng == "act":
            self.op(eng, lambda e: e.activation(self._fz[:, 0:1], self._fz[:, 1:2], AF.Copy), reads=list(res), writes=list(res) + ["fence_z"])
        else:
            self.op(eng, lambda e: e.memset(self._fz[:, 0:1], 0.0), reads=list(res), writes=list(res) + ["fence_z"])

    def barrier(self):
        toks = [("e", k, self.cnt[k]) for k in self.engs if self.cnt[k] > 0]
        toks += [("d", i, c) for i, c in enumerate(self.dcnt) if c > 0]
        if self.cccnt > 0:
            toks.append(("c", 0, self.cccnt))
        for e in self.engs:
            for t in toks:
                self._wait(e, t)

    def finish(self, eng, toks):
        for t in toks:
            self._wait(eng, t)

    def close(self):
        self.es.close()


def cmul(P, eng, o_re, o_im, a_re, a_im, b_re, b_im, t0, t1, rd, wr, neg_im=False):
    T = ["cm_t0", "cm_t1"]
    P.op(eng, lambda e: e.tensor_tensor(t0, a_re, b_re, ALU.mult), reads=rd, writes=[T[0]])
    P.op(eng, lambda e: e.tensor_tensor(t1, a_im, b_im, ALU.mult), reads=rd, writes=[T[1]])
    P.op(eng, lambda e: e.tensor_tensor(o_re, t0, t1, ALU.subtract), reads=T, writes=wr)
    P.op(eng, lambda e: e.tensor_tensor(t0, a_re, b_im, ALU.mult), reads=rd + wr, writes=[T[0]])
    P.op(eng, lambda e: e.tensor_tensor(t1, a_im, b_re, ALU.mult), reads=rd + wr, writes=[T[1]])
    if neg_im:
        P.op(eng, lambda e: e.scalar_tensor_tensor(o_im, t0, -1.0, t1, ALU.mult, ALU.subtract), reads=T, writes=wr)
    else:
        P.op(eng, lambda e: e.tensor_tensor(o_im, t0, t1, ALU.add), reads=T, writes=wr)


class S5:
    def __init__(self, nc, P, NS, NSC, dr, consts):
        self.nc, self.P, self.NS, self.NSC, self.d = nc, P, NS, NSC, dr
        self.NSB = min(128, NS)
        self.NBLK = NS // self.NSB
        self.c = consts

    def setup(self, es_keep):
        nc, P, d = self.nc, self.P, self.d
        es = ExitStack()

        def sb(name, shape, dt, keep=False):
            return (es_keep if keep else es).enter_context(nc.sbuf_tensor(name, shape, dt))

        lr = sb("lr", [128, 64], F32); li = sb("li", [128, 64], F32); ls = sb("ls", [128, 1], F32)
        br = sb("br", [128, 64, 16], F32); bi = sb("bi", [128, 64, 16], F32)
        cr = sb("cr", [128, 16, 64], F32); ci = sb("ci", [128, 16, 64], F32)
        P.dma("sp", lr[:], d["lam_re"], writes=["lr"])
        P.dma("sp", li[:], d["lam_im"], writes=["li"])
        P.dma("sp", ls[:], d["log_step"], writes=["ls"])
        P.dma("sp", br[:], d["b_re"], writes=["br"])
        P.dma("sp", bi[:], d["b_im"], writes=["bi"])
        P.dma("sp", cr[:], d["c_re"], writes=["cr"])
        P.dma("sp", ci[:], d["c_im"], writes=["ci"])
        step = sb("step", [128, 1], F32)
        drt = sb("drt", [128, 64], F32); dit = sb("dit", [128, 64], F32)
        mag = sb("mag", [128, 64], F32); sn = sb("sn", [128, 64], F32); cs = sb("cs", [128, 64], F32)
        tA = sb("tA", [128, 64], F32); tB = sb("tB", [128, 64], F32)
        PW = sb("PW", [128, 16, 2, 64], F32)
        P.op("act", lambda e: e.activation(step[:], ls[:], AF.Exp), reads=["ls"], writes=["step"])
        P.op("dve", lambda e: e.tensor_scalar(drt[:], lr[:], step[:, 0:1], None, ALU.mult), reads=["lr", "step"], writes=["drt"])
        P.op("dve", lambda e: e.tensor_scalar(dit[:], li[:], step[:, 0:1], None, ALU.mult), reads=["li", "step"], writes=["dit"])
        P.op("act", lambda e: e.activation(mag[:], drt[:], AF.Exp), reads=["drt"], writes=["mag"])
        kk = sb("kk", [128, 64], F32)
        for (dst, dn, off) in ((tA, "cm_t0", 0.0), (tB, "cm_t1", 0.5 * PI)):
            P.op("dve", lambda e: e.tensor_scalar(dst[:], dit[:], off, None, ALU.add), reads=["dit"], writes=[dn])
            P.op("dve", lambda e: e.tensor_scalar(kk[:], dst[:], PI, None, ALU.is_ge), reads=[dn], writes=["kk"])
            for m in (3, 5, 7):
                P.op("dve", lambda e: e.scalar_tensor_tensor(kk[:], dst[:], m * PI, kk[:], ALU.is_ge, ALU.add), reads=[dn, "kk"], writes=["kk"])
            P.op("dve", lambda e: e.scalar_tensor_tensor(dst[:], kk[:], -2 * PI, dst[:], ALU.mult, ALU.add), reads=[dn, "kk"], writes=[dn])
        P.op("act", lambda e: e.activation(sn[:], tA[:], AF.Sin), reads=["cm_t0"], writes=["sn"])
        P.op("act", lambda e: e.activation(cs[:], tB[:], AF.Sin), reads=["cm_t1"], writes=["cs"])
        P.op("dve", lambda e: e.tensor_tensor(PW[:, 8, 0, :], cs[:], mag[:], ALU.mult), reads=["cs", "mag"], writes=["PW8"])
        P.op("dve", lambda e: e.tensor_tensor(PW[:, 8, 1, :], sn[:], mag[:], ALU.mult), reads=["sn", "mag"], writes=["PW8"])
        P.op("dve", lambda e: e.memset(PW[:, 7, 0, :], 1.0), writes=["PW7"])
        P.op("dve", lambda e: e.memset(PW[:, 7, 1, :], 0.0), writes=["PW7"])
        for k in range(2, 9):
            cmul(P, "dve", PW[:, 7 + k, 0, :], PW[:, 7 + k, 1, :], PW[:, 6 + k, 0, :], PW[:, 6 + k, 1, :],
                 PW[:, 8, 0, :], PW[:, 8, 1, :], tA[:], tB[:], ["PW%d" % (6 + k), "PW8"], ["PW%d" % (7 + k)])
        den = sb("den", [128, 64], F32)
        P.op("dve", lambda e: e.tensor_tensor(tA[:], PW[:, 8, 0, :], PW[:, 8, 0, :], ALU.mult), reads=["PW8", "cm_t0", "cm_t1"], writes=["cm_t0"])
        P.op("dve", lambda e: e.tensor_tensor(tB[:], PW[:, 8, 1, :], PW[:, 8, 1, :], ALU.mult), reads=["PW8", "cm_t0", "cm_t1"], writes=["cm_t1"])
        P.op("dve", lambda e: e.tensor_tensor(den[:], tA[:], tB[:], ALU.add), reads=["cm_t0", "cm_t1"], writes=["den"])
        P.op("dve", lambda e: e.reciprocal(den[:], den[:]), reads=["den"], writes=["den"])
        P.op("dve", lambda e: e.tensor_tensor(PW[:, 6, 0, :], PW[:, 8, 0, :], den[:], ALU.mult), reads=["PW8", "den"], writes=["PW6"])
        P.op("dve", lambda e: e.scalar_tensor_tensor(PW[:, 6, 1, :], PW[:, 8, 1, :], -1.0, den[:], ALU.mult, ALU.mult), reads=["PW8", "den"], writes=["PW6"])
        for k in range(2, 8):
            cmul(P, "dve", PW[:, 7 - k, 0, :], PW[:, 7 - k, 1, :], PW[:, 8 - k, 0, :], PW[:, 8 - k, 1, :],
                 PW[:, 6, 0, :], PW[:, 6, 1, :], tA[:], tB[:], ["PW%d" % (8 - k), "PW6", "cm_t0", "cm_t1"], ["PW%d" % (7 - k)])
        allpw = ["PW%d" % k for k in range(16)]
        fr = sb("fr", [128, 64], F32); fi = sb("fi", [128, 64], F32); nr = sb("nr", [128, 64], F32)
        P.op("dve", lambda e: e.tensor_scalar(nr[:], PW[:, 8, 0, :], -1.0, None, ALU.add), reads=["PW8"], writes=["nr"])
        P.op("dve", lambda e: e.tensor_tensor(tA[:], lr[:], lr[:], ALU.mult), reads=["lr", "cm_t0", "cm_t1"], writes=["cm_t0"])
        P.op("dve", lambda e: e.tensor_tensor(tB[:], li[:], li[:], ALU.mult), reads=["li", "cm_t0", "cm_t1"], writes=["cm_t1"])
        P.op("dve", lambda e: e.tensor_tensor(den[:], tA[:], tB[:], ALU.add), reads=["cm_t0", "cm_t1"], writes=["den"])
        P.op("dve", lambda e: e.reciprocal(den[:], den[:]), reads=["den"], writes=["den"])
        P.op("dve", lambda e: e.tensor_tensor(tA[:], nr[:], lr[:], ALU.mult), reads=["nr", "lr", "den"], writes=["cm_t0"])
        P.op("dve", lambda e: e.tensor_tensor(tB[:], PW[:, 8, 1, :], li[:], ALU.mult), reads=["PW8", "li", "den"], writes=["cm_t1"])
        P.op("dve", lambda e: e.tensor_tensor(fr[:], tA[:], tB[:], ALU.add), reads=["cm_t0", "cm_t1"], writes=["fr"])
        P.op("dve", lambda e: e.tensor_tensor(fr[:], fr[:], den[:], ALU.mult), reads=["fr", "den"], writes=["fr"])
        P.op("dve", lambda e: e.tensor_tensor(tA[:], PW[:, 8, 1, :], lr[:], ALU.mult), reads=["PW8", "lr", "fr"], writes=["cm_t0"])
        P.op("dve", lambda e: e.tensor_tensor(tB[:], nr[:], li[:], ALU.mult), reads=["nr", "li", "fr"], writes=["cm_t1"])
        P.op("dve", lambda e: e.tensor_tensor(fi[:], tA[:], tB[:], ALU.subtract), reads=["cm_t0", "cm_t1"], writes=["fi"])
        P.op("dve", lambda e: e.tensor_tensor(fi[:], fi[:], den[:], ALU.mult), reads=["fi", "den"], writes=["fi"])
        bb = sb("bb", [128, 2, 64, 16], F32)
        t0 = sb("t0", [128, 4096], F32); t1 = sb("t1", [128, 4096], F32)
        t0b = t0[:, 0:1024].rearrange("q (p c) -> q p c", c=16); t1b = t1[:, 0:1024].rearrange("q (p c) -> q p c", c=16)
        frb = fr[:].unsqueeze(2).to_broadcast([128, 64, 16]); fib = fi[:].unsqueeze(2).to_broadcast([128, 64, 16])
        cmul(P, "dve", bb[:, 0], bb[:, 1], frb, fib, br[:], bi[:], t0b, t1b, ["fr", "fi", "br", "bi", "cm_t0", "cm_t1"], ["bb"])
        PWa = sb("PWa", [128, 8, 2, 64], F32); PWc = sb("PWc", [128, 8, 2, 64], F32); PWg = sb("PWg", [128, 8, 2, 64], F32)
        for x in range(8):
            for (tbl, nm, kf, kb) in ((PWa, "PWa", 7 - x, x), (PWc, "PWc", x + 1, 8 - x), (PWg, "PWg", x - 7, -x)):
                P.op("pool", lambda e: e.tensor_copy(tbl[0:64, x], PW[0:64, kf + 7]), reads=allpw, writes=[nm])
                P.op("pool", lambda e: e.tensor_copy(tbl[64:128, x], PW[64:128, kb + 7]), reads=allpw, writes=[nm])
        fam = sb("fam", [128, 4, 16, 2, 64], F32)
        t0f = t0[:].rearrange("q (x c p) -> q x c p", x=4, c=16); t1f = t1[:].rearrange("q (x c p) -> q x c p", x=4, c=16)

        def bx(ap3):
            return ap3.unsqueeze(2).to_broadcast([128, 4, 16, 64])

        bbr = bb[:, 0].rearrange("q p c -> q c p").unsqueeze(1).to_broadcast([128, 4, 16, 64])
        bbi = bb[:, 1].rearrange("q p c -> q c p").unsqueeze(1).to_broadcast([128, 4, 16, 64])
        crb = cr[:].unsqueeze(1).to_broadcast([128, 4, 16, 64]); cib = ci[:].unsqueeze(1).to_broadcast([128, 4, 16, 64])
        for h in range(2):
            xs = slice(4 * h, 4 * h + 4)
            for f, (tbl, nm, br_, bi_, rdn, neg) in enumerate(((PWa, "PWa", bbr, bbi, ["bb"], False), (PWc, "PWc", crb, cib, ["cr", "ci"], True),
                                                               (PWg, "PWg", crb, cib, ["cr", "ci"], True))):
                cmul(P, "dve", fam[:, :, :, 0, :], fam[:, :, :, 1, :], bx(tbl[:, xs, 0, :]), bx(tbl[:, xs, 1, :]), br_, bi_, t0f, t1f,
                     [nm] + rdn, ["fam"], neg_im=neg)
                P.fence("dve", ["fam"])
                P.dma("sp", d["SC"][f][:, xs], fam[:], reads=["fam"], writes=["SC%d" % f])
        sq = sb("sq", [128, 2, 2, 64], F32)
        P.op("dve", lambda e: e.tensor_copy(sq[:, 0], PW[:, 15]), reads=["PW15"], writes=["sq0"])
        cur = 0
        nsq = int(round(math.log2(self.NS)))
        assert 2 ** nsq == self.NS
        for s in range(nsq):
            cmul(P, "dve", sq[:, 1 - cur, 0, :], sq[:, 1 - cur, 1, :], sq[:, cur, 0, :], sq[:, cur, 1, :], sq[:, cur, 0, :], sq[:, cur, 1, :],
                 tA[:], tB[:], ["sq%d" % cur, "cm_t0", "cm_t1"], ["sq%d" % (1 - cur)])
            cur = 1 - cur
        for (R, I, src, rs) in ((self.AR2, self.AI2, PW[:, 15], "PW15"), (self.ANR, self.ANI, sq[:, cur], "sq%d" % cur)):
            nm = "AC"
            P.op("dve", lambda e: e.tensor_copy(R[:, 0, :], src[:, 0, :]), reads=[rs], writes=[nm])
            P.op("dve", lambda e: e.tensor_copy(R[:, 1, :], src[:, 0, :]), reads=[rs], writes=[nm])
            P.op("dve", lambda e: e.tensor_scalar(I[:, 0, :], src[:, 1, :], -1.0, None, ALU.mult), reads=[rs], writes=[nm])
            P.op("dve", lambda e: e.tensor_copy(I[:, 1, :], src[:, 1, :]), reads=[rs], writes=[nm])
        P.barrier()
        es.close()
    def setup_b(self):
        nc, P, d = self.nc, self.P, self.d
        es2 = ExitStack()
        CT = es2.enter_context(nc.sbuf_tensor("CTb", [128, 128, 128], BF16))
        DT = es2.enter_context(nc.sbuf_tensor("DTb", [128, 64, 128], BF16))
        XC = es2.enter_context(nc.sbuf_tensor("XC", [128, 128, 128], BF16))
        XG = XC
        BT = XC
        GB = es2.enter_context(nc.sbuf_tensor("GB", [128, 128, 128], BF16))
        GT = es2.enter_context(nc.sbuf_tensor("GT", [128, 128, 128], BF16))
        tmpd = es2.enter_context(nc.sbuf_tensor("tmpd", [128, 4, 128], F32))
        tmpe = es2.enter_context(nc.sbuf_tensor("tmpe", [128, 4, 128], F32))
        pT = es2.enter_context(nc.psum_tensor("pT", [128, 4, 4, 128], F32))
        pD = es2.enter_context(nc.psum_tensor("pD", [128, 2, 2, 4, 128], F32))
        idb = self.c["idb"]
        k = 0
        for f, (srcT, sn_, dstT, dn_) in enumerate(((BT, "XC", GB, "GB"), (XC, "XC", CT, "CT"), (XG, "XC", GT, "GT"))):
            src = d["SC"][f].rearrange("q x c s -> (x c) q s")
            for h in range(4):
                P.dma("pool", srcT[:, 32 * h:32 * h + 32, :], src[:, 32 * h:32 * h + 32, :], reads=["SC%d" % f], writes=[sn_ + str(h)])
            for q4 in range(32):
                slot = k % 4
                k += 1
                for u in range(4):
                    q = q4 * 4 + u
                    P.op("pe", lambda e: e.matmul(pT[:, slot, u, :], lhsT=srcT[:, q, :], rhs=idb[:], start=True, stop=True), reads=[sn_ + str(q // 32), "idb"], writes=["pT%d" % slot])
                P.op("act", lambda e: e.activation(dstT[:, q4 * 4:q4 * 4 + 4, :], pT[:, slot], AF.Copy), reads=["pT%d" % slot], writes=[dn_])
        self._b2 = lambda: self._setup_b2(GB, GT, CT, DT, tmpd, tmpe, pD)
        return es2

    def _setup_b2(self, GB, GT, CT, DT, tmpd, tmpe, pD):
        nc, P, d = self.nc, self.P, self.d
        ML, MU, idf, dcol = self.c["ML"], self.c["MU"], self.c["idf"], self.c["dcol"]
        MLb = ML[:].unsqueeze(1).to_broadcast([128, 4, 128]); MUb = MU[:].unsqueeze(1).to_broadcast([128, 4, 128])
        idb4 = idf[:].unsqueeze(1).to_broadcast([128, 4, 128])
        for g4 in range(16):
            s = g4 % 2
            for u in range(4):
                g = g4 * 4 + u
                P.op("pe", lambda e: e.matmul(pD[:, s, 0, u, :], lhsT=GB[:, g, :], rhs=GT[:, g, :], start=True, stop=True), reads=["GB", "GT"], writes=["pDf%d" % s])
            for u in range(4):
                g = g4 * 4 + u
                P.op("pe", lambda e: e.matmul(pD[:, s, 1, u, :], lhsT=GB[:, 64 + g, :], rhs=GT[:, 64 + g, :], start=True, stop=True), reads=["GB", "GT"], writes=["pDb%d" % s])
            P.op("act", lambda e: e.activation(tmpd[:], pD[:, s, 0], AF.Copy), reads=["pDf%d" % s], writes=["tmpd"])
            P.op("act", lambda e: e.activation(tmpe[:], pD[:, s, 1], AF.Copy), reads=["pDb%d" % s], writes=["tmpe"])
            P.op("pool", lambda e: e.tensor_tensor(tmpd[:], tmpd[:], MLb, ALU.mult), reads=["tmpd", "ML"], writes=["tmpd"])
            P.op("pool", lambda e: e.tensor_tensor(tmpe[:], tmpe[:], MUb, ALU.mult), reads=["tmpe", "MU"], writes=["tmpe"])
            P.op("pool", lambda e: e.tensor_tensor(tmpd[:], tmpd[:], tmpe[:], ALU.add), reads=["tmpd", "tmpe"], writes=["tmpd"])
            dcb = dcol[:, g4 * 4:g4 * 4 + 4].unsqueeze(2).to_broadcast([128, 4, 128])
            P.op("pool", lambda e: e.tensor_tensor(tmpe[:], idb4, dcb, ALU.mult), reads=["tmpd", "idf", "dcol"], writes=["tmpe"])
            P.op("pool", lambda e: e.tensor_tensor(DT[:, g4 * 4:g4 * 4 + 4, :], tmpe[:], tmpd[:], ALU.add), reads=["tmpd", "tmpe"], writes=["DT"])
        P.fence("pool", ["DT"]); P.fence("act", ["CT"])
        P.dma("pool", d["CTd"], CT[:], reads=["CT"], writes=["CTd"])
        P.dma("pool", d["DTd"], DT[:], reads=["DT"], writes=["DTd"])

    def scan_steps(self, SG, ZG, n, store):
        P = self.P
        tA, tB = self.sc_tA, self.sc_tB
        for i in range(n):
            a = i if store else i % 2
            b = i + 1 if store else (i + 1) % 2
            S = SG[:, a]
            Ssw = bass.AP(SG[:].tensor, SG[:, a, 1, :].offset, [list(SG[:].ap[0]), [-64, 2], [1, 64]])
            P.op("dve", lambda e: e.tensor_tensor(tA[:], self.AR2[:], S, ALU.mult), reads=["SGs%d" % a, "AC"], writes=["sc_tA"])
            P.op("dve", lambda e: e.tensor_tensor(tB[:], self.AI2[:], Ssw, ALU.mult), reads=["SGs%d" % a, "AC"], writes=["sc_tB"])
            P.op("dve", lambda e: e.tensor_tensor(tA[:], tA[:], tB[:], ALU.add), reads=["sc_tA", "sc_tB"], writes=["sc_tA"])
            P.op("dve", lambda e: e.tensor_tensor(SG[:, b], tA[:], ZG[:, i], ALU.add), reads=["sc_tA", "ZG"], writes=["SGs%d" % b])
        return (n if store else n % 2)


    def phase_z(self):
        nc, P, d, c = self.nc, self.P, self.d, self.c
        NS, NSC, NSB, NBLK = self.NS, self.NSC, self.NSB, self.NBLK
        idb, Jb = c["idb"], c["Jb"]
        es = ExitStack()

        def sb(name, shape, dt):
            return es.enter_context(nc.sbuf_tensor(name, shape, dt))

        U = sb("U5", [NSB, NBLK, 64, 128], BF16); Uc = sb("Uc5", [NSC, 64, 128], BF16)
        BT = sb("BT5", [128, 128, 128], BF16)
        UT = sb("UT", [128, 64, NS], BF16); UTr = sb("UTr", [128, 64, NS], BF16)
        UcT = sb("UcT", [128, 64, NSC], BF16); UcTr = sb("UcTr", [128, 64, NSC], BF16)
        ZR = sb("ZR", [128, 4, 8, 128], F32)
        pU = es.enter_context(nc.psum_tensor("pU", [128, 4, 4, 128], F32))
        P.dma("sp", U[:], d["U_d"], reads=["U_d"], writes=["U"])
        P.dma("sp", Uc[:], d["Uc_d"], reads=["Uc_d"], writes=["Uc"])
        srcb = d["SC"][0].rearrange("q x c s -> (x c) q s")
        stgz = sb("stgz", [128, 2, 16, 128], F32)
        for h in range(8):
            zs = h % 2
            P.dma("sp", stgz[:, zs], srcb[:, 16 * h:16 * h + 16, :], reads=["SC0"], writes=["stgz%d" % zs])
            if h % 2 == 0:
                P.op("act", lambda e: e.activation(BT[:, 16 * h:16 * h + 16, :], stgz[:, zs], AF.Copy), reads=["stgz%d" % zs], writes=["BT"])
            else:
                P.op("dve", lambda e: e.tensor_copy(BT[:, 16 * h:16 * h + 16, :], stgz[:, zs]), reads=["stgz%d" % zs], writes=["BT"])
        kslot = [0]

        def transposes(src_fn, nrows, nblk, dstT, dstTr, sname, dname):
            Isub = idb[0:nrows, 0:nrows]; Jsub = Jb[0:nrows, 128 - nrows:128]
            for blk in range(nblk):
                for g4 in range(16):
                    for (rhs, dst, col0, tag) in ((Isub, dstT, blk * nrows, "n"), (Jsub, dstTr, (nblk - 1 - blk) * nrows, "r")):
                        slot = kslot[0] % 4; kslot[0] += 1
                        for u in range(4):
                            g = g4 * 4 + u
                            P.op("pe", lambda e: e.matmul(pU[:, slot, u, 0:nrows], lhsT=src_fn(blk, g), rhs=rhs, start=True, stop=True),
                                 reads=[sname, "idb", "Jb"], writes=["pU%d" % slot])
                        if kslot[0] % 2 == 0:
                            P.op("act", lambda e: e.activation(dst[:, g4 * 4:g4 * 4 + 4, col0:col0 + nrows], pU[:, slot, :, 0:nrows], AF.Copy),
                                 reads=["pU%d" % slot], writes=[dname + tag + "_%d" % g4])
                        else:
                            P.op("dve", lambda e: e.tensor_copy(dst[:, g4 * 4:g4 * 4 + 4, col0:col0 + nrows], pU[:, slot, :, 0:nrows]),
                                 reads=["pU%d" % slot], writes=[dname + tag + "_%d" % g4])

        transposes(lambda blk, g: U[0:NSB, blk, g, :], NSB, NBLK, UT, UTr, "U", "UT")
        transposes(lambda blk, g: Uc[0:NSC, g, :], NSC, 1, UcT, UcTr, "Uc", "UcT")

        def zrows(T, Tr, nrows, nblk, Zd, tname, zname):
            for dd in range(2):
                src = T if dd == 0 else Tr
                for blk in range(nblk):
                    for g4 in range(16):
                        slot = kslot[0] % 4; kslot[0] += 1
                        for u in range(4):
                            g = g4 * 4 + u
                            P.op("pe", lambda e: e.matmul(pU[0:nrows, slot, u, :], lhsT=src[:, g, blk * nrows:(blk + 1) * nrows],
                                                          rhs=BT[:, dd * 64 + g, :], start=True, stop=True),
                                 reads=[tname + ("n" if dd == 0 else "r") + "_%d" % g4, "BT"], writes=["pU%d" % slot])
                        zb = (g4 // 2) % 4
                        P.op("act", lambda e: e.activation(ZR[0:nrows, zb, (g4 % 2) * 4:(g4 % 2) * 4 + 4, :], pU[0:nrows, slot], AF.Copy),
                             reads=["pU%d" % slot], writes=["ZR%d" % zb])
                        if g4 % 2 == 1:
                            gg = (g4 // 2) * 8
                            P.fence("act", ["ZR%d" % zb])
                            P.dma("sp", Zd[dd, blk * nrows:(blk + 1) * nrows, gg:gg + 8], ZR[0:nrows, zb], reads=["ZR%d" % zb], writes=[zname])

        zrows(UcT, UcTr, NSC, 1, d["Zc"], "UcT", "Zc")
        zrows(UT, UTr, NSB, NBLK, d["Z"], "UT", "Z")
        utn_all = ["UTn_%d" % i_ for i_ in range(16)]
        P.fence("act", utn_all); P.fence("dve", utn_all)
        P.dma("sp", d["UT_d"], UT[:], reads=utn_all, writes=["UT_d"])
        P.barrier()
        es.close()

    def phase_scan(self, flags, pre_emit=None):
        nc, P, d = self.nc, self.P, self.d
        NS, NSC = self.NS, self.NSC
        es = ExitStack()

        def sb(name, shape, dt):
            return es.enter_context(nc.sbuf_tensor(name, shape, dt))

        SGc = sb("SGc", [128, 2, 2, 64], F32); SGp = sb("SGp", [128, 2, 2, 64], F32)
        CH = min(16, NS)
        ZG = sb("ZG", [128, 2, CH, 2, 64], F32)
        SG = sb("SG", [128, 2, CH + 1, 2, 64], F32)
        self.sc_tA = sb("sc_tA", [128, 2, 64], F32); self.sc_tB = sb("sc_tB", [128, 2, 64], F32)
        EG = sb("EG", [128, 4, 128], F32); Es = sb("Es", [128, 2, 64], F32); acc = sb("acc", [128, 2, 64], F32)
        cand = sb("cand", [128, 2, 64], F32)
        zgk = [0]
        es_pre = pre_emit() if pre_emit is not None else None

        def load_zg(Zd, n0, n, zname):
            b = zgk[0] % 2; zgk[0] += 1
            for dd in range(2):
                P.dma("sp", ZG[dd * 64:(dd + 1) * 64, b, 0:n].rearrange("q n r p -> q n (r p)"),
                      Zd[dd, n0:n0 + n].rearrange("n g s -> g n s"), reads=[zname], writes=["ZG%d" % b])
            return b

        def steps(SGt, b, n, store, pre="SGs"):
            tA, tB = self.sc_tA, self.sc_tB
            fz = P._fz

            def spacer():
                P.op("dve", lambda e: e.memset(fz[:, 4:5], 0.0), writes=["fz_sp"], nosame=True)

            P.op("dve", lambda e: e.memset(fz[:, 5:6], 0.0), reads=[pre + str(i) for i in range(CH + 1)] + ["sc_tA", "sc_tB"], writes=["fz_sp2"])
            spacer()
            for i in range(n):
                a = i if store else i % 2
                bb = i + 1 if store else (i + 1) % 2
                S = SGt[:, a]
                Ssw = bass.AP(SGt[:].tensor, SGt[:, a, 1, :].offset, [list(SGt[:].ap[0]), [-64, 2], [1, 64]])
                P.op("dve", lambda e: e.tensor_tensor(tB[:], self.AI2[:], Ssw, ALU.mult), reads=[pre + str(a), "AC"], writes=["sc_tB"], nosame=True)
                P.op("dve", lambda e: e.tensor_tensor(tA[:], self.AR2[:], S, ALU.mult), reads=[pre + str(a), "AC"], writes=["sc_tA"], nosame=True)
                P.op("dve", lambda e: e.tensor_tensor(tB[:], tB[:], ZG[:, b, i], ALU.add), reads=["sc_tB", "ZG%d" % b], writes=["sc_tB"], nosame=True)
                spacer()
                P.op("dve", lambda e: e.tensor_tensor(SGt[:, bb], tA[:], tB[:], ALU.add), reads=["sc_tA", "sc_tB"], writes=[pre + str(bb)], nosame=True)
                spacer()
            P.op("dve", lambda e: e.memset(fz[:, 5:6], 0.0), reads=[pre + str(i) for i in range(CH + 1)] + ["sc_tA", "sc_tB", "fz_sp"], writes=["fz_sp2"] + [pre + str(i) for i in range(CH + 1)])

        P.op("dve", lambda e: e.memset(SGc[:, 0], 0.0), writes=["SGs0"])
        for ch in range(NSC // CH):
            b = load_zg(d["Zc"], ch * CH, CH, "Zc")
            steps(SGc, b, CH, False)
        P.op("dve", lambda e: e.tensor_copy(acc[:], SGc[:, 0]), reads=["SGs0"], writes=["acc"])
        P.op("dve", lambda e: e.memset(SGp[:, 0], 0.0), writes=["SGs0"], reads=["SGs0", "SGs1"])
        for ch in range(NS // CH):
            b = load_zg(d["Z"], ch * CH, CH, "Z")
            steps(SGp, b, CH, False)
        P.fence("dve", ["SGs0"])
        t = P.dma("sp", d["Ein"], SGp[:, 0].rearrange("q r p -> q (r p)"), reads=["SGs0"], writes=["Ein"])
        P.collective("AllGather", [[0, 1, 2, 3], [4, 5, 6, 7]], d["Ein"], d["Eout"], ["Ein"], ["Eout"], t)
        if es_pre is not None:
            self._b2()
        P.dma("sp", EG[:], d["Eout"].rearrange("(r q) s -> q r s", q=128), reads=["Eout"], writes=["EG"])
        tA, tB = self.sc_tA, self.sc_tB
        for jj in range(3):
            P.op("dve", lambda e: e.tensor_copy(Es[0:64], EG[0:64, jj, :].rearrange("q (r p) -> q r p", r=2)), reads=["EG"], writes=["Es"])
            P.op("dve", lambda e: e.tensor_copy(Es[64:128], EG[64:128, 3 - jj, :].rearrange("q (r p) -> q r p", r=2)), reads=["EG"], writes=["Es"])
            accsw = bass.AP(acc[:].tensor, acc[:, 1, :].offset, [list(acc[:].ap[0]), [-64, 2], [1, 64]])
            P.op("dve", lambda e: e.tensor_tensor(tA[:], self.ANR[:], acc[:], ALU.mult), reads=["acc", "AC"], writes=["sc_tA"])
            P.op("dve", lambda e: e.tensor_tensor(tB[:], self.ANI[:], accsw, ALU.mult), reads=["acc", "AC"], writes=["sc_tB"])
            P.op("dve", lambda e: e.tensor_tensor(tA[:], tA[:], tB[:], ALU.add), reads=["sc_tA", "sc_tB"], writes=["sc_tA"])
            P.op("dve", lambda e: e.tensor_tensor(cand[:], tA[:], Es[:], ALU.add), reads=["sc_tA", "Es"], writes=["cand"])
            P.op("dve", lambda e: e.tensor_tensor(cand[:], cand[:], acc[:], ALU.subtract), reads=["cand", "acc"], writes=["cand"])
            P.op("dve", lambda e: e.scalar_tensor_tensor(acc[:], cand[:], flags[:, jj:jj + 1], acc[:], ALU.mult, ALU.add),
                 reads=["cand", "acc", "flags"], writes=["acc"])
        names = [["SGA%d" % i for i in range(CH + 1)], ["SGB%d" % i for i in range(CH + 1)]]
        P.op("dve", lambda e: e.tensor_copy(SG[:, 0, 0], acc[:]), reads=["acc"], writes=[names[0][0]])
        bnext = load_zg(d["Z"], 0, CH, "Z")
        for ch in range(NS // CH):
            b = bnext
            kb_ = ch % 2
            pre = "SGA" if kb_ == 0 else "SGB"
            steps(SG[:, kb_], b, CH, True, pre)
            if ch + 1 < NS // CH:
                bnext = load_zg(d["Z"], (ch + 1) * CH, CH, "Z")
            allr = names[kb_]
            P.fence("dve", allr)
            for dd in range(2):
                P.dma("sp", d["SD"][dd, ch * CH:(ch + 1) * CH].rearrange("n g s -> g n s"),
                      SG[dd * 64:(dd + 1) * 64, kb_, 0:CH].rearrange("q n r p -> q n (r p)"), reads=allr, writes=["SD"])
            if ch + 1 < NS // CH:
                P.op("dve", lambda e: e.tensor_copy(SG[:, 1 - kb_, 0], SG[:, kb_, CH]), reads=[names[kb_][CH]], writes=[names[1 - kb_][0]])
        P.barrier()
        if es_pre is not None:
            es_pre.close()
        es.close()

    def phase_read(self):
        nc, P, d, c = self.nc, self.P, self.d, self.c
        NS, NSB, NBLK = self.NS, self.NSB, self.NBLK
        idb, Jb = c["idb"], c["Jb"]
        es = ExitStack()

        def sb(name, shape, dt):
            return es.enter_context(nc.sbuf_tensor(name, shape, dt))

        UT = sb("UT7", [128, 64, NS], BF16)
        ST = sb("ST", [128, 2, 64, NS], BF16)
        CT = sb("CT7", [128, 128, 128], BF16); DT = sb("DT7", [128, 64, 128], BF16)
        SRb = sb("SRb", [128, 2, 32, 128], BF16)
        YG = sb("YG", [NSB, NBLK, 8, 1024], BF16)
        pU = es.enter_context(nc.psum_tensor("pU7", [128, 4, 4, 128], F32))
        pY = es.enter_context(nc.psum_tensor("pY", [128, 2, 4, 128], F32))
        P.dma("sp", UT[:], d["UT_d"], reads=["UT_d"], writes=["UT7"])
        P.dma("sp", CT[:], d["CTd"], reads=["CTd"], writes=["CT7"])
        P.dma("sp", DT[:], d["DTd"], reads=["DTd"], writes=["DT7"])
        kslot = 0
        kb = 0
        for dd in range(2):
            for blk in range(NBLK):
                nat = blk if dd == 0 else NBLK - 1 - blk
                rhs = idb[0:NSB, 0:NSB] if dd == 0 else Jb[0:NSB, 128 - NSB:128]
                for gh in range(2):
                    bsel = kb % 2; kb += 1
                    P.dma("pool", SRb[0:NSB, bsel], d["SD"][dd, blk * NSB:(blk + 1) * NSB, gh * 32:(gh + 1) * 32], reads=["SD"], writes=["SRb%d" % bsel])
                    for g4 in range(8):
                        slot = kslot % 4; kslot += 1
                        for u in range(4):
                            P.op("pe", lambda e: e.matmul(pU[:, slot, u, 0:NSB], lhsT=SRb[0:NSB, bsel, g4 * 4 + u, :], rhs=rhs, start=True, stop=True),
                                 reads=["SRb%d" % bsel, "idb", "Jb"], writes=["pU%d" % slot])
                        G0 = gh * 32 + g4 * 4
                        if g4 % 2 == 0:
                            P.op("act", lambda e: e.activation(ST[:, dd, G0:G0 + 4, nat * NSB:(nat + 1) * NSB], pU[:, slot, :, 0:NSB], AF.Copy),
                                 reads=["pU%d" % slot], writes=["ST%d_%d" % (dd, G0 // 4)])
                        else:
                            P.op("dve", lambda e: e.tensor_copy(ST[:, dd, G0:G0 + 4, nat * NSB:(nat + 1) * NSB], pU[:, slot, :, 0:NSB]),
                                 reads=["pU%d" % slot], writes=["ST%d_%d" % (dd, G0 // 4)])
        for blk in range(NBLK):
            ns = slice(blk * NSB, (blk + 1) * NSB)
            for g4 in range(16):
                slot = g4 % 2
                for u in range(4):
                    g = g4 * 4 + u
                    P.op("pe", lambda e: e.matmul(pY[0:NSB, slot, u, :], lhsT=ST[:, 0, g, ns], rhs=CT[:, g, :], start=True, stop=False),
                         reads=["ST0_%d" % g4, "CT7"], writes=["pY%d" % slot])
                    P.op("pe", lambda e: e.matmul(pY[0:NSB, slot, u, :], lhsT=ST[:, 1, g, ns], rhs=CT[:, 64 + g, :], start=False, stop=False),
                         reads=["ST1_%d" % g4, "CT7"], writes=["pY%d" % slot])
                    P.op("pe", lambda e: e.matmul(pY[0:NSB, slot, u, :], lhsT=UT[:, g, ns], rhs=DT[:, g, :], start=False, stop=True),
                         reads=["UT7", "DT7"], writes=["pY%d" % slot])
                outv = YG[0:NSB, blk, :, g4 * 64:(g4 + 1) * 64].rearrange("n i (u c) -> n u i c", u=4)
                inv = pY[0:NSB, slot].rearrange("n u (i c) -> n u i c", i=8)
                P.op("act", lambda e: e.activation(outv, inv, AF.Gelu), reads=["pY%d" % slot], writes=["YG"])
        P.fence("act", ["YG"])
        for blk in range(NBLK):
            P.dma("sp", d["YG_d"][blk * NSB * 8:(blk + 1) * NSB * 8].rearrange("(n i) c -> n i c", i=8), YG[0:NSB, blk], reads=["YG"], writes=["YG_d"])
        P.barrier()
        es.close()


D = 2048
LN_EPS = 1e-6
ALPHA = 2.0 ** 0.25


def bcast_rows(ap_flat, n):
    return bass.AP(ap_flat.tensor, ap_flat.offset, [[0, 128], [1, n]])


class WL:
    def __init__(self, P, stg, nm):
        self.P, self.stg, self.nm, self.k = P, stg, nm, 0

    def dma(self, src, pat=None, **kw):
        s = self.k % 3; self.k += 1
        n = 1
        for dmn in src.shape[1:]:
            n *= dmn
        view = self.stg[:, s, 0:n]
        if pat is not None:
            view = view.rearrange(pat, **kw)
        self.P.dma("sp", view, src, writes=["%s%d" % (self.nm, s)])
        return (s, view)

    def cast(self, eng, h, dst, dst_res):
        s, view = h
        if eng == "act":
            self.P.op("act", lambda e: e.activation(dst, view, AF.Copy), reads=["%s%d" % (self.nm, s)], writes=dst_res)
        else:
            self.P.op(eng, lambda e: e.tensor_copy(dst, view), reads=["%s%d" % (self.nm, s)], writes=dst_res)


def build(NT, NCTX=256, debug=False, stop_after=None):
    nc = bass.Bass("TRN2", target_bir_lowering=False)
    NS, NSC = NT // 8, NCTX // 8
    NSB = min(128, NS); NBLK = NS // NSB
    NTT = NT // 128
    TB = min(512, NT)
    NTB = NT // TB

    def din(name, shape, dt=F32):
        return nc.dram_tensor(name, shape, dt, kind="ExternalInput").ap()

    def dsc(name, shape, dt=F32):
        return nc.dram_tensor(name, shape, dt).ap()

    x = din("x", [NT, D]); ctx = din("ctx", [NCTX, D]); cT = din("cT", [128, 16, 2])
    w_ada = din("w_ada", [D, 1536]); b_adaT = din("b_adaT", [128, 12]); w_in = din("w_in", [40, 128, 16, 128])
    sgu_g = din("sgu_g", [1, 1024]); sgu_b = din("sgu_b", [1, 1024])
    w_sp = din("w_sp", [8, 128, 128]); b_sp = din("b_sp", [1, 1024])
    w_glu = din("w_glu", [1024, 1024]); b_gluT = din("b_gluT", [128, 8]); w_out = din("w_out", [D, D])
    ln_g = din("ln_g", [1, D]); ln_b = din("ln_b", [1, D])
    dr = {}
    for nm, shp in (("lam_re", [128, 64]), ("lam_im", [128, 64]), ("log_step", [128, 1]), ("b_re", [128, 64, 16]),
                    ("b_im", [128, 64, 16]), ("c_re", [128, 16, 64]), ("c_im", [128, 16, 64])):
        dr[nm] = din(nm, shp)
    cd = {}
    for nm, shp, dt in (("idb", [128, 128], BF16), ("Jb", [128, 128], BF16), ("idf", [128, 128], F32), ("ML", [128, 128], F32), ("MU", [128, 128], F32),
                        ("dcol", [128, 64], F32), ("flags", [128, 3], F32)):
        cd[nm] = din("c_" + nm, shp, dt)
    y = nc.dram_tensor("y", [NT, D], F32, kind="ExternalOutput").ap()
    dr["SC"] = [dsc("SC%d" % f, [128, 8, 16, 128]) for f in range(3)]
    dr["BTd"] = dsc("BTd", [128, 128, 128], BF16); dr["CTd"] = dsc("CTd", [128, 128, 128], BF16); dr["DTd"] = dsc("DTd", [128, 64, 128], BF16)
    dr["Z"] = dsc("Zs", [2, NS, 64, 128]); dr["Zc"] = dsc("Zcs", [2, NSC, 64, 128]); dr["SD"] = dsc("SDs", [2, NS, 64, 128])
    dr["Ein"] = dsc("Ein", [128, 128]); dr["Eout"] = dsc("Eout", [512, 128])
    dr["U_d"] = dsc("U_d", [NSB, NBLK, 64, 128], BF16); dr["Uc_d"] = dsc("Uc_d", [NSC, 64, 128], BF16)
    dr["UT_d"] = dsc("UT_d", [128, 64, NS], BF16); dr["YG_d"] = dsc("YG_d", [NT, 1024], BF16)
    gsc = dsc("gsc", [16, 128]); mIn = dsc("mIn", [128, 24]); mOut = dsc("mOut", [512, 24]); YA_d = dsc("YA_d", [8, 128, NT], BF16); ZB_d = dsc("ZB_d", [8, 128, NT], BF16)
    dbg = {}
    if debug:
        for nm, shp, dt in (("xmT", [128, 16, NT], BF16), ("YA", [8, 128, NT], BF16), ("U", [NSB, NBLK, 64, 128], BF16), ("YG", [NT, 1024], BF16),
                            ("YB", [128, 8, NT], BF16), ("modT", [128, 48, 2], F32)):
            dbg[nm] = nc.dram_tensor("dbg_" + nm, shp, dt, kind="ExternalOutput").ap()

    P = Prog(nc, n_dma_sems=12)
    P.op("dve", lambda e: e.memset(P._fz[:], 0.0), writes=["fence_z"])
    keep = ExitStack()

    def kb(name, shape, dt):
        return keep.enter_context(nc.sbuf_tensor(name, shape, dt))

    consts = {}
    for nm in cd:
        consts[nm] = kb("k_" + nm, list(cd[nm].shape), cd[nm].dtype)
        P.dma("sp", consts[nm][:], cd[nm], writes=[nm])
    idb, idf = consts["idb"], consts["idf"]
    modT = kb("modT", [128, 48, 2], F32)
    S1 = kb("S1", [128, 16, 2], F32)
    bada = kb("bada", [128, 12], F32); bglu = kb("bglu", [128, 8], F32)
    P.dma("sp", bada[:], b_adaT, writes=["bada"]); P.dma("sp", bglu[:], b_gluT, writes=["bglu"])
    s5 = S5(nc, P, NS, NSC, dr, consts)

    s5.AR2 = kb("AR2", [128, 2, 64], F32); s5.AI2 = kb("AI2", [128, 2, 64], F32)
    s5.ANR = kb("ANR", [128, 2, 64], F32); s5.ANI = kb("ANI", [128, 2, 64], F32)
    es = ExitStack()
    cTt = es.enter_context(nc.sbuf_tensor("cTt", [128, 16, 2], F32))
    Wt = es.enter_context(nc.sbuf_tensor("Wt", [128, 2, 16, 384], F32))
    gt = es.enter_context(nc.sbuf_tensor("gt", [128, 16], F32)); gt2 = es.enter_context(nc.sbuf_tensor("gt2", [16, 128], F32))
    modP = es.enter_context(nc.sbuf_tensor("modP", [128, 12, 2], F32))
    P.dma("sp", cTt[:], cT, writes=["cTt"])
    wsrc = w_ada.rearrange("(kt p) n -> p kt n", p=128)
    for cb in range(2):
        P.dma("sp", Wt[:, cb], wsrc[:, :, cb * 384:(cb + 1) * 384], writes=["Wt%d" % cb])
    s5.setup(keep)
    if stop_after == "p1":
        P.barrier()
        es.close(); keep.close(); P.close()
        return nc

    P.op("act", lambda e: e.activation(cTt[:], cTt[:], AF.Silu), reads=["cTt"], writes=["cTt"])
    pM = es.enter_context(nc.psum_tensor("pM", [128, 2, 512], F32))
    for cb in range(4):
        wb = cb % 2
        if cb >= 2:
            P.dma("sp", Wt[:, wb], wsrc[:, :, cb * 384:(cb + 1) * 384], writes=["Wt%d" % wb])
        for c3 in range(3):
            ct = cb * 3 + c3
            slot = ct % 2
            for kt in range(16):
                P.op("pe", lambda e: e.matmul(pM[:, slot, 0:2], lhsT=Wt[:, wb, kt, c3 * 128:(c3 + 1) * 128], rhs=cTt[:, kt, :],
                                              start=(kt == 0), stop=(kt == 15)), reads=["Wt%d" % wb, "cTt"], writes=["pM%d" % slot])
            P.op("act", lambda e: e.activation(modP[:, ct, :], pM[:, slot, 0:2], AF.Identity, bias=bada[:, ct:ct + 1]),
                 reads=["pM%d" % slot, "bada"], writes=["modP"])
    P.fence("act", ["modP"])
    tg = P.dma("sp", mIn, modP[:].rearrange("p c t -> p (c t)"), reads=["modP"], writes=["mIn"])
    P.collective("AllGather", [[0, 1, 2, 3], [4, 5, 6, 7]], mIn, mOut, ["mIn"], ["mOut"], tg)
    P.dma("sp", modT[:].rearrange("p (r c) t -> p r (c t)", r=4), mOut.rearrange("(r p) f -> p r f", p=128), reads=["mOut"], writes=["modT"])
    P.op("dve", lambda e: e.tensor_scalar(S1[:], modT[:, 16:32, :], 1.0, None, ALU.add), reads=["modT"], writes=["S1"])
    P.op("dve", lambda e: e.tensor_copy(gt[:], modT[:, 32:48, 0]), reads=["modT"], writes=["gt"])
    P.op("pe", lambda e: e.transpose(pM[0:16, 0, 0:128], gt[:], idf[:]), reads=["gt", "idf", "pM0"], writes=["pM0"])
    P.op("act", lambda e: e.activation(gt2[:], pM[0:16, 0, 0:128], AF.Copy), reads=["pM0"], writes=["gt2"])
    P.fence("act", ["gt2"])
    P.dma("sp", gsc, gt2[:], reads=["gt2"], writes=["gsc"])
    if debug:
        P.fence("act", ["modT"])
        P.dma("sp", dbg["modT"], modT[:], reads=["modT"])
    P.barrier()
    es.close()
    if stop_after == 'p2':
        P.barrier()
        keep.close(); P.close()
        return nc

    esA = ExitStack()
    xmT = esA.enter_context(nc.sbuf_tensor("xmT", [128, 16, NT], BF16))
    xcT = esA.enter_context(nc.sbuf_tensor("xcT", [128, 16, NCTX], BF16))
    es = ExitStack()
    xt = es.enter_context(nc.sbuf_tensor("xt", [128, 2, D], F32))
    st6 = es.enter_context(nc.sbuf_tensor("st6", [128, 4, 6], F32)); mv = es.enter_context(nc.sbuf_tensor("mv", [128, 2], F32))
    rstd = es.enter_context(nc.sbuf_tensor("rstd", [128, 1], F32))
    pX = es.enter_context(nc.psum_tensor("pX", [128, 4, 4, 128], F32))

    def ln_stats(src, nm):
        for q in range(4):
            P.op("dve", lambda e: e.bn_stats(st6[:, q, :], src[:, q * 512:(q + 1) * 512]), reads=[nm], writes=["st6_%d" % q])
        P.op("dve", lambda e: e.bn_aggr(mv[:], st6[:]), reads=["st6_%d" % q_ for q_ in range(4)], writes=["mv"])
        P.op("dve", lambda e: e.tensor_scalar(rstd[:], mv[:, 1:2], LN_EPS, None, ALU.add), reads=["mv"], writes=["rstd"])
        P.op("act", lambda e: e.activation(rstd[:], rstd[:], AF.Sqrt), reads=["rstd"], writes=["rstd"])
        P.op("dve", lambda e: e.reciprocal(rstd[:], rstd[:]), reads=["rstd"], writes=["rstd"])

    ks = 0
    for t in range(NTT + NCTX // 128):
        isx = t < NTT
        src = x[t * 128:(t + 1) * 128] if isx else ctx[(t - NTT) * 128:(t - NTT + 1) * 128]
        dstT, tt, col = (xmT, t, 0) if isx else (xcT, t - NTT, 1)
        b = t % 2
        P.dma("sp", xt[:, b], src, writes=["xt%d" % b])
        ln_stats(xt[:, b], "xt%d" % b)
        P.op("dve", lambda e: e.tensor_scalar(xt[:, b], xt[:, b], mv[:, 0:1], rstd[:, 0:1], ALU.subtract, ALU.mult), reads=["xt%d" % b, "mv", "rstd"], writes=["xt%d" % b])
        for k4 in range(4):
            slot = ks % 4; ks += 1
            for u in range(4):
                kt = k4 * 4 + u
                P.op("pe", lambda e: e.transpose(pX[:, slot, u, :], xt[:, b, kt * 128:(kt + 1) * 128], idf[:]), reads=["xt%d" % b, "idf"], writes=["pX%d" % slot])
            for u in range(4):
                kt = k4 * 4 + u
                P.op("act", lambda e: e.activation(dstT[:, kt, tt * 128:(tt + 1) * 128], pX[:, slot, u, :], AF.Identity,
                                                   scale=S1[:, kt, col:col + 1], bias=modT[:, kt, col:col + 1]),
                     reads=["pX%d" % slot, "S1", "modT"], writes=[("xmT_%d_%d" if isx else "xcT_%d_%d") % (tt, kt)])
    if debug:
        P.barrier()
        P.dma("sp", dbg["xmT"], xmT[:], reads=[])
    P.barrier()
    es.close()
    if stop_after == 'p3':
        P.barrier()
        esA.close()
        keep.close(); P.close()
        return nc

    es = ExitStack()
    BIGN = max(8 * NT, NBLK * 8192)
    BIG = es.enter_context(nc.sbuf_tensor("BIG", [128, BIGN], BF16))
    MXF = BIG[:, 0:8 * NT].rearrange("p (c t) -> p c t", c=8)
    Wv = es.enter_context(nc.sbuf_tensor("Wv", [128, 16, 1024], BF16))
    Wu = es.enter_context(nc.sbuf_tensor("Wu", [128, 3, 16, 128], BF16))
    stg4 = es.enter_context(nc.sbuf_tensor("stg4", [128, 3, 2048], F32))
    wl = WL(P, stg4, "stg4_")
    Ucs = stg4[0:NSC].rearrange("p a b -> p (a b)").bitcast(BF16)[:, 0:8192].rearrange("p (g s) -> p g s", g=64)
    WsT = es.enter_context(nc.sbuf_tensor("WsT", [128, 8, 128], BF16))
    grow = es.enter_context(nc.sbuf_tensor("grow", [128, 1024], F32)); brow = es.enter_context(nc.sbuf_tensor("brow", [128, 1024], F32))
    bsrow = es.enter_context(nc.sbuf_tensor("bsrow", [128, 8, 128], F32)); mxt = es.enter_context(nc.sbuf_tensor("mxt", [128, 8, 128], F32))
    vg = es.enter_context(nc.sbuf_tensor("vg", [128, 1024], F32)); vnb = es.enter_context(nc.sbuf_tensor("vnb", [128, 2, 1024], BF16))
    tmpu = es.enter_context(nc.sbuf_tensor("tmpu", [128, 2, TB], BF16))
    zbt = es.enter_context(nc.sbuf_tensor("zbt", [128, 2, TB], BF16))
    st2 = es.enter_context(nc.sbuf_tensor("st2", [128, 2, 6], F32)); mv2 = es.enter_context(nc.sbuf_tensor("mv2", [128, 2], F32))
    rs2 = es.enter_context(nc.sbuf_tensor("rs2", [128, 1], F32))
    pV = es.enter_context(nc.psum_tensor("pV", [128, 2, 2, 512], F32))
    pS = es.enter_context(nc.psum_tensor("pS", [128, 2, 4, 128], F32))
    pA = es.enter_context(nc.psum_tensor("pA", [128, 2, 512], F32))
    P.dma("sp", grow[:], bcast_rows(sgu_g, 1024), writes=["grow"]); P.dma("sp", brow[:], bcast_rows(sgu_b, 1024), writes=["brow"])
    P.dma("sp", bsrow[:].rearrange("p h q -> p (h q)"), bcast_rows(b_sp, 1024), writes=["bsrow"])
    Wsl = mxt[:].rearrange("p h q -> p (h q)").bitcast(BF16)[:, 0:1024].rearrange("p (h q) -> p h q", h=8)
    P.dma("pool", Wsl, w_sp.rearrange("h p q -> p h q"), writes=["mxt"])
    for h in range(8):
        P.op("pe", lambda e: e.matmul(pS[:, h // 4, h % 4, :], lhsT=Wsl[:, h, :], rhs=idb[:], start=True, stop=True), reads=["mxt", "idb"], writes=["pS%d" % (h // 4)])
    for hh in range(2):
        P.op("act", lambda e: e.activation(WsT[:, hh * 4:hh * 4 + 4, :], pS[:, hh], AF.Copy), reads=["pS%d" % hh], writes=["WsT"])
    for i8 in range(8):
        h_ = wl.dma(w_in[8 + i8], "p (k c) -> p k c", c=128)
        wl.cast("act" if i8 % 2 == 0 else "dve", h_, Wv[:, :, i8 * 128:(i8 + 1) * 128], ["Wv"])
    def p4a_proj(c_):
        vb = c_ % 2
        for half in range(2):
            for kt in range(16):
                P.op("pe", lambda e: e.matmul(pV[:, vb, half, :], lhsT=xmT[:, kt, c_ * 128:(c_ + 1) * 128], rhs=Wv[:, kt, half * 512:(half + 1) * 512],
                                              start=(kt == 0), stop=(kt == 15)), reads=["xmT", "Wv"], writes=["pV%d" % vb])
        P.op("act", lambda e: e.activation(vg[:].rearrange("p (h n) -> p h n", h=2), pV[:, vb], AF.Gelu), reads=["pV%d" % vb], writes=["vg"])
        for q in range(2):
            P.op("dve", lambda e: e.bn_stats(st2[:, q, :], vg[:, q * 512:(q + 1) * 512]), reads=["vg"], writes=["st2_%d" % q])
        P.op("dve", lambda e: e.bn_aggr(mv2[:], st2[:]), reads=["st2_0", "st2_1"], writes=["mv2"])
        P.op("dve", lambda e: e.tensor_scalar(rs2[:], mv2[:, 1:2], LN_EPS, None, ALU.add), reads=["mv2"], writes=["rs2"])
        P.op("act", lambda e: e.activation(rs2[:], rs2[:], AF.Sqrt), reads=["rs2"], writes=["rs2"])
        P.op("dve", lambda e: e.reciprocal(rs2[:], rs2[:]), reads=["rs2"], writes=["rs2"])
        P.op("dve", lambda e: e.tensor_scalar(vg[:], vg[:], mv2[:, 0:1], rs2[:, 0:1], ALU.subtract, ALU.mult), reads=["vg", "mv2", "rs2"], writes=["vg"])
        P.op("dve", lambda e: e.tensor_tensor(vg[:], vg[:], grow[:], ALU.mult), reads=["vg", "grow"], writes=["vg"])
        P.op("dve", lambda e: e.tensor_tensor(vnb[:, vb, :], vg[:], brow[:], ALU.add), reads=["vg", "brow"], writes=["vnb%d" % vb])

    def p4a_spatial(c_):
        for h in range(8):
            hs = "pS%d" % (h // 4)
            o = pS[:, h // 4, h % 4, :]
            P.op("pe", lambda e: e.matmul(o, lhsT=vnb[:, c_ % 2, h * 128:(h + 1) * 128], rhs=WsT[:, h, :], start=True, stop=True), reads=["vnb%d" % (c_ % 2), "WsT"], writes=[hs])
        P.op("act", lambda e: e.activation(mxt[:, 0:4, :], pS[:, 0], AF.Copy), reads=["pS0"], writes=["mxt"])
        P.op("act", lambda e: e.activation(mxt[:, 4:8, :], pS[:, 1], AF.Copy), reads=["pS1"], writes=["mxt"])
        P.op("dve", lambda e: e.tensor_tensor(MXF[:, :, c_ * 128:(c_ + 1) * 128], mxt[:], bsrow[:], ALU.add), reads=["mxt", "bsrow"], writes=["MXF"])

    for c_ in range(NTT + 1):
        if c_ < NTT:
            p4a_proj(c_)
        if c_ >= 1:
            p4a_spatial(c_ - 1)

    blocks = [(grp, ct) for grp in range(3) for ct in range(8)]
    grp_info = ((0, AF.Gelu), (16, AF.Silu), (32, AF.Silu))
    hnd = {}
    ak = 0
    for k in range(-2, len(blocks)):
        if 0 <= k + 2 < len(blocks):
            g2, c2 = blocks[k + 2]
            hnd[k + 2] = wl.dma(w_in[grp_info[g2][0] + c2], "p (k c) -> p k c", c=128)
        if 0 <= k + 1 < len(blocks):
            wl.cast("act", hnd[k + 1], Wu[:, (k + 1) % 3], ["Wu%d" % ((k + 1) % 3)])
        if k < 0:
            continue
        grp, ct = blocks[k]
        func = grp_info[grp][1]
        wb = k % 3
        zb_ = ct % 2
        for tb in range(NTB):
            slot = ak % 2; ak += 1
            ts = slice(tb * TB, (tb + 1) * TB)
            for kt in range(16):
                P.op("pe", lambda e: e.matmul(pA[:, slot, 0:TB], lhsT=Wu[:, wb, kt, :], rhs=xmT[:, kt, ts], start=(kt == 0), stop=(kt == 15)),
                     reads=["Wu%d" % wb, "xmT"], writes=["pA%d" % slot])
            if grp < 2:
                P.op("act", lambda e: e.activation(tmpu[:, slot, :], pA[:, slot, 0:TB], func), reads=["pA%d" % slot], writes=["tmpu%d" % slot])
                P.op("dve", lambda e: e.tensor_tensor(MXF[:, ct, ts], MXF[:, ct, ts], tmpu[:, slot, :], ALU.mult), reads=["MXF", "tmpu%d" % slot], writes=["MXF"])
            else:
                P.op("act", lambda e: e.activation(zbt[:, slot, :], pA[:, slot, 0:TB], func), reads=["pA%d" % slot], writes=["zbt%d" % slot])
                P.fence("act", ["zbt%d" % slot])
                P.dma("pool", ZB_d[ct][:, ts], zbt[:, slot, :], reads=["zbt%d" % slot], writes=["ZB_d"])
        if grp == 1 and ct == 7:
            P.fence("dve", ["MXF"])
            P.dma("pool", YA_d.rearrange("c p t -> p c t"), MXF, reads=["MXF"], writes=["YA_d"])
            if debug:
                P.dma("sp", dbg["YA"].rearrange("c p t -> p c t"), MXF, reads=["MXF"])
    Uv = BIG[0:NSB, 0:NBLK * 8192].rearrange("n (b g s) -> n b g s", b=NBLK, g=64)
    for i8 in range(8):
        h_ = wl.dma(w_in[24 + i8], "p (k c) -> p k c", c=128)
        wl.cast("act" if i8 % 2 == 0 else "dve", h_, Wv[:, :, i8 * 128:(i8 + 1) * 128], ["Wv"])
    vk = 0
    for (srcT, nrows, nblk, dstv, sname, dname) in ((xmT, NSB, NBLK, None, "xmT", "Uv"), (xcT, NSC, 1, None, "xcT", "Ucs")):
        for blk in range(nblk):
            for j in range(8):
                vb = vk % 2; vk += 1
                t0 = blk * nrows * 8 + j
                for half in range(2):
                    for kt in range(16):
                        P.op("pe", lambda e: e.matmul(pV[0:nrows, vb, half, :], lhsT=srcT[:, kt, t0:t0 + (nrows - 1) * 8 + 1:8], rhs=Wv[:, kt, half * 512:(half + 1) * 512],
                                                      start=(kt == 0), stop=(kt == 15)), reads=[sname, "Wv"], writes=["pV%d" % vb])
                if dname == "Uv":
                    outv = Uv[:, blk, :, j * 16:(j + 1) * 16]
                else:
                    outv = Ucs[:, :, j * 16:(j + 1) * 16]
                inv = pV[0:nrows, vb].rearrange("n h (g c) -> n (h g) c", c=16)
                if vk % 2 == 0:
                    P.op("act", lambda e: e.activation(outv, inv, AF.Copy), reads=["pV%d" % vb], writes=[dname, "MXF"] if dname == "Uv" else [dname, "stg4_0", "stg4_1", "stg4_2"])
                else:
                    P.op("dve", lambda e: e.tensor_copy(outv, inv), reads=["pV%d" % vb], writes=[dname, "MXF"] if dname == "Uv" else [dname, "stg4_0", "stg4_1", "stg4_2"])
    P.fence("act", ["Uv", "Ucs"]); P.fence("dve", ["Uv", "Ucs"])
    P.dma("sp", dr["U_d"], Uv, reads=["Uv"], writes=["U_d"])
    P.dma("sp", dr["Uc_d"], Ucs, reads=["Ucs"], writes=["Uc_d"])
    if debug:
        P.dma("sp", dbg["U"], Uv, reads=["Uv"])
    P.barrier()
    es.close()
    esA.close()

    if stop_after == 'p4':
        P.barrier()
        keep.close(); P.close()
        return nc
    s5.phase_z()
    if stop_after == 'p5':
        P.barrier()
        keep.close(); P.close()
        return nc
    s5.phase_scan(consts["flags"], pre_emit=s5.setup_b)
    if stop_after == 'p6':
        P.barrier()
        keep.close(); P.close()
        return nc
    s5.phase_read()
    if stop_after == 'p7':
        P.barrier()
        keep.close(); P.close()
        return nc
    if debug:
        P.dma("sp", dbg["YG"], dr["YG_d"], reads=["YG_d"])

    esB = ExitStack()
    YB = esB.enter_context(nc.sbuf_tensor("YB", [128, 8, NT], BF16))
    Wo = esB.enter_context(nc.sbuf_tensor("Wo", [128, 16, D], BF16))
    stg8 = esB.enter_context(nc.sbuf_tensor("stg8", [128, 3, 2048], F32))
    wl8 = WL(P, stg8, "stg8_")
    wo = w_out.rearrange("(ct p) n -> p ct n", p=128)
    wgl = w_glu.rearrange("(ct p) n -> p ct n", p=128)
    es = ExitStack()
    ygT = es.enter_context(nc.sbuf_tensor("ygT", [128, 8, NT], BF16))
    YGt = es.enter_context(nc.sbuf_tensor("YGt", [128, 2, 1024], BF16))
    Wg = es.enter_context(nc.sbuf_tensor("Wg", [128, 8, 1024], BF16))
    sgt = es.enter_context(nc.sbuf_tensor("sgt", [128, 2, TB], BF16))
    zb2 = es.enter_context(nc.sbuf_tensor("zb2", [128, 2, NT], BF16))
    pX = es.enter_context(nc.psum_tensor("pX8", [128, 2, 4, 128], F32))
    pA = es.enter_context(nc.psum_tensor("pA8", [128, 2, 512], F32))
    for ct in range(8):
        h_ = wl8.dma(wgl[:, ct, :])
        wl8.cast("act" if ct % 2 == 0 else "dve", h_, Wg[:, ct, :], ["Wg"])
    wo_next = [0]

    def load_wo(n):
        for _ in range(n):
            if wo_next[0] < 16:
                c_ = wo_next[0]; wo_next[0] += 1
                h_ = wl8.dma(wo[:, c_, :])
                wl8.cast("dve", h_, Wo[:, c_, :], ["Wo"])

    for t in range(NTT):
        b = t % 2
        P.dma("sp", YGt[:, b], dr["YG_d"][t * 128:(t + 1) * 128], reads=["YG_d"], writes=["YGt%d" % b])
        load_wo(1)
        for hh in range(2):
            for u in range(4):
                ct = hh * 4 + u
                P.op("pe", lambda e: e.matmul(pX[:, hh, u, :], lhsT=YGt[:, b, ct * 128:(ct + 1) * 128], rhs=idb[:], start=True, stop=True),
                     reads=["YGt%d" % b, "idb"], writes=["pX%d" % hh])
            if hh == 0:
                P.op("act", lambda e: e.activation(ygT[:, 0:4, t * 128:(t + 1) * 128], pX[:, 0], AF.Copy), reads=["pX0"], writes=["ygT"])
            else:
                P.op("dve", lambda e: e.tensor_copy(ygT[:, 4:8, t * 128:(t + 1) * 128], pX[:, 1]), reads=["pX1"], writes=["ygT"])
    ak = 0
    for co in range(8):
        zb_ = co % 2
        P.dma("sp", zb2[:, zb_], ZB_d[co], reads=["ZB_d"], writes=["zb2%d" % zb_])
        for tb in range(NTB):
            slot = ak % 2; ak += 1
            ts = slice(tb * TB, (tb + 1) * TB)
            for ct in range(8):
                P.op("pe", lambda e: e.matmul(pA[:, slot, 0:TB], lhsT=Wg[:, ct, co * 128:(co + 1) * 128], rhs=ygT[:, ct, ts], start=(ct == 0), stop=(ct == 7)),
                     reads=["Wg", "ygT"], writes=["pA%d" % slot])
            P.op("act", lambda e: e.activation(sgt[:, slot, :], pA[:, slot, 0:TB], AF.Sigmoid, bias=bglu[:, co:co + 1]), reads=["pA%d" % slot, "bglu"], writes=["sgt%d" % slot])
            P.op("dve", lambda e: e.tensor_tensor(sgt[:, slot, :], sgt[:, slot, :], ygT[:, co, ts], ALU.mult), reads=["sgt%d" % slot, "ygT"], writes=["sgt%d" % slot])
            P.op("dve", lambda e: e.tensor_tensor(YB[:, co, ts], sgt[:, slot, :], zb2[:, zb_, ts], ALU.mult), reads=["sgt%d" % slot, "zb2%d" % zb_], writes=["YB"])
    load_wo(16)
    if debug:
        P.fence("dve", ["YB"])
        P.dma("sp", dbg["YB"], YB[:], reads=["YB"])
    P.barrier()
    es.close()
    if stop_after == 'p8':
        P.barrier()
        esB.close()
        keep.close(); P.close()
        return nc

    es = ExitStack()
    YA = es.enter_context(nc.sbuf_tensor("YA", [128, 8, NT], BF16))
    P.dma("sp", YA[:], YA_d.rearrange("c p t -> p c t"), reads=["YA_d"], writes=["YA"])
    Grow = es.enter_context(nc.sbuf_tensor("Grow", [128, D], F32))
    lgr = es.enter_context(nc.sbuf_tensor("lgr", [128, D], F32)); lbr = es.enter_context(nc.sbuf_tensor("lbr", [128, D], F32))
    xt = es.enter_context(nc.sbuf_tensor("xt9", [128, 2, D], F32)); ot = stg8
    st6 = es.enter_context(nc.sbuf_tensor("st69", [128, 4, 6], F32)); mv = es.enter_context(nc.sbuf_tensor("mv9", [128, 2], F32))
    rstd = es.enter_context(nc.sbuf_tensor("rstd9", [128, 1], F32))
    pO = es.enter_context(nc.psum_tensor("pO", [128, 2, 4, 512], F32))
    P.dma("sp", Grow[:], bcast_rows(gsc, D), reads=["gsc"], writes=["Grow"])
    P.dma("sp", lgr[:], bcast_rows(ln_g, D), writes=["lgr"]); P.dma("sp", lbr[:], bcast_rows(ln_b, D), writes=["lbr"])
    outs = []
    for t in range(NTT):
        b = t % 2
        tsl = slice(t * 128, (t + 1) * 128)
        P.dma("sp", xt[:, b], x[tsl], writes=["xt%d" % b])
        for db in range(4):
            for ct in range(16):
                src = YA if ct < 8 else YB
                P.op("pe", lambda e: e.matmul(pO[:, b, db, :], lhsT=src[:, ct % 8, tsl], rhs=Wo[:, ct, db * 512:(db + 1) * 512], start=(ct == 0), stop=(ct == 15)),
                     reads=["YA", "YB", "Wo"], writes=["pO%d" % b])
        o = ot[:, b]
        P.op("act", lambda e: e.activation(o.rearrange("p (a n) -> p a n", a=4), pO[:, b], AF.Copy), reads=["pO%d" % b], writes=["ot%d" % b])
        P.op("dve", lambda e: e.tensor_tensor(o, o, Grow[:], ALU.mult), reads=["ot%d" % b, "Grow"], writes=["ot%d" % b])
        P.op("dve", lambda e: e.scalar_tensor_tensor(o, xt[:, b], ALPHA, o, ALU.mult, ALU.add), reads=["ot%d" % b, "xt%d" % b], writes=["ot%d" % b])
        for q in range(4):
            P.op("dve", lambda e: e.bn_stats(st6[:, q, :], o[:, q * 512:(q + 1) * 512]), reads=["ot%d" % b], writes=["st6_%d" % q])
        P.op("dve", lambda e: e.bn_aggr(mv[:], st6[:]), reads=["st6_%d" % q_ for q_ in range(4)], writes=["mv"])
        P.op("dve", lambda e: e.tensor_scalar(rstd[:], mv[:, 1:2], LN_EPS, None, ALU.add), reads=["mv"], writes=["rstd"])
        P.op("act", lambda e: e.activation(rstd[:], rstd[:], AF.Sqrt), reads=["rstd"], writes=["rstd"])
        P.op("dve", lambda e: e.reciprocal(rstd[:], rstd[:]), reads=["rstd"], writes=["rstd"])
        P.op("dve", lambda e: e.tensor_scalar(o, o, mv[:, 0:1], rstd[:, 0:1], ALU.subtract, ALU.mult), reads=["ot%d" % b, "mv", "rstd"], writes=["ot%d" % b])
        P.op("pool", lambda e: e.tensor_tensor(o, o, lgr[:], ALU.mult), reads=["ot%d" % b, "lgr"], writes=["ot%d" % b])
        P.op("pool", lambda e: e.tensor_tensor(o, o, lbr[:], ALU.add), reads=["ot%d" % b, "lbr"], writes=["ot%d" % b])
        P.fence("pool", ["ot%d" % b])
        outs.append(P.dma("pool", y[tsl], o, reads=["ot%d" % b], writes=["y"]))
    P.finish("sp", outs)
    P.barrier()
    es.close()
    esB.close()
    keep.close()
    P.close()
    return nc


_NC_CACHE = {}


def make_in_maps(inputs, NT):
    import ml_dtypes
    f = lambda k: np.ascontiguousarray(np.asarray(inputs[k])[0], dtype=np.float32)
    x = np.asarray(inputs["x"], dtype=np.float32); c = np.asarray(inputs["c"], dtype=np.float32)
    ctx = np.asarray(inputs["ctx"], dtype=np.float32); cc = np.asarray(inputs["c_ctx"], dtype=np.float32)
    B, L, _ = x.shape
    cpb = L // NT
    assert B * cpb == 8 and cpb == 4
    ML = np.ascontiguousarray(np.kron(np.tril(np.ones((8, 8))), np.ones((16, 16))).astype(np.float32).T)
    MU = np.kron(np.tril(np.ones((8, 8))), np.ones((16, 16))).astype(np.float32)
    common = {
        "w_in": np.ascontiguousarray(f("w_in").reshape(16, 128, 40, 128).transpose(2, 1, 0, 3)),
        "sgu_g": f("sgu_ln_g").reshape(1, 1024), "sgu_b": f("sgu_ln_b").reshape(1, 1024),
        "w_sp": f("w_spatial"), "b_sp": f("b_spatial").reshape(1, 1024),
        "w_glu": f("w_glu"), "b_gluT": np.ascontiguousarray(f("b_glu").reshape(8, 128).T), "w_out": f("w_out"),
        "ln_g": f("ln_g").reshape(1, D), "ln_b": f("ln_b").reshape(1, D),
        "lam_re": f("s5_lam_re").reshape(128, 64), "lam_im": f("s5_lam_im").reshape(128, 64), "log_step": f("s5_log_step").reshape(128, 1),
        "b_re": f("s5_b_re").reshape(128, 64, 16), "b_im": f("s5_b_im").reshape(128, 64, 16),
        "c_re": f("s5_c_re").reshape(128, 16, 64), "c_im": f("s5_c_im").reshape(128, 16, 64),
        "c_idb": np.eye(128).astype(ml_dtypes.bfloat16), "c_Jb": np.ascontiguousarray(np.eye(128)[::-1]).astype(ml_dtypes.bfloat16),
        "c_idf": np.eye(128, dtype=np.float32), "c_ML": ML, "c_MU": MU,
        "c_dcol": np.ascontiguousarray(np.tile(f("s5_d").reshape(64, 16).T, (8, 1))),
    }
    ims = []
    for core in range(8):
        b, k = core // cpb, core % cpb
        fl = np.zeros((128, 3), np.float32)
        for jj in range(3):
            fl[0:64, jj] = 1.0 if jj < k else 0.0
            fl[64:128, jj] = 1.0 if (3 - jj) > k else 0.0
        cT = np.stack([c[b].reshape(16, 128).T, cc.reshape(16, 128).T], axis=-1).astype(np.float32)
        im = dict(common)
        wad = f("w_ada"); bad = f("b_ada")
        im["w_ada"] = np.ascontiguousarray(wad[:, k * 1536:(k + 1) * 1536])
        im["b_adaT"] = np.ascontiguousarray(bad[k * 1536:(k + 1) * 1536].reshape(12, 128).T)
        im.update({"x": np.ascontiguousarray(x[b, k * NT:(k + 1) * NT]), "ctx": np.ascontiguousarray(ctx[b]), "cT": np.ascontiguousarray(cT), "c_flags": fl})
        ims.append(im)
    return ims


def kernel(**inputs):
    x = np.asarray(inputs["x"])
    B, L, _ = x.shape
    NT = B * L // 8
    if NT not in _NC_CACHE:
        _NC_CACHE[NT] = build(NT, np.asarray(inputs["ctx"]).shape[1])
    nc = _NC_CACHE[NT]
    ims = make_in_maps(inputs, NT)
    res = run_bass_kernel_spmd(nc, ims, core_ids=list(range(8)))
    out = np.empty((B, L, D), np.float32)
    cpb = L // NT
    for core in range(8):
        b, k = core // cpb, core % cpb
        out[b, k * NT:(k + 1) * NT] = np.asarray(res.results[core]["y"])
    return out
```
